# Optimizing a Trainium2 kernel written in Bass

```python
import jax, jax.numpy as jnp
from jax import lax
import numpy as np

D_MODEL = 1024
BATCH = 16
SEQ = 256
DEPTH = 2
DEC_BATCH = 4
DEC_SEQ = 2048
PAST_LEN = 512

GRID_W = 64
HEAD_DIM = 64
A_HEADS = 8
A_KV = 2
WINDOW = 128
B_HEADS = 4
B_DIM = 128
C_HEADS = 4
C_DK = 128
C_DV = 128
D_HEADS = 8
D_KV = 2
D_FF = 2816
Q_BLOCK = 128
MLSTM_CHUNK = 64
HGRN_CHUNK = 32
ROPE_THETA = 10000.0
N_EVEN = (DEPTH + 1) // 2
N_ODD = DEPTH // 2
ALPHA = (2 * DEPTH) ** 0.25
BETA = (8 * DEPTH) ** -0.25
EVEN_SIZES = (A_HEADS * HEAD_DIM, A_KV * HEAD_DIM, A_KV * HEAD_DIM,
              B_HEADS * B_DIM, B_HEADS * B_DIM, B_HEADS * B_DIM, 4 * B_HEADS, B_HEADS * B_DIM)
ODD_SIZES = (C_HEADS * C_DK, C_HEADS * C_DK, C_HEADS * C_DK, C_HEADS * C_DV, C_HEADS * C_DV,
             D_HEADS * HEAD_DIM, D_KV * HEAD_DIM, D_KV * HEAD_DIM)
EVEN_OUT = A_HEADS * HEAD_DIM + B_HEADS * B_DIM
ODD_OUT = C_HEADS * C_DV + D_HEADS * HEAD_DIM
NEG_INF = -1e30
F32 = jnp.float32

kernel_name = 'hybrid_diffusion_prefix_step'


def _split(p, sizes):
    return jnp.split(p, np.cumsum(sizes)[:-1].tolist(), axis=-1)


def _flip(t):
    return jnp.flip(t, axis=1)


def _layernorm(x, g, b, eps=1e-5):
    xf = x.astype(F32)
    mu = jnp.mean(xf, -1, keepdims=True)
    xc = xf - mu
    var = jnp.mean(xc * xc, -1, keepdims=True)
    return (xc * lax.rsqrt(var + eps) * g.astype(F32) + b.astype(F32)).astype(x.dtype)


def _rms(x, g, eps=1e-6):
    xf = x.astype(F32)
    return (xf * lax.rsqrt(jnp.mean(xf * xf, -1, keepdims=True) + eps) * g.astype(F32)).astype(x.dtype)


def _modulation(cvec, w, b):
    return (jax.nn.silu(cvec) @ w + b).reshape(cvec.shape[0], 9, D_MODEL)


def _mod_parts(mod, j):
    return mod[:, 3 * j][:, None], mod[:, 3 * j + 1][:, None], mod[:, 3 * j + 2][:, None]


def _ffn_sublayer(x, mod, j, g, b, w1, w3, w2):
    shift, scale, gate = _mod_parts(mod, j)
    h = x * (1 + scale) + shift
    y = (jax.nn.silu(h @ w1) * (h @ w3)) @ w2
    return _layernorm(ALPHA * x + 0.5 * gate * y, g, b)


def _axial_rope(x):
    L = x.shape[1]
    rows = L // GRID_W
    row = jnp.repeat(jnp.arange(rows), GRID_W)
    col = jnp.arange(L) % GRID_W
    half = x.shape[-1] // 2
    nf = half // 2
    inv = ROPE_THETA ** (-jnp.arange(nf, dtype=F32) / nf)
    shp = (L,) + (1,) * (x.ndim - 3) + (nf,)

    def rot(xh, pos):
        ang = pos.astype(F32)[:, None] * inv[None]
        cos = jnp.cos(ang).reshape(shp).astype(x.dtype)
        sin = jnp.sin(ang).reshape(shp).astype(x.dtype)
        x1, x2 = xh[..., :nf], xh[..., nf:]
        return jnp.concatenate([x1 * cos - x2 * sin, x1 * sin + x2 * cos], -1)

    return jnp.concatenate([rot(x[..., :half], row), rot(x[..., half:], col)], -1)


def _to_blocks(x, size):
    b, L = x.shape[:2]
    return jnp.moveaxis(x.reshape((b, L // size, size) + x.shape[2:]), 1, 0)


def _from_blocks(x):
    n, b, size = x.shape[:3]
    return jnp.moveaxis(x, 0, 1).reshape((b, n * size) + x.shape[3:])


def _attend(q, k, v, mask, sink):
    s = jnp.einsum('bqhgd,bkhd->bhgqk', q, k).astype(F32) * (q.shape[-1] ** -0.5)
    if mask is not None:
        s = jnp.where(mask, s, NEG_INF)
    if sink is not None:
        sk = jnp.broadcast_to(sink.astype(F32)[None, :, :, None, None], s.shape[:-1] + (1,))
        s = jnp.concatenate([s, sk], axis=-1)
    p = jax.nn.softmax(s, axis=-1)
    if sink is not None:
        p = p[..., :-1]
    return jnp.einsum('bhgqk,bkhd->bqhgd', p.astype(v.dtype), v)


def _dense_attention(q, k, v, sink):
    out = lax.map(lambda qq: _attend(qq, k, v, None, sink), _to_blocks(q, Q_BLOCK))
    return _from_blocks(out)


def _banded_attention(q, k, v, k_ctx, v_ctx, sink):
    L = q.shape[1]
    span = Q_BLOCK + 2 * WINDOW
    pad = ((0, 0), (WINDOW, WINDOW), (0, 0), (0, 0))
    kp, vp = jnp.pad(k, pad), jnp.pad(v, pad)
    rel = jnp.arange(span)[None, :] - WINDOW - jnp.arange(Q_BLOCK)[:, None]
    band = jnp.abs(rel) <= WINDOW
    ctx_mask = jnp.ones((Q_BLOCK, k_ctx.shape[1]), bool)
    kc, vc = k_ctx.astype(k.dtype), v_ctx.astype(v.dtype)

    def block(args):
        qq, j = args
        start = j * Q_BLOCK
        kk = lax.dynamic_slice_in_dim(kp, start, span, axis=1)
        vv = lax.dynamic_slice_in_dim(vp, start, span, axis=1)
        kpos = start - WINDOW + jnp.arange(span)
        valid = band & ((kpos >= 0) & (kpos < L))[None, :]
        mask = jnp.concatenate([valid, ctx_mask], axis=1)
        return _attend(qq, jnp.concatenate([kk, kc], 1), jnp.concatenate([vv, vc], 1), mask, sink)

    out = lax.map(block, (_to_blocks(q, Q_BLOCK), jnp.arange(L // Q_BLOCK)))
    return _from_blocks(out)


def _mlstm_scan(q, k, v, log_i, log_f, state):
    T = MLSTM_CHUNK
    causal = jnp.tril(jnp.ones((T, T), bool))[None, :, :, None]

    def step(carry, xs):
        C, n, m = carry
        qc, kc, vc, ic, fc = xs
        bc = jnp.cumsum(fc, axis=1)
        dmat = jnp.where(causal, bc[:, :, None] - bc[:, None] + ic[:, None], -jnp.inf)
        inter = bc + m[:, None]
        m_t = jnp.maximum(inter, jnp.max(dmat, axis=2))
        w = jnp.exp(dmat - m_t[:, :, None]) * jnp.einsum('bthd,bshd->btsh', qc, kc)
        a = jnp.exp(inter - m_t)
        num = a[..., None] * jnp.einsum('bthd,bhde->bthe', qc, C) + jnp.einsum('btsh,bshe->bthe', w, vc)
        den = a * jnp.einsum('bthd,bhd->bth', qc, n) + jnp.sum(w, axis=2)
        h = num / jnp.maximum(jnp.abs(den), jnp.exp(-m_t))[..., None]
        b_tot = bc[:, -1]
        g = b_tot[:, None] - bc + ic
        m_new = jnp.maximum(b_tot + m, jnp.max(g, axis=1))
        ws = jnp.exp(g - m_new[:, None])
        decay = jnp.exp(b_tot + m - m_new)
        C_new = decay[..., None, None] * C + jnp.einsum('bsh,bshd,bshe->bhde', ws, kc, vc)
        n_new = decay[..., None] * n + jnp.einsum('bsh,bshd->bhd', ws, kc)
        return (C_new, n_new, m_new), h

    xs = tuple(_to_blocks(t.astype(F32), T) for t in (q, k, v, log_i, log_f))
    state = tuple(s.astype(F32) for s in state)
    final, hs = lax.scan(step, state, xs)
    return _from_blocks(hs), final


def _hgrn2_scan(q, k, v, log_f, S):
    T = HGRN_CHUNK
    causal = jnp.tril(jnp.ones((T, T), bool))[None, :, :, None, None]

    def step(S, xs):
        qc, kc, vc, fc = xs
        A = jnp.cumsum(fc, axis=1)
        decay = jnp.exp(jnp.where(causal, A[:, :, None] - A[:, None], -jnp.inf))
        att = jnp.einsum('bthd,btshd,bshd->btsh', qc, decay, kc)
        o = jnp.einsum('bthd,bhde->bthe', qc * jnp.exp(A), S) + jnp.einsum('btsh,bshe->bthe', att, vc)
        A_tot = A[:, -1]
        S_new = jnp.exp(A_tot)[..., None] * S + jnp.einsum('bshd,bshe->bhde', kc * jnp.exp(A_tot[:, None] - A), vc)
        return S_new, o

    xs = tuple(_to_blocks(t.astype(F32), T) for t in (q, k, v, log_f))
    final, os_ = lax.scan(step, S.astype(F32), xs)
    return _from_blocks(os_), final


def _even_mixer(h, w_in, w_out, sink, gate_bias, norm_g, ctx):
    bsz, L, _ = h.shape
    aq, ak, av, bq, bk, bv, bg, bo = _split(h @ w_in, EVEN_SIZES)
    aq = aq.reshape(bsz, L, A_KV, A_HEADS // A_KV, HEAD_DIM)
    ak = ak.reshape(bsz, L, A_KV, HEAD_DIM)
    av = av.reshape(bsz, L, A_KV, HEAD_DIM)
    bq = bq.reshape(bsz, L, B_HEADS, B_DIM)
    bk = bk.reshape(bsz, L, B_HEADS, B_DIM) * (B_DIM ** -0.5)
    bv = bv.reshape(bsz, L, B_HEADS, B_DIM)
    gates = bg.reshape(bsz, L, 4, B_HEADS).astype(F32) + gate_bias.astype(F32)
    if ctx is None:
        zero = (jnp.zeros((bsz, B_HEADS, B_DIM, B_DIM), F32), jnp.zeros((bsz, B_HEADS, B_DIM), F32),
                jnp.zeros((bsz, B_HEADS), F32))
        st_f, st_b = zero, zero
        ya = _dense_attention(aq, ak, av, sink)
    else:
        k_ctx, v_ctx, C0, n0, m0 = ctx
        st_f = (C0[:, 0], n0[:, 0], m0[:, 0])
        st_b = (C0[:, 1], n0[:, 1], m0[:, 1])
        ya = _banded_attention(_axial_rope(aq), _axial_rope(ak), av, k_ctx, v_ctx, sink)
    hf, (Cf, nf, mf) = _mlstm_scan(bq, bk, bv, gates[:, :, 0], jax.nn.log_sigmoid(gates[:, :, 1]), st_f)
    hb, (Cb, nb, mb) = _mlstm_scan(_flip(bq), _flip(bk), _flip(bv), _flip(gates[:, :, 2]),
                                   _flip(jax.nn.log_sigmoid(gates[:, :, 3])), st_b)
    yb = jax.nn.sigmoid(bo.reshape(bsz, L, B_HEADS, B_DIM).astype(F32)) * _rms(hf + _flip(hb), norm_g)
    y = jnp.concatenate([ya.reshape(bsz, L, -1), yb.reshape(bsz, L, -1).astype(ya.dtype)], -1) @ w_out
    if ctx is None:
        return y, (ak, av, jnp.stack([Cf, Cb], 1), jnp.stack([nf, nb], 1), jnp.stack([mf, mb], 1))
    return y, None


def _odd_mixer(h, w_in, w_out, lb, norm_g, q_norm, k_norm, ctx):
    bsz, L, _ = h.shape
    cq, cf_f, cf_b, ci, cg, dq, dk, dv = _split(h @ w_in, ODD_SIZES)
    q = jax.nn.silu(cq.astype(F32)).reshape(bsz, L, C_HEADS, C_DK)
    v = ci.astype(F32).reshape(bsz, L, C_HEADS, C_DV)
    lbh = lb.reshape(C_HEADS, C_DK)

    def hgrn_gates(fp):
        f = lbh + (1.0 - lbh) * jax.nn.sigmoid(fp.astype(F32).reshape(bsz, L, C_HEADS, C_DK))
        return jnp.log(f), 1.0 - f

    lf_f, k_f = hgrn_gates(cf_f)
    lf_b, k_b = hgrn_gates(cf_b)
    dq = _rms(dq.reshape(bsz, L, D_KV, D_HEADS // D_KV, HEAD_DIM), q_norm)
    dk = _rms(dk.reshape(bsz, L, D_KV, HEAD_DIM), k_norm)
    dv = dv.reshape(bsz, L, D_KV, HEAD_DIM)
    if ctx is None:
        s0 = jnp.zeros((bsz, C_HEADS, C_DK, C_DV), F32)
        S_f0, S_b0 = s0, s0
    else:
        k_ctx, v_ctx, S0 = ctx
        S_f0, S_b0 = S0[:, 0], S0[:, 1]
    of, S_f = _hgrn2_scan(q, k_f, v, lf_f, S_f0)
    ob, S_b = _hgrn2_scan(_flip(q), _flip(k_b), _flip(v), _flip(lf_b), S_b0)
    yc = _rms(of + _flip(ob), norm_g) * jax.nn.silu(cg.astype(F32)).reshape(bsz, L, C_HEADS, C_DV)
    if ctx is None:
        yd = _dense_attention(dq, dk, dv, None)
        new = (dk, dv, jnp.stack([S_f, S_b], 1))
    else:
        keys = jnp.concatenate([_axial_rope(dk), k_ctx.astype(dk.dtype)], 1)
        vals = jnp.concatenate([dv, v_ctx.astype(dv.dtype)], 1)
        yd = _dense_attention(_axial_rope(dq), keys, vals, None)
        new = None
    y = jnp.concatenate([yc.reshape(bsz, L, -1).astype(yd.dtype), yd.reshape(bsz, L, -1)], -1) @ w_out
    return y, new


def setup_inputs(seed: int = 0) -> dict:
    key = jax.random.key(seed)
    ks = iter(jax.random.split(key, 32))

    def nrm(shape, scale=1.0):
        return scale * jax.random.normal(next(ks), shape, F32)

    fgate_base = jnp.array([0.0, 1.0, 0.0, 1.0], F32)[:, None] * jnp.linspace(3.0, 6.0, B_HEADS)[None]
    return {
        'x_prompt': nrm((BATCH, SEQ, D_MODEL)),
        'x_sample': nrm((DEC_BATCH, DEC_SEQ, D_MODEL)),
        'c': nrm((DEC_BATCH, D_MODEL)),
        'cache_a_k': nrm((DEC_BATCH, N_EVEN, PAST_LEN, A_KV, HEAD_DIM)),
        'cache_a_v': nrm((DEC_BATCH, N_EVEN, PAST_LEN, A_KV, HEAD_DIM)),
        'state_b_C': nrm((DEC_BATCH, N_EVEN, 2, B_HEADS, B_DIM, B_DIM), 0.1),
        'state_b_n': nrm((DEC_BATCH, N_EVEN, 2, B_HEADS, B_DIM), 0.1),
        'state_b_m': nrm((DEC_BATCH, N_EVEN, 2, B_HEADS)),
        'state_c_S': nrm((DEC_BATCH, N_ODD, 2, C_HEADS, C_DK, C_DV), 0.5),
        'cache_d_k': nrm((DEC_BATCH, N_ODD, PAST_LEN, D_KV, HEAD_DIM)),
        'cache_d_v': nrm((DEC_BATCH, N_ODD, PAST_LEN, D_KV, HEAD_DIM)),
        'c_ctx': nrm((D_MODEL,)),
        'ada_w': nrm((DEPTH, D_MODEL, 9 * D_MODEL), 0.5 * D_MODEL ** -0.5),
        'ada_b': nrm((DEPTH, 9 * D_MODEL), 0.01),
        'ln_g': 1.0 + nrm((DEPTH, 3, D_MODEL), 0.02),
        'ln_b': nrm((DEPTH, 3, D_MODEL), 0.02),
        'ffn_w1': nrm((DEPTH, 2, D_MODEL, D_FF), D_MODEL ** -0.5),
        'ffn_w3': nrm((DEPTH, 2, D_MODEL, D_FF), D_MODEL ** -0.5),
        'ffn_w2': nrm((DEPTH, 2, D_FF, D_MODEL), BETA * D_FF ** -0.5),
        'w_in_even': nrm((N_EVEN, D_MODEL, sum(EVEN_SIZES)), D_MODEL ** -0.5),
        'w_out_even': nrm((N_EVEN, EVEN_OUT, D_MODEL), BETA * EVEN_OUT ** -0.5),
        'a_sink': nrm((N_EVEN, A_KV, A_HEADS // A_KV), 0.5),
        'b_gate_bias': fgate_base + nrm((N_EVEN, 4, B_HEADS), 0.1),
        'b_norm_g': 1.0 + nrm((N_EVEN, B_HEADS, B_DIM), 0.02),
        'w_in_odd': nrm((N_ODD, D_MODEL, sum(ODD_SIZES)), D_MODEL ** -0.5),
        'w_out_odd': nrm((N_ODD, ODD_OUT, D_MODEL), BETA * ODD_OUT ** -0.5),
        'c_lb_logits': nrm((DEPTH, C_HEADS * C_DK)),
        'c_norm_g': 1.0 + nrm((N_ODD, C_HEADS, C_DV), 0.02),
        'd_q_norm': 1.0 + nrm((N_ODD, HEAD_DIM), 0.02),
        'd_k_norm': 1.0 + nrm((N_ODD, HEAD_DIM), 0.02),
    }


def reference(x_prompt, x_sample, c, cache_a_k, cache_a_v, state_b_C, state_b_n, state_b_m, state_c_S,
              cache_d_k, cache_d_v, c_ctx, ada_w, ada_b, ln_g, ln_b, ffn_w1, ffn_w3, ffn_w2,
              w_in_even, w_out_even, a_sink, b_gate_bias, b_norm_g, w_in_odd, w_out_odd,
              c_lb_logits, c_norm_g, d_q_norm, d_k_norm):
    lb_sm = jax.nn.softmax(c_lb_logits.astype(F32), axis=0)
    lower_bounds = jnp.cumsum(lb_sm, axis=0) - lb_sm[0]
    xp, xs = x_prompt, x_sample
    a_k_l, a_v_l, b_C_l, b_n_l, b_m_l, c_S_l, d_k_l, d_v_l = [], [], [], [], [], [], [], []
    for l in range(DEPTH):
        mod_p = _modulation(c_ctx[None], ada_w[l], ada_b[l])
        mod_s = _modulation(c, ada_w[l], ada_b[l])
        xp = _ffn_sublayer(xp, mod_p, 0, ln_g[l, 0], ln_b[l, 0], ffn_w1[l, 0], ffn_w3[l, 0], ffn_w2[l, 0])
        xs = _ffn_sublayer(xs, mod_s, 0, ln_g[l, 0], ln_b[l, 0], ffn_w1[l, 0], ffn_w3[l, 0], ffn_w2[l, 0])
        sh_p, sc_p, gt_p = _mod_parts(mod_p, 1)
        sh_s, sc_s, gt_s = _mod_parts(mod_s, 1)
        hp = xp * (1 + sc_p) + sh_p
        hs = xs * (1 + sc_s) + sh_s
        i = l // 2
        if l % 2 == 0:
            yp, (ak, av, bC, bn, bm) = _even_mixer(hp, w_in_even[i], w_out_even[i], a_sink[i], b_gate_bias[i],
                                                   b_norm_g[i], None)
            ys, _ = _even_mixer(hs, w_in_even[i], w_out_even[i], a_sink[i], b_gate_bias[i], b_norm_g[i],
                                (cache_a_k[:, i], cache_a_v[:, i], state_b_C[:, i], state_b_n[:, i],
                                 state_b_m[:, i]))
            a_k_l.append(ak)
            a_v_l.append(av)
            b_C_l.append(bC)
            b_n_l.append(bn)
            b_m_l.append(bm)
        else:
            yp, (dk, dv, cS) = _odd_mixer(hp, w_in_odd[i], w_out_odd[i], lower_bounds[l], c_norm_g[i],
                                          d_q_norm[i], d_k_norm[i], None)
            ys, _ = _odd_mixer(hs, w_in_odd[i], w_out_odd[i], lower_bounds[l], c_norm_g[i], d_q_norm[i],
                               d_k_norm[i], (cache_d_k[:, i], cache_d_v[:, i], state_c_S[:, i]))
            d_k_l.append(dk)
            d_v_l.append(dv)
            c_S_l.append(cS)
        xp = _layernorm(ALPHA * xp + gt_p * yp, ln_g[l, 1], ln_b[l, 1])
        xs = _layernorm(ALPHA * xs + gt_s * ys, ln_g[l, 1], ln_b[l, 1])
        xp = _ffn_sublayer(xp, mod_p, 2, ln_g[l, 2], ln_b[l, 2], ffn_w1[l, 1], ffn_w3[l, 1], ffn_w2[l, 1])
        xs = _ffn_sublayer(xs, mod_s, 2, ln_g[l, 2], ln_b[l, 2], ffn_w1[l, 1], ffn_w3[l, 1], ffn_w2[l, 1])
    return (xp, xs, jnp.stack(a_k_l, 1), jnp.stack(a_v_l, 1), jnp.stack(b_C_l, 1), jnp.stack(b_n_l, 1),
            jnp.stack(b_m_l, 1), jnp.stack(c_S_l, 1), jnp.stack(d_k_l, 1), jnp.stack(d_v_l, 1))
```

```python
import numpy as np
import concourse.bass as bass
import concourse.mybir as mybir
from concourse.bass_utils import run_bass_kernel_spmd

F32 = mybir.dt.float32
BF16 = mybir.dt.bfloat16
AF = mybir.ActivationFunctionType
ALU = mybir.AluOpType

D = 1024
NC_ = 8
DFF = 2816
NFC = 22
LS = 2048
LP = 256
NPS = 2
TT = 512
NTT = 5
ALPHA = 4 ** 0.25
LN_EPS = 1e-5 / (ALPHA * ALPHA)
PH = 30000
DBG = {}
PHASE_MARKS = []


class Buf:
    UID = 0

    def __init__(self, h, name, space):
        self.h = h
        self.name = name
        self.space = space
        self.last_w = None
        self.readers = []
        self.dsem = None
        self.dcnt = 0
        Buf.UID += 1
        self.uid = Buf.UID

    def __getitem__(self, idx):
        return self.h[idx]


class KB:
    def __init__(self, nc):
        self.nc = nc
        self.eng = {'pe': nc.tensor, 'act': nc.scalar, 'dve': nc.vector, 'pool': nc.gpsimd, 'sp': nc.sync}
        self.cnt = {e: 0 for e in self.eng}
        self.sems = {e: [] for e in self.eng}
        self.waited = {e: {} for e in self.eng}
        self.nbuf = 0
        self.dma_bufs = []
        self.guards = []
        self.gbufs = []
        self.free_dsems = []
        self.nsem = 0

    def sb(self, name, shape, dt):
        self.nbuf += 1
        g = self.nc.sbuf_tensor('sb_' + name + f'_{self.nbuf}', list(shape), dt)
        h = g.__enter__()
        self.guards.append(g)
        b = Buf(h, name, 'sb')
        self.gbufs.append(b)
        return b

    def mark(self):
        return len(self.guards)

    def release(self, mark):
        self.barrier()
        while len(self.guards) > mark:
            self.guards.pop().__exit__(None, None, None)
            b = self.gbufs.pop()
            if b.dsem is not None:
                self.free_dsems.append((b.dsem, b.dcnt))
                self.dma_bufs.remove(b)
                b.dsem = None

    def ps(self, name, shape, dt=F32):
        return Buf(self.nc.alloc_psum_tensor(name, list(shape), dt), name, 'ps')

    def dram(self, name, shape, dt, kind):
        return Buf(self.nc.dram_tensor(name, list(shape), dt, kind=kind).ap(), name, 'dram')

    def _sem(self, e, phase):
        while len(self.sems[e]) <= phase:
            self.sems[e].append(self.nc.alloc_semaphore(name=f"s_{e}_{len(self.sems[e])}"))
        return self.sems[e][phase]

    def _wait(self, e, ev):
        if ev is None:
            return
        if ev[0] == 'eng':
            _, e2, seq = ev
            if e2 == e and e == 'pe':
                return
            key = ('eng', e2)
            if self.waited[e].get(key, 0) >= seq:
                return
            self.waited[e][key] = seq
            ph, val = (seq - 1) // PH, (seq - 1) % PH + 1
            self.eng[e].wait_ge(self._sem(e2, ph), val)
        else:
            owner = ev[1]
            key = ('dma', owner.uid)
            need = owner.dcnt
            if self.waited[e].get(key, 0) >= need:
                return
            self.waited[e][key] = need
            self.eng[e].wait_ge(owner.dsem, need * 16)

    def _sync(self, e, reads, writes):
        for r in reads:
            self._wait(e, r.last_w)
            if r.space == 'ps':
                for ev in r.readers:
                    if not (ev[0] == 'eng' and ev[1] == e):
                        self._wait(e, ev)
        for w in writes:
            self._wait(e, w.last_w)
            for ev in w.readers:
                if ev[0] == 'eng' and ev[1] == e:
                    continue
                self._wait(e, ev)

    def _record(self, ev, reads, writes):
        for r in reads:
            if r in writes:
                continue
            if ev[0] == 'eng':
                r.readers = [x for x in r.readers if not (x[0] == 'eng' and x[1] == ev[1])]
            else:
                r.readers = [x for x in r.readers if not (x[0] == 'dma' and x[1] is ev[1])]
            r.readers.append(ev)
        for w in writes:
            w.last_w = ev
            w.readers = []

    def op(self, e, fn, reads, writes):
        self._sync(e, reads, writes)
        inst = fn()
        self.cnt[e] += 1
        seq = self.cnt[e]
        inst.then_inc(self._sem(e, (seq - 1) // PH), 1)
        self._record(('eng', e, seq), reads, writes)
        return inst

    def dma(self, q, out_ap, in_ap, dst, src, owner=None):
        if owner is None:
            owner = dst if dst.space == 'sb' else src
        if owner.dsem is None:
            if self.free_dsems:
                owner.dsem, owner.dcnt = self.free_dsems.pop()
            else:
                self.nsem += 1
                owner.dsem = self.nc.alloc_semaphore(name=f"d_{self.nsem}")
            self.dma_bufs.append(owner)
        self._sync(q, [src], [dst])
        inst = self.eng[q].dma_start(out=out_ap, in_=in_ap)
        owner.dcnt += 1
        inst.then_inc(owner.dsem, 16)
        self._record(('dma', owner), [src], [dst])

    def barrier(self):
        for e in self.eng:
            for e2 in self.eng:
                if e2 != e and self.cnt[e2] > 0:
                    self._wait(e, ('eng', e2, self.cnt[e2]))
            for b in self.dma_bufs:
                self._wait(e, ('dma', b))

    def mm(self, out, out_ap, lhsT, lhsT_ap, rhs, rhs_ap, start=True, stop=True):
        rd = [lhsT, rhs]
        return self.op('pe', lambda: self.nc.tensor.matmul(out_ap, lhsT=lhsT_ap, rhs=rhs_ap, start=start,
                                                          stop=stop), rd, [out])

    def tr(self, out, out_ap, in_, in_ap, ident, ident_ap):
        return self.op('pe', lambda: self.nc.tensor.transpose(out_ap, in_ap, ident_ap), [in_, ident], [out])

    def act(self, out, out_ap, in_, in_ap, func, bias=None, scale=None, extra=()):
        kw = {}
        if bias is not None:
            kw['bias'] = bias
        if scale is not None:
            kw['scale'] = scale
        return self.op('act', lambda: self.nc.scalar.activation(out=out_ap, in_=in_ap, func=func, **kw),
                       [in_] + list(extra), [out])

    def tt(self, e, out, out_ap, a, a_ap, b, b_ap, op):
        eng = self.eng[e]
        return self.op(e, lambda: eng.tensor_tensor(out=out_ap, in0=a_ap, in1=b_ap, op=op), [a, b], [out])

    def ts(self, e, out, out_ap, a, a_ap, s1, s2, op0, op1=None, extra=()):
        eng = self.eng[e]
        if op1 is None:
            f = lambda: eng.tensor_scalar(out=out_ap, in0=a_ap, scalar1=s1, scalar2=None, op0=op0)
        else:
            f = lambda: eng.tensor_scalar(out=out_ap, in0=a_ap, scalar1=s1, scalar2=s2, op0=op0, op1=op1)
        return self.op(e, f, [a] + list(extra), [out])

    def stt(self, out, out_ap, a, a_ap, scalar, b, b_ap, op0, op1, extra=()):
        return self.op('dve', lambda: self.nc.vector.scalar_tensor_tensor(out=out_ap, in0=a_ap, scalar=scalar,
                                                                         in1=b_ap, op0=op0, op1=op1),
                       [a, b] + list(extra), [out])

    def copy(self, e, out, out_ap, in_, in_ap):
        if e == 'act':
            return self.act(out, out_ap, in_, in_ap, AF.Copy)
        eng = self.eng[e]
        return self.op(e, lambda: eng.tensor_copy(out=out_ap, in_=in_ap), [in_], [out])

    def memset(self, e, out, out_ap, val):
        eng = self.eng[e]
        return self.op(e, lambda: eng.memset(out_ap, val), [], [out])


class RR:
    def __init__(self, engs):
        self.engs = engs
        self.i = 0

    def __call__(self):
        e = self.engs[self.i % len(self.engs)]
        self.i += 1
        return e


def build(debug_stage=None):
    nc = bass.Bass("TRN2", target_bir_lowering=False)
    k = KB(nc)
    xs_d = k.dram("xs", [LS, D], F32, "ExternalInput")
    xp_d = k.dram("xp", [NPS * LP, D], F32, "ExternalInput")
    cv_d = k.dram("cvec", [2 * NC_, 128], F32, "ExternalInput")
    adaw_d = k.dram("ada_w", [2, D, 9 * D], F32, "ExternalInput")
    adab_d = k.dram("ada_b", [2 * 72, 128], F32, "ExternalInput")
    lnp_d = k.dram("lnp", [96, 128], F32, "ExternalInput")
    w1_d = k.dram("ffn_w1", [4, D, DFF], F32, "ExternalInput")
    w3_d = k.dram("ffn_w3", [4, D, DFF], F32, "ExternalInput")
    w2_d = k.dram("ffn_w2", [4, DFF, D], F32, "ExternalInput")
    ident_d = k.dram("ident", [128, 128], F32, "ExternalInput")
    wine_d = k.dram("win_e", [D, 2832], F32, "ExternalInput")
    woute_d = k.dram("wout_e", [D, D], F32, "ExternalInput")
    wino_d = k.dram("win_o", [D, 3328], F32, "ExternalInput")
    wouto_d = k.dram("wout_o", [D, D], F32, "ExternalInput")
    kctxa_d = k.dram("kctxa", [512, 256], F32, "ExternalInput")
    vctxa_d = k.dram("vctxa", [512, 128], F32, "ExternalInput")
    kctxd_d = k.dram("kctxd", [512, 256], F32, "ExternalInput")
    vctxd_d = k.dram("vctxd", [512, 128], F32, "ExternalInput")
    stC_d = k.dram("stC", [2, 4, 128, 128], F32, "ExternalInput")
    stn_d = k.dram("stn", [2, 4, 128], F32, "ExternalInput")
    stm_d = k.dram("stm", [2, 4], F32, "ExternalInput")
    stS_d = k.dram("stS", [2, 4, 128, 128], F32, "ExternalInput")
    gbias_d = k.dram("gbias", [4, 4], F32, "ExternalInput")
    smallc_d = k.dram("smallc", [128, 32], F32, "ExternalInput")
    kgrow_d = k.dram("kgrow", [128, 128], F32, "ExternalInput")
    cos_d = k.dram("cosT", [128, LS], F32, "ExternalInput")
    sin_d = k.dram("sinT", [128, LS], F32, "ExternalInput")
    RT_d = k.dram("RT", [128, 128], F32, "ExternalInput")
    shiftT_d = k.dram("shiftT", [128, 64], F32, "ExternalInput")
    sel4_d = k.dram("sel4", [4, 512], F32, "ExternalInput")
    mskb_d = k.dram("mskb", [128, 1408], BF16, "ExternalInput")
    ys_d = k.dram("ys", [LS // 2, D], F32, "ExternalOutput")
    ak_o = k.dram("ak_o", [NPS, LP, 128], F32, "ExternalOutput")
    av_o = k.dram("av_o", [NPS, LP, 128], F32, "ExternalOutput")
    dk_o = k.dram("dk_o", [NPS, LP, 128], F32, "ExternalOutput")
    dv_o = k.dram("dv_o", [NPS, LP, 128], F32, "ExternalOutput")
    bC_o = k.dram("bC_o", [NPS, 2, 4, 128, 128], F32, "ExternalOutput")
    bn_o = k.dram("bn_o", [NPS, 2, 4, 128], F32, "ExternalOutput")
    bm_o = k.dram("bm_o", [NPS, 2, 4], F32, "ExternalOutput")
    cS_o = k.dram("cS_o", [NPS, 2, 4, 128, 128], F32, "ExternalOutput")
    wine_s = k.dram("wine_s", [D, 2832], BF16, "Internal")
    woute_s = k.dram("woute_s", [D, D], BF16, "Internal")
    wino_s = k.dram("wino_s", [D, 3328], BF16, "Internal")
    wouto_s = k.dram("wouto_s", [D, D], BF16, "Internal")
    SL = [LS, LP, LP]
    ya_s = [k.dram(f"ya_s{i}", [64, 8, SL[i]], BF16, "Internal") for i in range(3)]
    yb_s = [k.dram(f"yb_s{i}", [128, 4, SL[i]], BF16, "Internal") for i in range(3)]
    hf_s = [k.dram(f"hf_s{i}", [4, 128, SL[i]], F32, "Internal") for i in range(3)]
    of_s = hf_s
    yp_d = k.dram("yp", [NPS * LP, D], F32, "ExternalOutput")
    w1_s = k.dram("w1_s", [4, D, DFF], BF16, "Internal")
    w3_s = k.dram("w3_s", [4, D, DFF], BF16, "Internal")
    w2_s = k.dram("w2_s", [4, DFF, D], BF16, "Internal")

    X = [k.sb(f"X{t}", [128, NC_, TT], F32) for t in range(NTT)]
    H = [k.sb(f"H{t}", [128, NC_, TT], BF16) for t in range(NTT)]
    ident = k.sb("ident", [128, 128], F32)
    ones_bf = k.sb("ones_bf", [128, 128], BF16)
    modv = [k.sb(f"modv{l}", [128, 72, 2], F32) for l in range(2)]
    sc1 = [k.sb(f"sc1{l}", [128, 72, 2], F32) for l in range(2)]
    cf = [k.sb(f"cf{l}", [128, 72, 2], F32) for l in range(2)]
    lnp = k.sb("lnp", [128, 96], F32)
    PS = [k.ps(f"ps{i}", [128, 512]) for i in range(8)]

    k.dma('sp', ident[:, :], ident_d[:, :], ident, ident_d)
    RT = k.sb("RT", [128, 128], F32)
    k.dma('sp', RT[:, :], RT_d[:, :], RT, RT_d)
    shiftT = k.sb("shiftT", [128, 64], F32)
    k.dma('sp', shiftT[:, :], shiftT_d[:, :], shiftT, shiftT_d)
    sel4 = k.sb("sel4", [4, 4, 128], F32)
    k.dma('sp', sel4[:, :, :], sel4_d.h[:, :].rearrange("p (h t) -> p h t", h=4), sel4, sel4_d)

    class View:
        def __init__(self, buf, c0, c1):
            self.buf, self.c0, self.c1 = buf, c0, c1
    tri_hi4 = k.sb("tri_hi4", [128, 512], BF16)
    tri_lo4 = k.sb("tri_lo4", [128, 512], BF16)
    bd_hi = k.sb("bd_hi", [128, 128], BF16)
    bd_lo = k.sb("bd_lo", [128, 128], BF16)
    bones64 = k.sb("bones64", [128, 128], BF16)
    smallc = k.sb("smallc", [128, 32], F32)
    bng = k.sb("bng", [128, 4], F32)
    cng = k.sb("cng", [128, 4], F32)
    gvec = k.sb("gvec", [128, 2], F32)
    lbc = k.sb("lbc", [128, 4], F32)
    oml = k.sb("oml", [128, 4], F32)
    sinkexp = k.sb("sinkexp", [64, 2, 512], F32)
    sk8 = k.sb("sk8", [64, 8], F32)
    kgrow = k.sb("kgrow", [128, 128], F32)
    mk_c = k.mark()
    mskb = k.sb("mskb", [128, 1408], BF16)
    k.dma('sp', mskb[:, :], mskb_d[:, :], mskb, mskb_d)
    k.copy('pool', tri_hi4, tri_hi4[:, :], mskb, mskb[:, 0:512])
    k.copy('pool', tri_lo4, tri_lo4[:, :], mskb, mskb[:, 512:1024])
    k.copy('pool', bd_hi, bd_hi[:, :], mskb, mskb[:, 1024:1152])
    k.copy('pool', bd_lo, bd_lo[:, :], mskb, mskb[:, 1152:1280])
    k.copy('pool', bones64, bones64[:, :], mskb, mskb[:, 1280:1408])
    k.release(mk_c)
    k.dma('sp', smallc[:, :], smallc_d[:, :], smallc, smallc_d)
    k.copy('dve', bng, bng[:, :], smallc, smallc[:, 0:4])
    k.copy('dve', cng, cng[:, :], smallc, smallc[:, 4:8])
    k.copy('dve', gvec, gvec[:, :], smallc, smallc[:, 8:10])
    k.tt('dve', lbc, lbc[:, :], smallc, smallc[:, 14:18], smallc, smallc[:, 10:14], ALU.subtract)
    k.act(lbc, lbc[:, :], lbc, lbc[:, :], AF.Sigmoid)
    k.ts('dve', oml, oml[:, :], lbc, lbc[:, :], -1.0, 1.0, ALU.mult, ALU.add)
    k.act(sk8, sk8[:, :], smallc, smallc[0:64, 18:26], AF.Exp)
    for kv_ in range(2):
        for hp_ in range(2):
            for ab_ in range(2):
                g_ = ab_ * 2 + hp_
                c0_ = hp_ * 256 + ab_ * 128
                k.copy('dve', sinkexp, sinkexp[:, kv_, c0_:c0_ + 128], sk8,
                       sk8[:, kv_ * 4 + g_:kv_ * 4 + g_ + 1].broadcast_to([64, 128]))
    k.dma('sp', kgrow[:, :], kgrow_d[:, :], kgrow, kgrow_d)
    k.memset('pool', ones_bf, ones_bf[:, :], 1.0 / D)

    stage = k.sb("stage_small", [128, 128], F32)
    silu_c = k.sb("silu_c", [128, NC_, 2], F32)

    def load_cols(src_d, r0, nrows, dst, dst_ap, func=None):
        k.dma('sp', stage[0:nrows, :], src_d[r0:r0 + nrows, :], stage, src_d)
        k.tr(PS[0], PS[0][:, 0:nrows], stage, stage[0:nrows, :], ident, ident[0:nrows, 0:nrows])
        if func is None:
            k.copy('dve', dst, dst_ap, PS[0], PS[0][:, 0:nrows])
        else:
            k.act(dst, dst_ap, PS[0], PS[0][:, 0:nrows], func)

    for j in range(2):
        load_cols(cv_d, j * NC_, NC_, silu_c, silu_c[:, :, j], AF.Silu)
    load_cols(lnp_d, 0, 96, lnp, lnp[:, :])

    mk0 = k.mark()
    adab = k.sb("adab", [128, 144], F32)
    load_cols(adab_d, 0, 72, adab, adab[:, 0:72])
    load_cols(adab_d, 72, 72, adab, adab[:, 72:144])
    NADA = 2
    adat = [k.sb(f"adat{i}", [128, NC_, 512], F32) for i in range(NADA)]
    NST = 2
    st_f = [k.sb(f"stf{i}", [128, 2816], F32) for i in range(NST)]
    st_b = [k.sb(f"stb{i}", [128, 2816], BF16) for i in range(NST)]
    cast_rr = RR(['act', 'dve', 'pool'])
    pi = [0]
    prep_units = []

    def prep_rows(src_d, dst_d, src_ap, dst_ap, ncols):
        def f():
            i = pi[0] % NST
            pi[0] += 1
            k.dma('sp', st_f[i][:, 0:ncols], src_ap, st_f[i], src_d)
            k.copy(cast_rr(), st_b[i], st_b[i][:, 0:ncols], st_f[i], st_f[i][:, 0:ncols])
            k.dma('pool', dst_ap, st_b[i][:, 0:ncols], dst_d, st_b[i])
        prep_units.append(f)

    def prep_rows2(s, r):
        def f():
            i = pi[0] % NST
            pi[0] += 1
            src = w2_d.h[s, r * 256:(r + 1) * 256, :].rearrange("(a p) n -> p a n", p=128)
            dst = w2_s.h[s, r * 256:(r + 1) * 256, :].rearrange("(a p) n -> p a n", p=128)
            sf = st_f[i].h[:, 0:2048].rearrange("p (a n) -> p a n", a=2)
            sbb = st_b[i].h[:, 0:2048].rearrange("p (a n) -> p a n", a=2)
            k.dma('sp', sf, src, st_f[i], w2_d)
            k.copy(cast_rr(), st_b[i], st_b[i][:, 0:2048], st_f[i], st_f[i][:, 0:2048])
            k.dma('pool', dst, sbb, w2_s, st_b[i])
        prep_units.append(f)

    def prep_ffn(s):
        for kc in range(NC_):
            prep_rows(w1_d, w1_s, w1_d.h[s, kc * 128:(kc + 1) * 128, :], w1_s.h[s, kc * 128:(kc + 1) * 128, :], DFF)
            prep_rows(w3_d, w3_s, w3_d.h[s, kc * 128:(kc + 1) * 128, :], w3_s.h[s, kc * 128:(kc + 1) * 128, :], DFF)
        for r in range(11):
            prep_rows2(s, r)

    def prep_mat(src_d, dst_d, ncols):
        for kc in range(NC_):
            for c0 in range(0, ncols, 2048):
                n = min(2048, ncols - c0)
                prep_rows(src_d, dst_d, src_d.h[kc * 128:(kc + 1) * 128, c0:c0 + n],
                          dst_d.h[kc * 128:(kc + 1) * 128, c0:c0 + n], n)

    prep_ffn(0)
    per_blk = (len(prep_units) + 35) // 36
    pu = [0]

    def emit_prep(n):
        for _ in range(n):
            if pu[0] < len(prep_units):
                prep_units[pu[0]]()
                pu[0] += 1

    it = 0
    for l in range(2):
        for nb4 in range(18):
            at = adat[it % NADA]
            it += 1
            k.dma('sp', at[:, :, :], adaw_d.h[l].rearrange("(kc p) n -> p kc n", p=128)[:, :, nb4 * 512:(nb4 + 1) * 512],
                  at, adaw_d)
            pm = PS[1 + (nb4 % 2)]
            for q in range(4):
                for kc in range(NC_):
                    k.mm(pm, pm[:, q * 2:q * 2 + 2], at, at[:, kc, q * 128:(q + 1) * 128], silu_c, silu_c[:, kc, :],
                         start=(kc == 0), stop=(kc == NC_ - 1))
            for q in range(4):
                nb = nb4 * 4 + q
                k.ts('dve', modv[l], modv[l][:, nb, :], pm, pm[:, q * 2:q * 2 + 2],
                     adab[:, l * 72 + nb:l * 72 + nb + 1], None, ALU.add, extra=[adab])
            emit_prep(per_blk)
        k.ts('pool', sc1[l], sc1[l][:, :, :], modv[l], modv[l][:, :, :], 1.0, None, ALU.add)
        for j in range(3):
            cc = (1.0 if j == 1 else 0.5) / ALPHA
            g0 = (3 * j + 2) * 8
            k.ts('pool', cf[l], cf[l][:, g0:g0 + 8, :], modv[l], modv[l][:, g0:g0 + 8, :], cc, None, ALU.mult)
    emit_prep(len(prep_units))
    k.release(mk0)
    mk2 = k.mark()
    xin = [k.sb(f"xin{i}", [128, D], F32) for i in range(2)]
    ev_rr = RR(['dve', 'act'])

    def load_x(src_d, r0, Xt, t0, i):
        xi = xin[i % 2]
        k.dma('sp', xi[:, :], src_d[r0:r0 + 128, :], xi, src_d)
        for g in range(2):
            pt = PS[2 + (2 * i + g) % 4]
            for q in range(4):
                c = g * 4 + q
                k.tr(pt, pt[:, q * 128:(q + 1) * 128], xi, xi[:, c * 128:(c + 1) * 128], ident, ident[:, :])
            k.copy(ev_rr(), Xt, Xt.h[:, g * 4:(g + 1) * 4, t0:t0 + 128],
                   pt, pt.h[:, :].rearrange("p (q t) -> p q t", q=4))

    for t in range(NTT):
        for b in range(4):
            if t < 4:
                load_x(xs_d, t * TT + b * 128, X[t], b * 128, t * 4 + b)
            else:
                load_x(xp_d, b * 128, X[t], b * 128, t * 4 + b)

    k.release(mk2)
    mod_rr = RR(['act', 'pool'])

    def modulate(l, j, t):
        col = 0 if t < 4 else 1
        for c in range(NC_):
            s_ap = sc1[l][:, (3 * j + 1) * 8 + c, col:col + 1]
            b_ap = modv[l][:, (3 * j) * 8 + c, col:col + 1]
            e = mod_rr()
            if e == 'act':
                k.act(H[t], H[t][:, c, :], X[t], X[t][:, c, :], AF.Identity, bias=b_ap, scale=s_ap,
                      extra=[sc1[l], modv[l]])
            else:
                k.ts('pool', H[t], H[t][:, c, :], X[t], X[t][:, c, :], s_ap, b_ap, ALU.mult, ALU.add,
                     extra=[sc1[l], modv[l]])

    class NS:
        pass
    F = NS()

    def alloc_ln():
        F.zb = k.sb("zb", [128, 16, TT], BF16)
        F.mean_sb = k.sb("ln_mean", [128, TT], F32)
        F.rstd_sb = k.sb("ln_rstd", [128, TT], F32)
        F.tmp_sb = k.sb("ln_tmp", [128, TT], F32)
        F.lt = [k.sb(f"ln_t{i}", [128, TT], F32) for i in range(2)]

    def alloc_ffn():
        F.G = k.sb("G", [128, NFC, TT], BF16)
        F.zb = F.G
        F.mean_sb = k.sb("ln_mean", [128, TT], F32)
        F.rstd_sb = k.sb("ln_rstd", [128, TT], F32)
        F.tmp_sb = k.sb("ln_tmp", [128, TT], F32)
        F.w1r = [k.sb(f"w1r{i}", [128, NC_, 256], BF16) for i in range(NW13)]
        F.w3r = [k.sb(f"w3r{i}", [128, NC_, 256], BF16) for i in range(NW13)]
        F.w2r = [k.sb(f"w2r{i}", [128, 512], BF16) for i in range(NW2)]
        F.sg = [k.sb(f"sg{i}", [128, TT], F32) for i in range(2)]
        F.lt = F.sg
        F.bgf = [k.sb(f"bgf{i}", [128, BGW], F32) for i in range(2)]
        F.bgb = [k.sb(f"bgb{i}", [128, BGW], BF16) for i in range(2)]

    NW13 = 2
    NW2 = 6

    def layernorm(l, j, t):
        Xt = X[t]
        pm, pq = PS[0], PS[1]
        zb, zq = F.zb, F.zb
        mean_sb, rstd_sb, tmp_sb, lt = F.mean_sb, F.rstd_sb, F.tmp_sb, F.lt
        for c in range(NC_):
            k.copy('dve' if c % 2 == 0 else 'pool', zb, zb[:, c, :], Xt, Xt[:, c, :])
            k.act(zq, zq[:, 8 + c, :], Xt, Xt[:, c, :], AF.Square)
        for c in range(NC_):
            k.mm(pm, pm[:, :], ones_bf, ones_bf[:, :], zb, zb[:, c, :], start=(c == 0), stop=(c == NC_ - 1))
        for c in range(NC_):
            k.mm(pq, pq[:, :], ones_bf, ones_bf[:, :], zq, zq[:, 8 + c, :], start=(c == 0), stop=(c == NC_ - 1))
        k.copy('dve', mean_sb, mean_sb[:, :], pm, pm[:, :])
        k.tt('dve', tmp_sb, tmp_sb[:, :], mean_sb, mean_sb[:, :], mean_sb, mean_sb[:, :], ALU.mult)
        k.tt('dve', tmp_sb, tmp_sb[:, :], pq, pq[:, :], tmp_sb, tmp_sb[:, :], ALU.subtract)
        k.ts('dve', tmp_sb, tmp_sb[:, :], tmp_sb, tmp_sb[:, :], 0.0, LN_EPS, ALU.max, ALU.add)
        k.act(rstd_sb, rstd_sb[:, :], tmp_sb, tmp_sb[:, :], AF.Ln)
        k.act(rstd_sb, rstd_sb[:, :], rstd_sb, rstd_sb[:, :], AF.Exp, scale=-0.5)
        gi = (l * 3 + j) * 8
        for c in range(NC_):
            tb = lt[c % 2]
            e = 'pool' if c % 4 == 3 else 'dve'
            k.tt(e, tb, tb[:, :], Xt, Xt[:, c, :], mean_sb, mean_sb[:, :], ALU.subtract)
            k.tt(e, tb, tb[:, :], tb, tb[:, :], rstd_sb, rstd_sb[:, :], ALU.mult)
            k.act(Xt, Xt[:, c, :], tb, tb[:, :], AF.Identity, bias=lnp[:, 48 + gi + c:48 + gi + c + 1],
                  scale=lnp[:, gi + c:gi + c + 1], extra=[lnp])

    BGW = 1416
    BG = NS()
    BG.q = []
    BG.pos = 0
    BG.tick = 0
    BG.stride = 1
    BG.i = 0

    def bg_add(src_d, dst_d, src2d, dst2d, R, C, W=None):
        W = W or BGW
        npc = (C + W - 1) // W
        w = (C + npc - 1) // npc
        for rc in range(R // 128):
            for pc in range(npc):
                c0 = pc * w
                n = min(w, C - c0)
                BG.q.append((src_d, dst_d, src2d[rc * 128:(rc + 1) * 128, c0:c0 + n],
                             dst2d[rc * 128:(rc + 1) * 128, c0:c0 + n], n))

    def bg_add_ffn(si, W=None):
        bg_add(w1_d, w1_s, w1_d.h[si], w1_s.h[si], D, DFF, W)
        bg_add(w3_d, w3_s, w3_d.h[si], w3_s.h[si], D, DFF, W)
        bg_add(w2_d, w2_s, w2_d.h[si], w2_s.h[si], DFF, D, W)

    def bg_load(u):
        src_d, dst_d, sap, dap, n = BG.q[u]
        i = u % 2
        k.dma('pool', F.bgf[i][:, 0:n], sap, F.bgf[i], src_d)

    def bg_cast_store(u):
        src_d, dst_d, sap, dap, n = BG.q[u]
        i = u % 2
        k.copy('pool', F.bgb[i], F.bgb[i][:, 0:n], F.bgf[i], F.bgf[i][:, 0:n])
        k.dma('pool', dap, F.bgb[i][:, 0:n], dst_d, F.bgb[i])

    def bg_start():
        n = len(BG.q) - BG.pos
        BG.stride = max(1, 300 // max(n, 1))
        BG.tick = 0
        BG.loaded = BG.pos - 1

    def bg_step():
        u = BG.pos
        if BG.loaded < u:
            bg_load(u)
            BG.loaded = u
        if u + 1 < len(BG.q) and BG.loaded < u + 1:
            bg_load(u + 1)
            BG.loaded = u + 1
        bg_cast_store(u)
        BG.pos += 1

    def bg_tick():
        BG.tick += 1
        if BG.tick % BG.stride == 0 and BG.pos < len(BG.q):
            bg_step()

    def bg_flush():
        while BG.pos < len(BG.q):
            bg_step()

    wc = {'a': 0, 'b': 0}

    def ffn(l, j, t):
        s = l * 2 + (0 if j == 0 else 1)
        col = 0 if t < 4 else 1
        Ht = H[t]
        G, w1r, w3r, w2r, sg = F.G, F.w1r, F.w3r, F.w2r, F.sg
        for fc in range(NFC):
            if fc % 2 == 0:
                i = wc['a'] % NW13
                wc['a'] += 1
                k.dma('sp', w1r[i][:, :, :],
                      w1_s.h[s].rearrange("(kc p) n -> p kc n", p=128)[:, :, fc * 128:(fc + 2) * 128], w1r[i], w1_s)
                k.dma('sp', w3r[i][:, :, :],
                      w3_s.h[s].rearrange("(kc p) n -> p kc n", p=128)[:, :, fc * 128:(fc + 2) * 128], w3r[i], w3_s)
            wo = (fc % 2) * 128
            pa, pb = PS[(fc % 2) * 2], PS[(fc % 2) * 2 + 1]
            for kc in range(NC_):
                k.mm(pa, pa[:, :], w1r[i], w1r[i][:, kc, wo:wo + 128], Ht, Ht[:, kc, :], start=(kc == 0), stop=(kc == NC_ - 1))
            for kc in range(NC_):
                k.mm(pb, pb[:, :], w3r[i], w3r[i][:, kc, wo:wo + 128], Ht, Ht[:, kc, :], start=(kc == 0), stop=(kc == NC_ - 1))
            sgi = sg[fc % 2]
            k.act(sgi, sgi[:, :], pa, pa[:, :], AF.Silu)
            k.tt('dve', G, G[:, fc, :], sgi, sgi[:, :], pb, pb[:, :], ALU.mult)
            bg_tick()
        for dg in range(2):
            for fc in range(NFC):
                i = wc['b'] % NW2
                wc['b'] += 1
                k.dma('sp', w2r[i][:, :], w2_s.h[s, fc * 128:(fc + 1) * 128, dg * 512:(dg + 1) * 512], w2r[i], w2_s)
                for q in range(4):
                    py = PS[4 + q]
                    k.mm(py, py[:, :], w2r[i], w2r[i][:, q * 128:(q + 1) * 128], G, G[:, fc, :],
                         start=(fc == 0), stop=(fc == NFC - 1))
                bg_tick()
            for q in range(4):
                c = dg * 4 + q
                py = PS[4 + q]
                k.stt(X[t], X[t][:, c, :], py, py[:, :], cf[l][:, (3 * j + 2) * 8 + c, col:col + 1],
                      X[t], X[t][:, c, :], ALU.mult, ALU.add, extra=[cf[l]])
        layernorm(l, j, t)

    def hs(seq, kc, i0, n):
        if seq == 0:
            t, c0 = i0 // TT, i0 % TT
        else:
            t, c0 = 4, (seq - 1) * LP + i0
        return H[t], H[t][:, kc, c0:c0 + n]

    M = NS()
    NWM = 4
    wmc = [0]

    def alloc_wm(n=3):
        M.wm = [k.sb(f"wm{i}", [128, NC_, 256], BF16) for i in range(n)]

    def next_wm():
        w = M.wm[wmc[0] % len(M.wm)]
        wmc[0] += 1
        return w

    def wload(wt, dcol, scr, col0, n):
        k.dma('sp', wt[:, :, dcol:dcol + n], scr.h.rearrange("(kc p) n -> p kc n", p=128)[:, :, col0:col0 + n], wt, scr)

    def proj_fm(seq, i0, n, wt, wc0, Mo, ps, ps_ap):
        for kc in range(NC_):
            Hb, hap = hs(seq, kc, i0, n)
            k.mm(ps, ps_ap, wt, wt[:, kc, wc0:wc0 + Mo], Hb, hap, start=(kc == 0), stop=(kc == NC_ - 1))

    def proj_tm(seq, i0, wt, wc0, ncols, ps, ps_ap):
        for kc in range(NC_):
            Hb, hap = hs(seq, kc, i0, 128)
            k.mm(ps, ps_ap, Hb, hap, wt, wt[:, kc, wc0:wc0 + ncols], start=(kc == 0), stop=(kc == NC_ - 1))

    def rstd_from(dst, dst_ap, src, src_ap, scale, eps):
        k.act(dst, dst_ap, src, src_ap, AF.Ln, bias=float(eps), scale=float(scale))
        k.act(dst, dst_ap, dst, dst_ap, AF.Exp, scale=-0.5)

    def attention(seq, L, cfg):
        nt = L // 128
        ctx = cfg['ctx'] if seq == 0 else None
        rope = cfg['rope'] and seq == 0
        band = cfg['band'] and seq == 0
        qknorm = cfg['qknorm']
        sinkexp = cfg['sinkexp']
        win = cfg['win']
        nctx = 4 if ctx is not None else 0
        qhalf = bool(cfg.get('qhalf')) and seq == 0
        mk = k.mark()
        bgl = cfg.get('bg') if seq == 0 else None
        if bgl is not None:
            F.bgf = [k.sb(f"abgf{i}", [128, 704], F32) for i in range(2)]
            F.bgb = [k.sb(f"abgb{i}", [128, 704], BF16) for i in range(2)]
            bgl()
            n_ = len(BG.q) - BG.pos
            BG.stride = max(1, 240 // max(n_, 1))
            BG.tick = 0
            BG.loaded = BG.pos - 1
        QT = k.sb("QT", [128, 4, L], BF16)
        KT = k.sb("KT", [128, 2, L + 128 * nctx], BF16)
        VA = k.sb("VA", [128, nt + nctx, 2, 128], BF16)
        k.memset('pool', VA, VA[:, :, :, 64:128], 1.0)
        mkA = k.mark()
        alloc_wm(2 if bgl is not None else 3)
        step = min(L, 512)
        qf = [k.sb(f"qf{i}", [128, step], F32) for i in range(2)]
        t1 = [k.sb(f"t1{i}", [128, step], F32) for i in range(1 if bgl is not None else 2)] * 2
        if rope:
            ropeT = k.sb("ropeT", [128, 2, step], F32)
        if qknorm:
            sq = k.sb("sq", [128, step], BF16)
            rs = k.sb("rs", [128, step], F32)
        kvo = [k.sb(f"kvo{i}", [128, 256], F32) for i in range(2)]
        if qknorm:
            kt2 = k.sb("kt2", [128, 128], F32)
            kss = k.sb("kss", [128, 2], F32)
        bi = 0
        for i0 in range(0, L, step):
            n = step
            if rope:
                k.dma('sp', ropeT[:, 0, :], cos_d[:, i0:i0 + n], ropeT, cos_d)
                k.dma('sp', ropeT[:, 1, :], sin_d[:, i0:i0 + n], ropeT, sin_d)
            for blk in range(DBG.get('nblk', 6)):
                if blk < 4 and qhalf and i0 >= L // 2:
                    continue
                wt = next_wm()
                if blk < 4:
                    wload(wt, 0, win, cfg['qcol'] + blk * 128, 128)
                    dst, dst_ap = QT, QT[:, blk, i0:i0 + n]
                    gcol = cfg.get('qg')
                else:
                    kv = blk - 4
                    wload(wt, 0, win, cfg['kcol'] + kv * 64, 64)
                    wload(wt, 64, win, cfg['kcol'] + kv * 64, 64)
                    dst, dst_ap = KT, KT[:, kv, i0:i0 + n]
                    gcol = cfg.get('kg')
                pp = PS[blk % 2]
                proj_fm(seq, i0, n, wt, 0, 128, pp, pp[:, :n])
                if bgl is not None:
                    bg_tick()
                qfi = qf[bi % 2]
                t1i = t1[bi % 2]
                bi += 1
                if qknorm:
                    k.act(sq, sq[:, :n], pp, pp[:, :n], AF.Square)
                    pn = PS[2]
                    k.mm(pn, pn[:, :n], bones64, bones64[:, :], sq, sq[:, :n])
                    rstd_from(rs, rs[:, :n], pn, pn[:, :n], 1.0, 1e-6)
                    k.stt(qfi, qfi[:, :n], pp, pp[:, :n], gcol, rs, rs[:, :n], ALU.mult, ALU.mult, extra=[gvec])
                    src, src_ap = qfi, qfi[:, :n]
                elif rope:
                    k.copy('act', qfi, qfi[:, :n], pp, pp[:, :n])
                    src, src_ap = qfi, qfi[:, :n]
                else:
                    src, src_ap = pp, pp[:, :n]
                if rope:
                    pr = PS[3]
                    k.mm(pr, pr[:, :n], RT, RT[:, :], src, src_ap)
                    k.tt('pool', t1i, t1i[:, :n], src, src_ap, ropeT, ropeT[:, 0, :n], ALU.mult)
                    k.tt('dve', src, src_ap, pr, pr[:, :n], ropeT, ropeT[:, 1, :n], ALU.mult)
                    k.tt('pool', dst, dst_ap, t1i, t1i[:, :n], src, src_ap, ALU.add)
                else:
                    if src.space == 'ps':
                        k.copy('act', dst, dst_ap, src, src_ap)
                    else:
                        k.copy('pool', dst, dst_ap, src, src_ap)
            wt = next_wm()
            wload(wt, 0, win, cfg['kcol'], 256)
            for b in range(0 if DBG.get('notm') else n // 128):
                pk = PS[4 + b % 2]
                proj_tm(seq, i0 + b * 128, wt, 0, 256, pk, pk[:, 0:256])
                tile = i0 // 128 + b
                if not DBG.get('nova'):
                    k.copy('dve', VA, VA.h[:, tile, :, 0:64], pk, pk.h[:, 128:256].rearrange("p (kv d) -> p kv d", kv=2))
                if seq > 0 and not DBG.get('noko'):
                    ko = kvo[b % 2]
                    k.copy('dve' if DBG.get('kodve') else 'act', ko, ko[:, :], pk, pk[:, 0:256])
                    if qknorm:
                        k.tt('dve', kt2, kt2[:, :], ko, ko[:, 0:128], ko, ko[:, 0:128], ALU.mult)
                        k.op('dve', lambda: nc.vector.tensor_reduce(out=kss[:, :], in_=kt2.h[:, :].rearrange("p (a d) -> p a d", a=2),
                                                                    op=ALU.add, axis=mybir.AxisListType.X), [kt2], [kss])
                        rstd_from(kss, kss[:, :], kss, kss[:, :], 1.0 / 64, 1e-6)
                        k.tt('dve', kt2, kt2.h[:, :].rearrange("p (a d) -> p a d", a=2),
                             ko, ko.h[:, 0:128].rearrange("p (a d) -> p a d", a=2),
                             kss, kss.h[:, :].unsqueeze(2).broadcast_to([128, 2, 64]), ALU.mult)
                        k.tt('dve', ko, ko[:, 0:128], kt2, kt2[:, :], kgrow, kgrow[:, :], ALU.mult)
                    r0 = i0 + b * 128
                    k.dma('pool', cfg['ko'][seq - 1, r0:r0 + 128, :], ko[:, 0:128], cfg['ko'], ko)
                    k.dma('pool', cfg['vo'][seq - 1, r0:r0 + 128, :], ko[:, 128:256], cfg['vo'], ko)
        if ctx is not None:
            kd, vd = ctx
            kcs = k.sb("kcs", [128, 4, 256], F32)
            vcs = k.sb("vcs", [128, 4, 128], F32)
            k.dma('sp', kcs[:, :, :], kd.h.rearrange("(t p) c -> p t c", p=128), kcs, kd)
            k.dma('sp', vcs[:, :, :], vd.h.rearrange("(t p) c -> p t c", p=128), vcs, vd)
            for tl in range(4):
                for kv in range(2):
                    pt = PS[4 + kv]
                    k.tr(pt, pt[:, 0:128], kcs, kcs[:, tl, kv * 128:(kv + 1) * 128], ident, ident[:, :])
                    k.copy('act', KT, KT[:, kv, L + tl * 128:L + (tl + 1) * 128], pt, pt[:, 0:128])
            k.copy('dve', VA, VA.h[:, nt:nt + 4, :, 0:64], vcs, vcs.h[:, :, :].rearrange("p t (kv d) -> p t kv d", kv=2))
        k.release(mkA)
        if DBG.get('attnA'):
            k.release(mk)
            return
        pT = [k.sb(f"pT{i}", [128, 512], BF16) for i in range(3)]
        osb = [k.sb(f"osb{i}", [128, 512], F32) for i in range(2)]
        rden = k.sb("rden", [64, 512], F32)
        yat = [k.sb(f"yat{i}", [64, 512], BF16) for i in range(2)]
        ya_s = cfg['ya_s'][seq]
        units = []
        steps = []
        for qb in range(nt // 2 if qhalf else nt):
            for kv in range(2):
                if band:
                    tiles = [(kt, (None if kt == qb else ('lo' if kt < qb else 'hi')))
                             for kt in (qb - 1, qb, qb + 1) if 0 <= kt < nt]
                else:
                    tiles = [(kt, None) for kt in range(nt)]
                tiles += [(nt + c, None) for c in range(nctx)]
                u = len(units)
                units.append((qb, kv))
                for ti, (kt, msk) in enumerate(tiles):
                    steps.append((u, kt, msk, ti == 0, ti == len(tiles) - 1))

        def emit_qk(si):
            u, kt, msk, first, last = steps[si]
            qb, kv = units[u]
            p = pT[si % 3]
            for hp in range(2):
                ps_s = PS[4 + 2 * (si % 2) + hp]
                k.mm(ps_s, ps_s.h[:, 0:256].rearrange("p (a q) -> p a q", a=2),
                     KT, KT[hp * 64:(hp + 1) * 64, kv, kt * 128:(kt + 1) * 128],
                     QT, QT[hp * 64:(hp + 1) * 64, kv * 2:kv * 2 + 2, qb * 128:(qb + 1) * 128])
                k.act(p, p[:, hp * 256:(hp + 1) * 256], ps_s, ps_s[:, 0:256], AF.Exp, scale=0.125)
            if msk is not None:
                mt_ = tri_lo4 if msk == 'lo' else tri_hi4
                k.tt('pool', p, p[:, :], p, p[:, :], mt_, mt_[:, :], ALU.mult)
            return p

        def emit_pv(si, p):
            u, kt, msk, first, last = steps[si]
            qb, kv = units[u]
            po = PS[2 + u % 2]
            k.mm(po, po[:, :], VA, VA[:, kt, kv, :], p, p[:, :], start=first, stop=last)
            if not last:
                return
            ob = osb[u % 2]
            k.copy('act', ob, ob[:, :], po, po[:, :])
            pd = PS[1]
            k.mm(pd, pd[0:64, :], shiftT, shiftT[:, :], ob, ob[:, :])
            if sinkexp is not None:
                k.tt('dve', rden, rden[:, :], pd, pd[0:64, :], sinkexp, sinkexp[:, kv, :], ALU.add)
                k.op('dve', lambda: nc.vector.reciprocal(out=rden[:, :], in_=rden[:, :]), [rden], [rden])
            else:
                k.op('dve', lambda: nc.vector.reciprocal(out=rden[:, :], in_=pd[0:64, :]), [pd], [rden])
            ya = yat[u % 2]
            k.tt('dve', ya, ya[:, :], ob, ob[0:64, :], rden, rden[:, :], ALU.mult)
            dstv = ya_s.h.rearrange("p (kv ab hp) l -> p kv hp ab l", kv=2, ab=2, hp=2)[:, kv, :, :, qb * 128:(qb + 1) * 128]
            srcv = ya.h[:, :].rearrange("p (hp ab q) -> p hp ab q", hp=2, ab=2)
            for hp in range(2):
                k.dma('pool', dstv[:, hp, :, :], srcv[:, hp, :, :], ya_s, ya)

        pcur = emit_qk(0)
        for si in range(len(steps)):
            pnext = emit_qk(si + 1) if si + 1 < len(steps) else None
            emit_pv(si, pcur)
            pcur = pnext
            if bgl is not None:
                bg_tick()
        if bgl is not None:
            bg_flush()
        k.release(mk)

    def mlstm(seq, L):
        SEG = min(L, 512)
        nseg = L // SEG
        ncs = SEG // 128
        win = wine_s
        mk = k.mark()
        alloc_wm(2)
        rw = {nm: k.sb("rw_" + nm, [4, SEG], F32) for nm in ('x', 'a', 'mn', 'B', 'r', 'M', 'one')}
        k.memset('pool', rw['one'], rw['one'][:, :], 1.0)
        rows3 = k.sb("rows3", [4, ncs, 3, 128], F32)
        rcol = k.sb("rcol", [128, ncs * 4], F32)
        carB = k.sb("carB", [4, 1], F32)
        carM = k.sb("carM", [4, 1], F32)
        Mprev = k.sb("Mprev", [4, ncs], F32)
        gb = k.sb("gb", [4, 4], F32)
        k.dma('sp', gb[:, :], gbias_d[:, :], gb, gbias_d)
        QTh = [k.sb(f"mQT{h}", [128, SEG], BF16) for h in range(4)]
        KTh = [k.sb(f"mKT{h}", [128, SEG], BF16) for h in range(4)]
        KVh = [k.sb(f"mKV{h}", [128, ncs, 257], BF16) for h in range(4)]
        for h in range(4):
            k.memset('pool', KVh[h], KVh[h][:, :, 256:257], 1.0)
        hseg = [k.sb(f"hseg{h}", [128, SEG], F32) for h in range(4)]
        Cst = [k.sb(f"Cst{h}", [128, 129], F32) for h in range(4)]
        nrep = [k.sb(f"nrep{h}", [128, 128], F32) for h in range(4)]
        onesf = k.sb("onesf", [128, 128], F32)
        k.memset('pool', onesf, onesf[:, :], 1.0)
        ones1b = k.sb("ones1b", [128, 128], BF16)
        k.memset('pool', ones1b, ones1b[:, :], 1.0)
        NB = 4
        Dt = [k.sb(f"Dt{i}", [128, 128], F32) for i in range(NB)]
        cols2 = [k.sb(f"cols2{i}", [128, 2], F32) for i in range(NB)]
        Wt = [k.sb(f"Wt{i}", [128, 128], BF16) for i in range(NB)]
        Qs = [k.sb(f"Qs{i}", [128, 128], F32) for i in range(NB)]
        dd = [k.sb(f"dd{i}", [128, 128], F32) for i in range(NB)]
        wsb = [k.sb(f"wsb{i}", [128, 1], F32) for i in range(NB)]
        Ks = [k.sb(f"Ks{i}", [128, 128], BF16) for i in range(NB)]
        hfl = k.sb("hfl", [128, SEG], F32)
        rsb = k.sb("rsb", [128, SEG], F32)
        sgb = k.sb("sgb", [128, SEG], F32)
        ybt = k.sb("ybt", [128, SEG], BF16)
        sqb = ybt
        ui = 0
        for d in range(2):
            fwd = (d == 0)
            for h in range(4):
                if seq == 0:
                    k.dma('sp', Cst[h][:, 0:128], stC_d[d, h, :, :], Cst[h], stC_d)
                    k.dma('sp', Cst[h][:, 128:129], stn_d.h[d, h, :].rearrange("(p o) -> p o", o=1), Cst[h], stn_d)
                else:
                    k.memset('pool', Cst[h], Cst[h][:, :], 0.0)
                k.ts('pool', nrep[h], nrep[h][:, :], onesf, onesf[:, :], Cst[h][:, 128:129], None, ALU.mult, extra=[Cst[h]])
            k.memset('pool', carB, carB[:, :], 0.0)
            if seq == 0:
                k.dma('sp', carM[:, :], stm_d.h[d, :].rearrange("(p o) -> p o", o=1), carM, stm_d)
            else:
                k.memset('pool', carM, carM[:, :], 0.0)
            segs = list(range(nseg)) if fwd else list(range(nseg - 1, -1, -1))
            for sg_ in segs:
                i0 = sg_ * SEG
                wt = next_wm()
                wload(wt, 0, win, 2304, 16)
                pgi, pgf = PS[0], PS[1]
                proj_fm(seq, i0, SEG, wt, (2 * d) * 4, 4, pgi, pgi[0:4, :SEG])
                proj_fm(seq, i0, SEG, wt, (2 * d + 1) * 4, 4, pgf, pgf[0:4, :SEG])
                x, a_, mn, B, r, Mx, one = (rw[nm] for nm in ('x', 'a', 'mn', 'B', 'r', 'M', 'one'))
                mt, tmp = a_, mn
                k.act(x, x[:, :], pgf, pgf[0:4, :SEG], AF.Identity, bias=gb[:, 2 * d + 1:2 * d + 2], extra=[gb])
                k.act(r, r[:, :], pgi, pgi[0:4, :SEG], AF.Identity, bias=gb[:, 2 * d:2 * d + 1], extra=[gb])
                k.act(a_, a_[:, :], x, x[:, :], AF.Abs)
                k.act(a_, a_[:, :], a_, a_[:, :], AF.Exp, scale=-1.0)
                k.act(a_, a_[:, :], a_, a_[:, :], AF.Ln, bias=1.0)
                k.ts('dve', mn, mn[:, :], x, x[:, :], -1.0, 0.0, ALU.mult, ALU.max)
                k.tt('dve', x, x[:, :], mn, mn[:, :], a_, a_[:, :], ALU.add)
                k.ts('dve', x, x[:, :], x, x[:, :], -1.0, None, ALU.mult)
                rv = (lambda t_: t_[:, :]) if fwd else (lambda t_: t_[:, ::-1])
                k.op('dve', lambda: nc.vector.tensor_tensor_scan(out=rv(B), data0=one[:, :], data1=rv(x), initial=carB[:, 0:1],
                                                                 op0=ALU.mult, op1=ALU.add), [one, x, carB], [B])
                k.tt('dve', r, r[:, :], r, r[:, :], B, B[:, :], ALU.subtract)
                k.op('dve', lambda: nc.vector.tensor_tensor_scan(out=rv(Mx), data0=one[:, :], data1=rv(r), initial=carM[:, 0:1],
                                                                 op0=ALU.mult, op1=ALU.max), [one, r, carM], [Mx])
                k.tt('dve', mt, mt[:, :], B, B[:, :], Mx, Mx[:, :], ALU.add)
                Mv = Mx.h[:, :].rearrange("p (c t) -> p c t", t=128)
                if fwd:
                    k.copy('dve', Mprev, Mprev[:, 0:1], carM, carM[:, 0:1])
                    if ncs > 1:
                        k.copy('dve', Mprev, Mprev[:, 1:ncs], Mx, Mv[:, 0:ncs - 1, 127])
                else:
                    k.copy('dve', Mprev, Mprev[:, ncs - 1:ncs], carM, carM[:, 0:1])
                    if ncs > 1:
                        k.copy('dve', Mprev, Mprev[:, 0:ncs - 1], Mx, Mv[:, 1:ncs, 0])
                last = SEG - 1 if fwd else 0
                k.copy('dve', carB, carB[:, :], B, B[:, last:last + 1])
                k.copy('dve', carM, carM[:, :], Mx, Mx[:, last:last + 1])
                k.ts('dve', rows3, rows3.h[:, :, 0, :], Mx, Mv, -1.0, None, ALU.mult)
                k.tt('dve', tmp, tmp.h[:, :].rearrange("p (c t) -> p c t", t=128), Mprev,
                     Mprev.h[:, :].unsqueeze(2).broadcast_to([4, ncs, 128]), Mx, Mv, ALU.subtract)
                k.act(rows3, rows3.h[:, :, 1, :], tmp, tmp.h[:, :].rearrange("p (c t) -> p c t", t=128), AF.Exp)
                k.act(rows3, rows3.h[:, :, 2, :], mt, mt.h[:, :].rearrange("p (c t) -> p c t", t=128), AF.Exp, scale=-1.0)
                prc = PS[2]
                for c in range(ncs):
                    k.tr(prc, prc[:, c * 4:(c + 1) * 4], r, r[0:4, c * 128:(c + 1) * 128], ident, ident[0:4, 0:4])
                k.copy('dve', rcol, rcol[:, :], prc, prc[:, 0:ncs * 4])
                if seq > 0 and sg_ == segs[-1]:
                    k.dma('pool', bm_o.h[seq - 1, d, :].rearrange("(p o) -> p o", o=1), mt[:, last:last + 1], bm_o, mt)
                for h in range(4):
                    wt = next_wm()
                    wload(wt, 0, win, 768 + h * 128, 128)
                    wload(wt, 128, win, 1280 + h * 128, 128)
                    pq, pk_ = PS[0], PS[1]
                    proj_fm(seq, i0, SEG, wt, 0, 128, pq, pq[:, :SEG])
                    k.copy('act', QTh[h], QTh[h][:, :], pq, pq[:, :SEG])
                    proj_fm(seq, i0, SEG, wt, 128, 128, pk_, pk_[:, :SEG])
                    k.act(KTh[h], KTh[h][:, :], pk_, pk_[:, :SEG], AF.Identity, scale=128 ** -0.5)
                    wt2 = next_wm()
                    wload(wt2, 0, win, 1280 + h * 128, 128)
                    wload(wt2, 128, win, 1792 + h * 128, 128)
                    for c in range(ncs):
                        pkv = PS[2 + c % 2]
                        proj_tm(seq, i0 + c * 128, wt2, 0, 256, pkv, pkv[:, 0:256])
                        k.act(KVh[h], KVh[h][:, c, 0:128], pkv, pkv[:, 0:128], AF.Identity, scale=128 ** -0.5)
                        k.copy('dve', KVh[h], KVh[h][:, c, 128:256], pkv, pkv[:, 128:256])
                chunks = list(range(ncs)) if fwd else list(range(ncs - 1, -1, -1))
                edge = 127 if fwd else 0
                msk = tri_hi4 if fwd else tri_lo4
                for c in chunks:
                    cs = slice(c * 128, (c + 1) * 128)
                    bA = [PS[2 * h] for h in range(4)]
                    bB = [PS[2 * h + 1] for h in range(4)]
                    for h in range(4):
                        k.mm(bA[h], bA[h][:, 0:384], sel4, sel4[:, h, :], rows3, rows3.h[:, c, :, :].rearrange("p a t -> p (a t)"))
                        k.mm(bA[h], bA[h][:, 384:512], KTh[h], KTh[h][:, cs], QTh[h], QTh[h][:, cs])
                    for h in range(4):
                        u = h
                        k.act(Dt[u], Dt[u][:, :], bA[h], bA[h][:, 0:128], AF.Exp, bias=rcol[:, c * 4 + h:c * 4 + h + 1], extra=[rcol])
                        k.copy('act', cols2[u], cols2[u][:, :], bA[h], bA[h][:, edge:edge + 129:128])
                        k.tt('pool', Dt[u], Dt[u][:, :], Dt[u], Dt[u][:, :], msk, msk[:, 0:128], ALU.mult)
                        k.tt('dve', Qs[u], Qs[u][:, :], QTh[h], QTh[h][:, cs], bA[h], bA[h][:, 128:256], ALU.mult)
                        k.tt('dve', Wt[u], Wt[u][:, :], Dt[u], Dt[u][:, :], bA[h], bA[h][:, 384:512], ALU.mult)
                    for h in range(4):
                        u = h
                        k.mm(bB[h], bB[h][:, 0:128], Cst[h], Cst[h][:, 0:128], Qs[u], Qs[u][:, :], start=True, stop=False)
                        k.mm(bB[h], bB[h][:, 0:128], KVh[h], KVh[h][:, c, 128:256], Wt[u], Wt[u][:, :], start=False, stop=True)
                        k.mm(bB[h], bB[h][:, 128:256], nrep[h], nrep[h][:, :], Qs[u], Qs[u][:, :], start=True, stop=False)
                        k.mm(bB[h], bB[h][:, 128:256], ones1b, ones1b[:, :], Wt[u], Wt[u][:, :], start=False, stop=True)
                    for h in range(4):
                        u = h
                        k.act(dd[u], dd[u][:, :], bB[h], bB[h][:, 128:256], AF.Abs)
                        k.act(wsb[u], wsb[u][:, :], rcol, rcol[:, c * 4 + h:c * 4 + h + 1], AF.Exp, bias=cols2[u][:, 0:1],
                              extra=[cols2[u]])
                        k.tt('dve', dd[u], dd[u][:, :], dd[u], dd[u][:, :], bA[h], bA[h][:, 256:384], ALU.max)
                        k.op('dve', lambda: nc.vector.reciprocal(out=dd[u][:, :], in_=dd[u][:, :]), [dd[u]], [dd[u]])
                        k.tt('dve', hseg[h], hseg[h][:, cs], bB[h], bB[h][:, 0:128], dd[u], dd[u][:, :], ALU.mult)
                        k.ts('pool', Ks[u], Ks[u][:, :], KVh[h], KVh[h][:, c, 0:128], wsb[u][:, 0:1], None, ALU.mult,
                             extra=[wsb[u]])
                    for h in range(4):
                        u = h
                        k.mm(bB[h], bB[h][:, 256:385], Ks[u], Ks[u][:, :], KVh[h], KVh[h][:, c, 128:257])
                    for h in range(4):
                        u = h
                        k.stt(Cst[h], Cst[h][:, :], Cst[h], Cst[h][:, :], cols2[u][:, 1:2], bB[h], bB[h][:, 256:385], ALU.mult, ALU.add,
                              extra=[cols2[u]])
                        k.ts('pool', nrep[h], nrep[h][:, :], onesf, onesf[:, :], Cst[h][:, 128:129], None, ALU.mult,
                             extra=[Cst[h]])
                for h in range(4):
                    if fwd:
                        k.dma('pool', hf_s[seq][h, :, i0:i0 + SEG], hseg[h][:, :], hf_s[seq], hseg[h])
                    else:
                        k.dma('sp', hfl[:, :], hf_s[seq][h, :, i0:i0 + SEG], hfl, hf_s[seq])
                        k.tt('pool', hfl, hfl[:, :], hfl, hfl[:, :], hseg[h], hseg[h][:, :], ALU.add)
                        k.act(sqb, sqb[:, :], hfl, hfl[:, :], AF.Square)
                        pr = PS[0]
                        k.mm(pr, pr[:, :SEG], ones_bf, ones_bf[:, :], sqb, sqb[:, :])
                        rstd_from(rsb, rsb[:, :], pr, pr[:, :SEG], 8.0, 1e-6)
                        wt = next_wm()
                        wload(wt, 0, win, 2320 + h * 128, 128)
                        po_ = PS[1]
                        proj_fm(seq, i0, SEG, wt, 0, 128, po_, po_[:, :SEG])
                        k.act(sgb, sgb[:, :], po_, po_[:, :SEG], AF.Sigmoid)
                        k.stt(hfl, hfl[:, :], hfl, hfl[:, :], bng[:, h:h + 1], rsb, rsb[:, :], ALU.mult, ALU.mult, extra=[bng])
                        k.tt('pool', ybt, ybt[:, :], hfl, hfl[:, :], sgb, sgb[:, :], ALU.mult)
                        k.dma('pool', yb_s[seq][:, h, i0:i0 + SEG], ybt[:, :], yb_s[seq], ybt)
            if seq > 0:
                for h in range(4):
                    k.dma('pool', bC_o[seq - 1, d, h, :, :], Cst[h][:, 0:128], bC_o, Cst[h])
                    k.dma('pool', bn_o.h[seq - 1, d, h, :].rearrange("(p o) -> p o", o=1), Cst[h][:, 128:129], bn_o, Cst[h])
        k.release(mk)
    def hgrn(seq, L):
        SEG = min(L, 512)
        nseg = L // SEG
        ngr = SEG // 128
        nch = SEG // 32
        win = wino_s
        mk = k.mark()
        wA2 = [k.sb(f"gwA{i}", [128, NC_, 256], BF16) for i in range(2)]
        wB2 = [k.sb(f"gwB{i}", [128, NC_, 128], BF16) for i in range(2)]
        onesS = k.sb("onesS", [128, SEG], F32)
        k.memset('pool', onesS, onesS[:, :], 1.0)
        fT2 = [k.sb(f"fT{i}", [128, SEG], F32) for i in range(2)]
        kT2 = [k.sb(f"kT{i}", [128, SEG], F32) for i in range(2)]
        eT2 = [k.sb(f"eT{i}", [128, SEG], F32) for i in range(2)]
        Zh2 = [k.sb(f"gZ{i}", [128, SEG], F32) for i in range(2)]
        khf2 = [k.sb(f"gkh{i}", [128, SEG], F32) for i in range(2)]
        qT = [k.sb(f"gqT{h}", [128, SEG], F32) for h in range(4)]
        qb2 = [k.sb(f"gqb{i}", [128, SEG], BF16) for i in range(2)]
        Kmix = [k.sb(f"gKmix{i}", [128, 4, 128], BF16) for i in range(2)]
        tmpE = [k.sb(f"gtmpE{i}", [128, 128], F32) for i in range(2)]
        Kh = [k.sb(f"gKh{h}", [128, ngr, 128], BF16) for h in range(4)]
        Vt = [k.sb(f"gVt{h}", [128, ngr, 128], BF16) for h in range(4)]
        att = [k.sb(f"gatt{h}", [128, ngr, 128], BF16) for h in range(4)]
        oseg = [k.sb(f"goseg{h}", [128, SEG], F32) for h in range(4)]
        dec = [k.sb(f"gdec{h}", [128, ngr], F32) for h in range(4)]
        refc2 = [k.sb(f"grefc{i}", [128, nch], F32) for i in range(2)]
        S = [k.sb(f"gS{h}", [128, 128], F32) for h in range(4)]
        carZ = [k.sb(f"gcarZ{h}", [128, 1], F32) for h in range(4)]
        ofl, rsb, sgb = fT2[0], fT2[1], kT2[0]
        sqb, yct = qb2[0], qb2[1]
        v32 = lambda t_: t_.h[:, :].rearrange("p (c t) -> p c t", t=32)
        v128 = lambda t_: t_.h[:, :].rearrange("p (c t) -> p c t", t=128)
        for d in range(2):
            fwd = (d == 0)
            for i in range(2):
                k.memset('pool', Kmix[i], Kmix[i][:, :, :], 0.0)
            for h in range(4):
                if seq == 0:
                    k.dma('sp', S[h][:, :], stS_d[d, h, :, :], S[h], stS_d)
                else:
                    k.memset('pool', S[h], S[h][:, :], 0.0)
                k.memset('pool', carZ[h], carZ[h][:, :], 0.0)
            segs = list(range(nseg)) if fwd else list(range(nseg - 1, -1, -1))
            if seq == 0 and fwd:
                segs = segs[:nseg // 2]
            rv = (lambda t_: t_[:, :]) if fwd else (lambda t_: t_[:, ::-1])
            msk = tri_hi4 if fwd else tri_lo4
            for sg_ in segs:
                i0 = sg_ * SEG
                so = (seq == 0 and sg_ >= nseg // 2)
                def head_prep(h, par):
                    fT, kT, eT, Zh, khf, qb, refc = fT2[par], kT2[par], eT2[par], Zh2[par], khf2[par], qb2[par], refc2[par]
                    lfT = eT
                    PB = 4 * par
                    wt = wA2[par]
                    wload(wt, 0, win, h * 128, 128)
                    wload(wt, 128, win, 512 * (1 + d) + h * 128, 128)
                    pq, pf = PS[PB + 0], PS[PB + 1]
                    if not so:
                        proj_fm(seq, i0, SEG, wt, 0, 128, pq, pq[:, :SEG])
                        yield
                    proj_fm(seq, i0, SEG, wt, 128, 128, pf, pf[:, :SEG])
                    yield
                    if not so:
                        k.act(qT[h], qT[h][:, :], pq, pq[:, :SEG], AF.Silu)
                        yield
                    k.act(fT, fT[:, :], pf, pf[:, :SEG], AF.Sigmoid)
                    yield
                    k.ts('dve', fT, fT[:, :], fT, fT[:, :], oml[:, h:h + 1], lbc[:, h:h + 1], ALU.mult, ALU.add, extra=[oml, lbc])
                    yield
                    k.act(lfT, lfT[:, :], fT, fT[:, :], AF.Ln)
                    yield
                    k.ts('pool', kT, kT[:, :], fT, fT[:, :], -1.0, 1.0, ALU.mult, ALU.add)
                    yield
                    k.op('dve', lambda: nc.vector.tensor_tensor_scan(out=rv(Zh), data0=onesS[:, :], data1=rv(lfT),
                                                                     initial=carZ[h][:, 0:1], op0=ALU.mult, op1=ALU.add),
                         [onesS, lfT, carZ[h]], [Zh])
                    yield
                    Zv = v32(Zh)
                    Zg = v128(Zh)
                    if fwd:
                        k.copy('dve', refc, refc[:, 0:1], carZ[h], carZ[h][:, 0:1])
                        yield
                        k.copy('dve', refc, refc[:, 1:nch], Zh, Zv[:, 0:nch - 1, 31])
                        yield
                        refg_ap = refc[:, 0:nch:4]
                        edgeg_ap = Zg[:, :, 127]
                    else:
                        k.copy('dve', refc, refc[:, nch - 1:nch], carZ[h], carZ[h][:, 0:1])
                        yield
                        k.copy('dve', refc, refc[:, 0:nch - 1], Zh, Zv[:, 1:nch, 0])
                        yield
                        refg_ap = refc[:, 3:nch:4]
                        edgeg_ap = Zg[:, :, 0]
                    last = SEG - 1 if fwd else 0
                    k.copy('dve', carZ[h], carZ[h][:, :], Zh, Zh[:, last:last + 1])
                    yield
                    refb = refc.h[:, :].unsqueeze(2).broadcast_to([128, nch, 32])
                    refgb = refg_ap.unsqueeze(2).broadcast_to([128, ngr, 128])
                    edgegb = edgeg_ap.unsqueeze(2).broadcast_to([128, ngr, 128])
                    if not so:
                        k.tt('dve', eT, v32(eT), Zh, Zv, refc, refb, ALU.subtract)
                        yield
                        k.act(eT, eT[:, :], eT, eT[:, :], AF.Exp)
                        yield
                        k.tt('pool', qb, qb[:, :], qT[h], qT[h][:, :], eT, eT[:, :], ALU.mult)
                        yield
                        k.tt('dve', eT, v128(eT), Zh, Zg, refc, refgb, ALU.subtract)
                        yield
                        k.act(eT, eT[:, :], eT, eT[:, :], AF.Exp)
                        yield
                        k.tt('pool', qT[h], qT[h][:, :], qT[h], qT[h][:, :], eT, eT[:, :], ALU.mult)
                        yield
                    k.tt('dve', eT, v128(eT), Zh, edgegb, Zh, Zg, ALU.subtract)
                    yield
                    k.act(eT, eT[:, :], eT, eT[:, :], AF.Exp)
                    yield
                    k.tt('pool', khf, khf[:, :], kT, kT[:, :], eT, eT[:, :], ALU.mult)
                    yield
                    k.tt('dve', dec[h], dec[h][:, :], Zh, edgeg_ap, refc, refg_ap, ALU.subtract)
                    yield
                    k.act(dec[h], dec[h][:, :], dec[h], dec[h][:, :], AF.Exp)
                    yield
                    ptb = PS[PB + 1]
                    for g in range(ngr):
                        k.tr(ptb, ptb[:, g * 128:(g + 1) * 128], khf, khf[:, g * 128:(g + 1) * 128], ident, ident[:, :])
                        yield
                    k.copy('act', Kh[h], Kh[h].h[:, :, :], ptb, ptb.h[:, 0:ngr * 128].rearrange("p (g d) -> p g d", g=ngr))
                    yield
                    wt2 = wB2[par]
                    wload(wt2, 0, win, 1536 + h * 128, 128)
                    for g in range(ngr):
                        pv = PS[PB + 2]
                        proj_tm(seq, i0 + g * 128, wt2, 0, 128, pv, pv[:, 0:128])
                        yield
                        k.copy('act', Vt[h], Vt[h][:, g, :], pv, pv[:, 0:128])
                        yield
                    for g in range(0 if so else ngr):
                        Km = Kmix[par]
                        pa = PS[PB + 3]
                        for a in range(4):
                            c0, c1 = (0, 32 * (a + 1)) if fwd else (32 * a, 128)
                            te = tmpE[par]
                            cs = slice(g * 128 + c0, g * 128 + c1)
                            ci = g * 4 + a
                            k.act(te, te[:, c0:c1], Zh, Zh[:, cs], AF.Exp, bias=refc[:, ci:ci + 1], scale=-1.0, extra=[refc])
                            yield
                            k.tt('pool', Km, Km[:, a, c0:c1], kT, kT[:, cs], te, te[:, c0:c1], ALU.mult)
                            yield
                            k.mm(pa, pa[:, a * 32:(a + 1) * 32], Km, Km[:, a, :], qb, qb[:, g * 128 + a * 32:g * 128 + (a + 1) * 32])
                            yield
                        k.tt('dve', att[h], att[h][:, g, :], pa, pa[:, 0:128], msk, msk[:, 0:128], ALU.mult)
                        yield

                for h0 in (0, 2):
                    gens = [head_prep(h0, 0), head_prep(h0 + 1, 1)]
                    alive = [True, True]
                    while any(alive):
                        for gi in range(2):
                            if alive[gi]:
                                try:
                                    next(gens[gi])
                                except StopIteration:
                                    alive[gi] = False
                groups = list(range(ngr)) if fwd else list(range(ngr - 1, -1, -1))
                for g in groups:
                    gs = slice(g * 128, (g + 1) * 128)
                    for h in range(4):
                        pO = PS[h % 2]
                        if not so:
                            k.mm(pO, pO[:, 0:128], Vt[h], Vt[h][:, g, :], att[h], att[h][:, g, :], start=True, stop=False)
                            k.mm(pO, pO[:, 0:128], S[h], S[h][:, :], qT[h], qT[h][:, gs], start=False, stop=True)
                        pD = PS[2 + h % 2]
                        k.mm(pD, pD[:, 0:128], Kh[h], Kh[h][:, g, :], Vt[h], Vt[h][:, g, :])
                        k.stt(S[h], S[h][:, :], S[h], S[h][:, :], dec[h][:, g:g + 1], pD, pD[:, 0:128], ALU.mult, ALU.add,
                              extra=[dec[h]])
                        if not so:
                            k.copy('act', oseg[h], oseg[h][:, gs], pO, pO[:, 0:128])
                for h in range(0 if so else 4):
                    if fwd:
                        k.dma('pool', of_s[seq][h, :, i0:i0 + SEG], oseg[h][:, :], of_s[seq], oseg[h])
                    else:
                        k.dma('sp', ofl[:, :], of_s[seq][h, :, i0:i0 + SEG], ofl, of_s[seq])
                        k.tt('pool', ofl, ofl[:, :], ofl, ofl[:, :], oseg[h], oseg[h][:, :], ALU.add)
                        k.act(sqb, sqb[:, :], ofl, ofl[:, :], AF.Square)
                        pr = PS[6]
                        k.mm(pr, pr[:, :SEG], ones_bf, ones_bf[:, :], sqb, sqb[:, :])
                        rstd_from(rsb, rsb[:, :], pr, pr[:, :SEG], 8.0, 1e-6)
                        wt = wB2[h % 2]
                        wload(wt, 0, win, 2048 + h * 128, 128)
                        pg = PS[7]
                        proj_fm(seq, i0, SEG, wt, 0, 128, pg, pg[:, :SEG])
                        k.act(sgb, sgb[:, :], pg, pg[:, :SEG], AF.Silu)
                        k.stt(ofl, ofl[:, :], ofl, ofl[:, :], cng[:, h:h + 1], rsb, rsb[:, :], ALU.mult, ALU.mult, extra=[cng])
                        k.tt('pool', yct, yct[:, :], ofl, ofl[:, :], sgb, sgb[:, :], ALU.mult)
                        k.dma('pool', yb_s[seq][:, h, i0:i0 + SEG], yct[:, :], yb_s[seq], yct)
            if seq > 0:
                for h in range(4):
                    k.dma('pool', cS_o[seq - 1, d, h, :, :], S[h][:, :], cS_o, S[h])
        k.release(mk)

    def mixer_out(l, wout, ya_first):
        mk = k.mark()
        woa = k.sb("woa", [64, 8, D], BF16)
        wob = k.sb("wob", [128, 4, D], BF16)
        ra, rb = (0, 512) if ya_first else (512, 0)
        k.dma('sp', woa[:, :, :], wout.h[ra:ra + 512, :].rearrange("(h p) n -> p h n", p=64), woa, wout)
        k.dma('sp', wob[:, :, :], wout.h[rb:rb + 512, :].rearrange("(h p) n -> p h n", p=128), wob, wout)
        yat = [k.sb(f"oyat{i}", [64, 8, 256], BF16) for i in range(2)]
        ybt = [k.sb(f"oybt{i}", [128, 4, 256], BF16) for i in range(2)]
        alloc_ln()
        ii = 0
        for t in ((0, 1, 4) if l == 1 else range(NTT)):
            col = 0 if t < 4 else 1
            for hf in range(2):
                ya_, yb_ = yat[ii % 2], ybt[ii % 2]
                ii += 1
                if t < 4:
                    seq, c0 = 0, t * TT + hf * 256
                else:
                    seq, c0 = 1 + hf, 0
                k.dma('sp', ya_[:, :, :], ya_s[seq][:, :, c0:c0 + 256], ya_, ya_s[seq])
                k.dma('sp', yb_[:, :, :], yb_s[seq][:, :, c0:c0 + 256], yb_, yb_s[seq])
                for dc in range(NC_):
                    py = PS[dc % 4]
                    for hd in range(8):
                        k.mm(py, py[:, 0:256], woa, woa[:, hd, dc * 128:(dc + 1) * 128], ya_, ya_[:, hd, :],
                             start=(hd == 0), stop=False)
                    for hd in range(4):
                        k.mm(py, py[:, 0:256], wob, wob[:, hd, dc * 128:(dc + 1) * 128], yb_, yb_[:, hd, :],
                             start=False, stop=(hd == 3))
                    xs = slice(hf * 256, (hf + 1) * 256)
                    k.stt(X[t], X[t][:, dc, xs], py, py[:, 0:256], cf[l][:, (3 * 1 + 2) * 8 + dc, col:col + 1],
                          X[t], X[t][:, dc, xs], ALU.mult, ALU.add, extra=[cf[l]])
            layernorm(l, 1, t)
            modulate(l, 2, t)
        k.release(mk)

    stop_after = None if debug_stage is None else debug_stage.get('stop')
    cfgA = dict(win=wine_s, qcol=0, kcol=512, rope=True, band=True, qknorm=False, sinkexp=sinkexp,
                ctx=(kctxa_d, vctxa_d), ko=ak_o, vo=av_o, ya_s=ya_s, bg=lambda: (bg_add_ffn(1, 704), bg_add_ffn(2, 704)))
    cfgD = dict(win=wino_s, qcol=2560, kcol=3072, rope=True, band=False, qknorm=True, sinkexp=None,
                ctx=(kctxd_d, vctxd_d), ko=dk_o, vo=dv_o, ya_s=ya_s, qg=gvec[:, 0:1], kg=gvec[:, 1:2], qhalf=True)

    def ffn_phase(l, j, nxt=None):
        mk = k.mark()
        alloc_ffn()
        if (l, j) == (0, 0):
            bg_add(wine_d, wine_s, wine_d.h, wine_s.h, D, 2832)
            bg_add(woute_d, woute_s, woute_d.h, woute_s.h, D, D)
        elif (l, j) == (0, 2):
            bg_add(wino_d, wino_s, wino_d.h, wino_s.h, D, 3328)
        elif (l, j) == (1, 0):
            bg_add(wouto_d, wouto_s, wouto_d.h, wouto_s.h, D, D)
            bg_add_ffn(3)
        bg_start()
        for t in ((0, 1, 4) if (l, j) == (1, 2) else range(NTT)):
            ffn(l, j, t)
            if nxt is not None:
                modulate(nxt[0], nxt[1], t)
        bg_flush()
        k.release(mk)

    def mark_phase(label):
        PHASE_MARKS.append((label, dict(k.cnt)))

    def run_all_marked():
        mark_phase('setup_end')
        for t in range(NTT):
            modulate(0, 0, t)
        ffn_phase(0, 0, nxt=(0, 1)); mark_phase('ffn00')
        for seq in range(3):
            attention(seq, SL[seq], cfgA); mark_phase(f'attnA{seq}')
            mlstm(seq, SL[seq]); mark_phase(f'mlstm{seq}')
        mixer_out(0, woute_s, True); mark_phase('mout0')
        ffn_phase(0, 2, nxt=(1, 0)); mark_phase('ffn02')
        ffn_phase(1, 0, nxt=(1, 1)); mark_phase('ffn10')
        for seq in range(3):
            hgrn(seq, SL[seq]); mark_phase(f'hgrn{seq}')
            attention(seq, SL[seq], cfgD); mark_phase(f'attnD{seq}')
        mixer_out(1, wouto_s, False); mark_phase('mout1')
        ffn_phase(1, 2, nxt=None); mark_phase('ffn12')

    def run_all():
        if debug_stage is None:
            return run_all_marked()
        for t in range(NTT):
            modulate(0, 0, t)
        ffn_phase(0, 0, nxt=(0, 1))
        if stop_after == 'f00':
            return
        only = None if debug_stage is None else debug_stage.get('only')
        if only is not None:
            for nm in only:
                if nm[0] == 'a':
                    attention(int(nm[1]), SL[int(nm[1])], cfgA)
                if nm[0] == 'm':
                    mlstm(int(nm[1]), SL[int(nm[1])])
                if nm[0] == 'o':
                    mixer_out(0, woute_s, True)
            return
        for seq in range(3):
            attention(seq, SL[seq], cfgA)
            mlstm(seq, SL[seq])
        mixer_out(0, woute_s, True)
        if stop_after == 'm0':
            return
        ffn_phase(0, 2, nxt=(1, 0))
        ffn_phase(1, 0, nxt=(1, 1))
        if stop_after == 'f10':
            return
        only1 = None if debug_stage is None else debug_stage.get('only1')
        if only1 is not None:
            for nm in only1:
                if nm[0] == 'a':
                    attention(int(nm[1]), SL[int(nm[1])], cfgD)
                if nm[0] == 'g':
                    hgrn(int(nm[1]), SL[int(nm[1])])
                if nm[0] == 'o':
                    mixer_out(1, wouto_s, False)
            return
        for seq in range(3):
            hgrn(seq, SL[seq])
            attention(seq, SL[seq], cfgD)
        mixer_out(1, wouto_s, False)
        if stop_after == 'm1':
            return
        ffn_phase(1, 2, nxt=None)

    run_all()

    xo = [k.sb(f"xo{i}", [128, D], F32) for i in range(2)]

    def store_x(dst_d, r0, Xt, t0, i):
        xi = xo[i % 2]
        for g in range(2):
            pt = PS[2 + (2 * i + g) % 4]
            for q in range(4):
                c = g * 4 + q
                k.tr(pt, pt[:, q * 128:(q + 1) * 128], Xt, Xt[:, c, t0:t0 + 128], ident, ident[:, :])
            k.copy(ev_rr(), xi, xi[:, g * 512:(g + 1) * 512], pt, pt[:, :])
        k.dma('pool', dst_d[r0:r0 + 128, :], xi[:, :], dst_d, xi)

    for t in (0, 1, 4):
        for b in range(4):
            if t < 4:
                store_x(ys_d, t * TT + b * 128, X[t], b * 128, t * 4 + b)
            else:
                store_x(yp_d, b * 128, X[t], b * 128, t * 4 + b)
    mark_phase('store')
    k.barrier()
    return nc


def _consts(mir=False):
    import ml_dtypes
    c = {}
    c["ident"] = np.eye(128, dtype=np.float32)
    nf = 16
    inv = (10000.0 ** (-np.arange(nf, dtype=np.float32) / nf)).astype(np.float32)
    t = np.arange(LS)
    if mir:
        t = (LS - 1) - t
    row = (t // 64).astype(np.float32)
    colp = (t % 64).astype(np.float32)
    cosT = np.zeros((128, LS), np.float32)
    sinT = np.zeros((128, LS), np.float32)
    for p in range(128):
        d = p % 64
        pos = row if d < 32 else colp
        ang = (pos * inv[d % 16]).astype(np.float32)
        cosT[p] = np.cos(ang)
        sinT[p] = np.sin(ang)
    c["cosT"], c["sinT"] = cosT, sinT
    R = np.zeros((128, 128), np.float32)
    for m in range(128):
        if (m % 32) < 16:
            R[m, m + 16] = -1.0
        else:
            R[m, m - 16] = 1.0
    c["RT"] = np.ascontiguousarray(R.T)
    sh = np.zeros((128, 64), np.float32)
    for i in range(64):
        sh[64 + i, i] = 1.0
    c["shiftT"] = sh
    sel = np.zeros((4, 4, 128), np.float32)
    for h in range(4):
        sel[h, h, :] = 1.0
    c["sel4"] = sel.reshape(4, 512)
    s = np.arange(128)[:, None]
    tt_ = np.arange(128)[None, :]
    hi = (s <= tt_).astype(np.float32)
    lo = (s >= tt_).astype(np.float32)
    same = ((s // 32) == (tt_ // 32)).astype(np.float32)
    b64 = ((s // 64) == (tt_ // 64)).astype(np.float32) / 64.0
    mskb = np.concatenate([np.tile(hi, (1, 4)), np.tile(lo, (1, 4)), hi * same, lo * same, b64], 1)
    c["mskb"] = mskb.astype(ml_dtypes.bfloat16)
    return c


def make_in_maps(inp):
    f = lambda a: np.ascontiguousarray(np.asarray(a, dtype=np.float32))
    maps = []
    lnp = np.concatenate([f(inp['ln_g']).reshape(48, 128), f(inp['ln_b']).reshape(48, 128)], 0)
    lbl = f(inp['c_lb_logits']).reshape(2, 4, 128)
    smallc = np.zeros((128, 32), np.float32)
    smallc[:, 0:4] = f(inp['b_norm_g'])[0].T
    smallc[:, 4:8] = f(inp['c_norm_g'])[0].T
    smallc[:, 8] = np.tile(f(inp['d_q_norm'])[0], 2)
    smallc[:, 9] = np.tile(f(inp['d_k_norm'])[0], 2)
    smallc[:, 10:14] = lbl[0].T
    smallc[:, 14:18] = lbl[1].T
    smallc[:, 18:26] = np.broadcast_to(f(inp['a_sink'])[0].reshape(1, 8), (128, 8))
    kgrow = np.broadcast_to(np.tile(f(inp['d_k_norm'])[0], 2)[None, :], (128, 128)).copy()
    base = {
        "ada_w": f(inp['ada_w']),
        "ada_b": f(inp['ada_b']).reshape(144, 128),
        "lnp": lnp,
        "ffn_w1": f(inp['ffn_w1']).reshape(4, D, DFF),
        "ffn_w3": f(inp['ffn_w3']).reshape(4, D, DFF),
        "ffn_w2": f(inp['ffn_w2']).reshape(4, DFF, D),
        "wout_e": f(inp['w_out_even'])[0],
        "wout_o": f(inp['w_out_odd'])[0],
        "smallc": smallc, "kgrow": kgrow,
    }
    win_e = f(inp['w_in_even'])[0]
    win_o = f(inp['w_in_odd'])[0]
    gb = f(inp['b_gate_bias'])[0]
    variants = []
    for mir in (False, True):
        v = dict(base)
        v.update(_consts(mir))
        if not mir:
            v["win_e"], v["win_o"] = win_e, win_o
            v["gbias"] = np.ascontiguousarray(gb.T)
        else:
            we = win_e.copy()
            we[:, 2304:2312], we[:, 2312:2320] = win_e[:, 2312:2320], win_e[:, 2304:2312]
            wo = win_o.copy()
            wo[:, 512:1024], wo[:, 1024:1536] = win_o[:, 1024:1536], win_o[:, 512:1024]
            v["win_e"], v["win_o"] = we, wo
            v["gbias"] = np.ascontiguousarray(gb[[2, 3, 0, 1]].T)
        variants.append(v)

    def dupk(kc):
        return np.ascontiguousarray(np.concatenate([kc[:, 0], kc[:, 0], kc[:, 1], kc[:, 1]], 1))

    for i in range(8):
        b = i // 2
        mir = (i % 2 == 1)
        m = dict(variants[1 if mir else 0])
        xs = f(inp['x_sample'][b])
        xp = f(inp['x_prompt'][2 * i:2 * i + 2])
        stC, stn, stm, stS = (f(inp['state_b_C'][b, 0]), f(inp['state_b_n'][b, 0]), f(inp['state_b_m'][b, 0]),
                              f(inp['state_c_S'][b, 0]))
        if mir:
            xs = np.ascontiguousarray(xs[::-1])
            xp = np.ascontiguousarray(xp[:, ::-1])
            stC, stn, stm, stS = (np.ascontiguousarray(a_[::-1]) for a_ in (stC, stn, stm, stS))
        m.update({
            "xs": xs,
            "xp": xp.reshape(NPS * LP, D),
            "cvec": np.concatenate([f(inp['c'][b]).reshape(8, 128), f(inp['c_ctx']).reshape(8, 128)], 0),
            "kctxa": dupk(f(inp['cache_a_k'][b, 0])), "vctxa": f(inp['cache_a_v'][b, 0]).reshape(512, 128),
            "kctxd": dupk(f(inp['cache_d_k'][b, 0])), "vctxd": f(inp['cache_d_v'][b, 0]).reshape(512, 128),
            "stC": stC, "stn": stn, "stm": stm, "stS": stS,
        })
        maps.append(m)
    return maps


def gather(r):
    H2 = LS // 2
    ys = np.stack([np.concatenate([r[2 * b]["ys"], r[2 * b + 1]["ys"][::-1]], 0) for b in range(4)], 0)

    def per_core(nm, tok_axis=None, dir_axis=None):
        outs = []
        for i in range(8):
            a = r[i][nm]
            if i % 2 == 1:
                if tok_axis is not None:
                    a = np.flip(a, axis=tok_axis)
                if dir_axis is not None:
                    a = np.flip(a, axis=dir_axis)
            outs.append(a)
        return np.concatenate(outs, 0)

    yp = np.concatenate([(r[i]["yp"].reshape(NPS, LP, D)[:, ::-1] if i % 2 else r[i]["yp"].reshape(NPS, LP, D))
                         for i in range(8)], 0)
    ak = per_core("ak_o", tok_axis=1).reshape(16, 1, LP, 2, 64)
    av = per_core("av_o", tok_axis=1).reshape(16, 1, LP, 2, 64)
    bC = per_core("bC_o", dir_axis=1).reshape(16, 1, 2, 4, 128, 128)
    bn = per_core("bn_o", dir_axis=1).reshape(16, 1, 2, 4, 128)
    bm = per_core("bm_o", dir_axis=1).reshape(16, 1, 2, 4)
    cS = per_core("cS_o", dir_axis=1).reshape(16, 1, 2, 4, 128, 128)
    dk = per_core("dk_o", tok_axis=1).reshape(16, 1, LP, 2, 64)
    dv = per_core("dv_o", tok_axis=1).reshape(16, 1, LP, 2, 64)
    return (np.ascontiguousarray(yp), ys, np.ascontiguousarray(ak), np.ascontiguousarray(av), np.ascontiguousarray(bC),
            np.ascontiguousarray(bn), np.ascontiguousarray(bm), np.ascontiguousarray(cS), np.ascontiguousarray(dk),
            np.ascontiguousarray(dv))


def kernel(**inputs):
    nc = build()
    in_maps = make_in_maps(inputs)
    res = run_bass_kernel_spmd(nc, in_maps, core_ids=list(range(8)))
    return gather(res.results)
```

```python
import numpy as np
import concourse.bass as bass
import concourse.mybir as mybir
from concourse.bass_utils import run_bass_kernel_spmd

F32 = mybir.dt.float32
BF16 = mybir.dt.bfloat16
AF = mybir.ActivationFunctionType
ALU = mybir.AluOpType

D = 1024
NC_ = 8
DFF = 2816
NFC = 22
LS = 2048
LP = 256
NPS = 2
TT = 512
NTT = 5
ALPHA = 4 ** 0.25
LN_EPS = 1e-5 / (ALPHA * ALPHA)
PH = 30000
DBG = {}
PHASE_MARKS = []


class Buf:
    UID = 0

    def __init__(self, h, name, space):
        self.h = h
        self.name = name
        self.space = space
        self.last_w = None
        self.readers = []
        self.dsem = None
        self.dcnt = 0
        Buf.UID += 1
        self.uid = Buf.UID

    def __getitem__(self, idx):
        return self.h[idx]


class KB:
    def __init__(self, nc):
        self.nc = nc
        self.eng = {'pe': nc.tensor, 'act': nc.scalar, 'dve': nc.vector, 'pool': nc.gpsimd, 'sp': nc.sync}
        self.cnt = {e: 0 for e in self.eng}
        self.sems = {e: [] for e in self.eng}
        self.waited = {e: {} for e in self.eng}
        self.nbuf = 0
        self.dma_bufs = []
        self.guards = []
        self.gbufs = []
        self.free_dsems = []
        self.nsem = 0

    def sb(self, name, shape, dt):
        self.nbuf += 1
        g = self.nc.sbuf_tensor('sb_' + name + f'_{self.nbuf}', list(shape), dt)
        h = g.__enter__()
        self.guards.append(g)
        b = Buf(h, name, 'sb')
        self.gbufs.append(b)
        return b

    def mark(self):
        return len(self.guards)

    def release(self, mark):
        self.barrier()
        while len(self.guards) > mark:
            self.guards.pop().__exit__(None, None, None)
            b = self.gbufs.pop()
            if b.dsem is not None:
                self.free_dsems.append((b.dsem, b.dcnt))
                self.dma_bufs.remove(b)
                b.dsem = None

    def ps(self, name, shape, dt=F32):
        return Buf(self.nc.alloc_psum_tensor(name, list(shape), dt), name, 'ps')

    def dram(self, name, shape, dt, kind):
        return Buf(self.nc.dram_tensor(name, list(shape), dt, kind=kind).ap(), name, 'dram')

    def _sem(self, e, phase):
        while len(self.sems[e]) <= phase:
            self.sems[e].append(self.nc.alloc_semaphore(name=f"s_{e}_{len(self.sems[e])}"))
        return self.sems[e][phase]

    def _wait(self, e, ev):
        if ev is None:
            return
        if ev[0] == 'eng':
            _, e2, seq = ev
            if e2 == e and e == 'pe':
                return
            key = ('eng', e2)
            if self.waited[e].get(key, 0) >= seq:
                return
            self.waited[e][key] = seq
            ph, val = (seq - 1) // PH, (seq - 1) % PH + 1
            self.eng[e].wait_ge(self._sem(e2, ph), val)
        else:
            owner = ev[1]
            key = ('dma', owner.uid)
            need = owner.dcnt
            if self.waited[e].get(key, 0) >= need:
                return
            self.waited[e][key] = need
            self.eng[e].wait_ge(owner.dsem, need * 16)

    def _sync(self, e, reads, writes):
        for r in reads:
            self._wait(e, r.last_w)
            if r.space == 'ps':
                for ev in r.readers:
                    if not (ev[0] == 'eng' and ev[1] == e):
                        self._wait(e, ev)
        for w in writes:
            self._wait(e, w.last_w)
            for ev in w.readers:
                if ev[0] == 'eng' and ev[1] == e:
                    continue
                self._wait(e, ev)

    def _record(self, ev, reads, writes):
        for r in reads:
            if r in writes:
                continue
            if ev[0] == 'eng':
                r.readers = [x for x in r.readers if not (x[0] == 'eng' and x[1] == ev[1])]
            else:
                r.readers = [x for x in r.readers if not (x[0] == 'dma' and x[1] is ev[1])]
            r.readers.append(ev)
        for w in writes:
            w.last_w = ev
            w.readers = []

    def op(self, e, fn, reads, writes):
        self._sync(e, reads, writes)
        inst = fn()
        self.cnt[e] += 1
        seq = self.cnt[e]
        inst.then_inc(self._sem(e, (seq - 1) // PH), 1)
        self._record(('eng', e, seq), reads, writes)
        return inst

    def dma(self, q, out_ap, in_ap, dst, src, owner=None):
        if owner is None:
            owner = dst if dst.space == 'sb' else src
        if owner.dsem is None:
            if self.free_dsems:
                owner.dsem, owner.dcnt = self.free_dsems.pop()
            else:
                self.nsem += 1
                owner.dsem = self.nc.alloc_semaphore(name=f"d_{self.nsem}")
            self.dma_bufs.append(owner)
        self._sync(q, [src], [dst])
        inst = self.eng[q].dma_start(out=out_ap, in_=in_ap)
        owner.dcnt += 1
        inst.then_inc(owner.dsem, 16)
        self._record(('dma', owner), [src], [dst])

    def barrier(self):
        for e in self.eng:
            for e2 in self.eng:
                if e2 != e and self.cnt[e2] > 0:
                    self._wait(e, ('eng', e2, self.cnt[e2]))
            for b in self.dma_bufs:
                self._wait(e, ('dma', b))

    def mm(self, out, out_ap, lhsT, lhsT_ap, rhs, rhs_ap, start=True, stop=True):
        rd = [lhsT, rhs]
        return self.op('pe', lambda: self.nc.tensor.matmul(out_ap, lhsT=lhsT_ap, rhs=rhs_ap, start=start,
                                                          stop=stop), rd, [out])

    def tr(self, out, out_ap, in_, in_ap, ident, ident_ap):
        return self.op('pe', lambda: self.nc.tensor.transpose(out_ap, in_ap, ident_ap), [in_, ident], [out])

    def act(self, out, out_ap, in_, in_ap, func, bias=None, scale=None, extra=()):
        kw = {}
        if bias is not None:
            kw['bias'] = bias
        if scale is not None:
            kw['scale'] = scale
        return self.op('act', lambda: self.nc.scalar.activation(out=out_ap, in_=in_ap, func=func, **kw),
                       [in_] + list(extra), [out])

    def tt(self, e, out, out_ap, a, a_ap, b, b_ap, op):
        eng = self.eng[e]
        return self.op(e, lambda: eng.tensor_tensor(out=out_ap, in0=a_ap, in1=b_ap, op=op), [a, b], [out])

    def ts(self, e, out, out_ap, a, a_ap, s1, s2, op0, op1=None, extra=()):
        eng = self.eng[e]
        if op1 is None:
            f = lambda: eng.tensor_scalar(out=out_ap, in0=a_ap, scalar1=s1, scalar2=None, op0=op0)
        else:
            f = lambda: eng.tensor_scalar(out=out_ap, in0=a_ap, scalar1=s1, scalar2=s2, op0=op0, op1=op1)
        return self.op(e, f, [a] + list(extra), [out])

    def stt(self, out, out_ap, a, a_ap, scalar, b, b_ap, op0, op1, extra=()):
        return self.op('dve', lambda: self.nc.vector.scalar_tensor_tensor(out=out_ap, in0=a_ap, scalar=scalar,
                                                                         in1=b_ap, op0=op0, op1=op1),
                       [a, b] + list(extra), [out])

    def copy(self, e, out, out_ap, in_, in_ap):
        if e == 'act':
            return self.act(out, out_ap, in_, in_ap, AF.Copy)
        eng = self.eng[e]
        return self.op(e, lambda: eng.tensor_copy(out=out_ap, in_=in_ap), [in_], [out])

    def memset(self, e, out, out_ap, val):
        eng = self.eng[e]
        return self.op(e, lambda: eng.memset(out_ap, val), [], [out])


class RR:
    def __init__(self, engs):
        self.engs = engs
        self.i = 0

    def __call__(self):
        e = self.engs[self.i % len(self.engs)]
        self.i += 1
        return e


def build(debug_stage=None):
    nc = bass.Bass("TRN2", target_bir_lowering=False)
    k = KB(nc)
    xs_d = k.dram("xs", [LS, D], F32, "ExternalInput")
    xp_d = k.dram("xp", [NPS * LP, D], F32, "ExternalInput")
    cv_d = k.dram("cvec", [2 * NC_, 128], F32, "ExternalInput")
    adaw_d = k.dram("ada_w", [2, D, 9 * D], F32, "ExternalInput")
    adab_d = k.dram("ada_b", [2 * 72, 128], F32, "ExternalInput")
    lnp_d = k.dram("lnp", [96, 128], F32, "ExternalInput")
    w1_d = k.dram("ffn_w1", [4, D, DFF], F32, "ExternalInput")
    w3_d = k.dram("ffn_w3", [4, D, DFF], F32, "ExternalInput")
    w2_d = k.dram("ffn_w2", [4, DFF, D], F32, "ExternalInput")
    ident_d = k.dram("ident", [128, 128], F32, "ExternalInput")
    wine_d = k.dram("win_e", [D, 2832], F32, "ExternalInput")
    woute_d = k.dram("wout_e", [D, D], F32, "ExternalInput")
    wino_d = k.dram("win_o", [D, 3328], F32, "ExternalInput")
    wouto_d = k.dram("wout_o", [D, D], F32, "ExternalInput")
    kctxa_d = k.dram("kctxa", [512, 256], F32, "ExternalInput")
    vctxa_d = k.dram("vctxa", [512, 128], F32, "ExternalInput")
    kctxd_d = k.dram("kctxd", [512, 256], F32, "ExternalInput")
    vctxd_d = k.dram("vctxd", [512, 128], F32, "ExternalInput")
    stC_d = k.dram("stC", [2, 4, 128, 128], F32, "ExternalInput")
    stn_d = k.dram("stn", [2, 4, 128], F32, "ExternalInput")
    stm_d = k.dram("stm", [2, 4], F32, "ExternalInput")
    stS_d = k.dram("stS", [2, 4, 128, 128], F32, "ExternalInput")
    gbias_d = k.dram("gbias", [4, 4], F32, "ExternalInput")
    smallc_d = k.dram("smallc", [128, 32], F32, "ExternalInput")
    kgrow_d = k.dram("kgrow", [128, 128], F32, "ExternalInput")
    cos_d = k.dram("cosT", [128, LS], F32, "ExternalInput")
    sin_d = k.dram("sinT", [128, LS], F32, "ExternalInput")
    RT_d = k.dram("RT", [128, 128], F32, "ExternalInput")
    shiftT_d = k.dram("shiftT", [128, 64], F32, "ExternalInput")
    sel4_d = k.dram("sel4", [4, 512], F32, "ExternalInput")
    mskb_d = k.dram("mskb", [128, 1408], BF16, "ExternalInput")
    ys_d = k.dram("ys", [LS // 2, D], F32, "ExternalOutput")
    ak_o = k.dram("ak_o", [NPS, LP, 128], F32, "ExternalOutput")
    av_o = k.dram("av_o", [NPS, LP, 128], F32, "ExternalOutput")
    dk_o = k.dram("dk_o", [NPS, LP, 128], F32, "ExternalOutput")
    dv_o = k.dram("dv_o", [NPS, LP, 128], F32, "ExternalOutput")
    bC_o = k.dram("bC_o", [NPS, 2, 4, 128, 128], F32, "ExternalOutput")
    bn_o = k.dram("bn_o", [NPS, 2, 4, 128], F32, "ExternalOutput")
    bm_o = k.dram("bm_o", [NPS, 2, 4], F32, "ExternalOutput")
    cS_o = k.dram("cS_o", [NPS, 2, 4, 128, 128], F32, "ExternalOutput")
    wine_s = k.dram("wine_s", [D, 2832], BF16, "Internal")
    woute_s = k.dram("woute_s", [D, D], BF16, "Internal")
    wino_s = k.dram("wino_s", [D, 3328], BF16, "Internal")
    wouto_s = k.dram("wouto_s", [D, D], BF16, "Internal")
    SL = [LS, LP, LP]
    ya_s = [k.dram(f"ya_s{i}", [64, 8, SL[i]], BF16, "Internal") for i in range(3)]
    yb_s = [k.dram(f"yb_s{i}", [128, 4, SL[i]], BF16, "Internal") for i in range(3)]
    hf_s = [k.dram(f"hf_s{i}", [4, 128, SL[i]], F32, "Internal") for i in range(3)]
    of_s = hf_s
    yp_d = k.dram("yp", [NPS * LP, D], F32, "ExternalOutput")
    w1_s = k.dram("w1_s", [4, D, DFF], BF16, "Internal")
    w3_s = k.dram("w3_s", [4, D, DFF], BF16, "Internal")
    w2_s = k.dram("w2_s", [4, DFF, D], BF16, "Internal")

    X = [k.sb(f"X{t}", [128, NC_, TT], F32) for t in range(NTT)]
    H = [k.sb(f"H{t}", [128, NC_, TT], BF16) for t in range(NTT)]
    ident = k.sb("ident", [128, 128], F32)
    ones_bf = k.sb("ones_bf", [128, 128], BF16)
    modv = [k.sb(f"modv{l}", [128, 72, 2], F32) for l in range(2)]
    sc1 = [k.sb(f"sc1{l}", [128, 72, 2], F32) for l in range(2)]
    cf = [k.sb(f"cf{l}", [128, 72, 2], F32) for l in range(2)]
    lnp = k.sb("lnp", [128, 96], F32)
    PS = [k.ps(f"ps{i}", [128, 512]) for i in range(8)]

    k.dma('sp', ident[:, :], ident_d[:, :], ident, ident_d)
    RT = k.sb("RT", [128, 128], F32)
    k.dma('sp', RT[:, :], RT_d[:, :], RT, RT_d)
    shiftT = k.sb("shiftT", [128, 64], F32)
    k.dma('sp', shiftT[:, :], shiftT_d[:, :], shiftT, shiftT_d)
    sel4 = k.sb("sel4", [4, 4, 128], F32)
    k.dma('sp', sel4[:, :, :], sel4_d.h[:, :].rearrange("p (h t) -> p h t", h=4), sel4, sel4_d)

    class View:
        def __init__(self, buf, c0, c1):
            self.buf, self.c0, self.c1 = buf, c0, c1
    tri_hi4 = k.sb("tri_hi4", [128, 512], BF16)
    tri_lo4 = k.sb("tri_lo4", [128, 512], BF16)
    bd_hi = k.sb("bd_hi", [128, 128], BF16)
    bd_lo = k.sb("bd_lo", [128, 128], BF16)
    bones64 = k.sb("bones64", [128, 128], BF16)
    smallc = k.sb("smallc", [128, 32], F32)
    bng = k.sb("bng", [128, 4], F32)
    cng = k.sb("cng", [128, 4], F32)
    gvec = k.sb("gvec", [128, 2], F32)
    lbc = k.sb("lbc", [128, 4], F32)
    oml = k.sb("oml", [128, 4], F32)
    sinkexp = k.sb("sinkexp", [64, 2, 512], F32)
    sk8 = k.sb("sk8", [64, 8], F32)
    kgrow = k.sb("kgrow", [128, 128], F32)
    mk_c = k.mark()
    mskb = k.sb("mskb", [128, 1408], BF16)
    k.dma('sp', mskb[:, :], mskb_d[:, :], mskb, mskb_d)
    k.copy('pool', tri_hi4, tri_hi4[:, :], mskb, mskb[:, 0:512])
    k.copy('pool', tri_lo4, tri_lo4[:, :], mskb, mskb[:, 512:1024])
    k.copy('pool', bd_hi, bd_hi[:, :], mskb, mskb[:, 1024:1152])
    k.copy('pool', bd_lo, bd_lo[:, :], mskb, mskb[:, 1152:1280])
    k.copy('pool', bones64, bones64[:, :], mskb, mskb[:, 1280:1408])
    k.release(mk_c)
    k.dma('sp', smallc[:, :], smallc_d[:, :], smallc, smallc_d)
    k.copy('dve', bng, bng[:, :], smallc, smallc[:, 0:4])
    k.copy('dve', cng, cng[:, :], smallc, smallc[:, 4:8])
    k.copy('dve', gvec, gvec[:, :], smallc, smallc[:, 8:10])
    k.tt('dve', lbc, lbc[:, :], smallc, smallc[:, 14:18], smallc, smallc[:, 10:14], ALU.subtract)
    k.act(lbc, lbc[:, :], lbc, lbc[:, :], AF.Sigmoid)
    k.ts('dve', oml, oml[:, :], lbc, lbc[:, :], -1.0, 1.0, ALU.mult, ALU.add)
    k.act(sk8, sk8[:, :], smallc, smallc[0:64, 18:26], AF.Exp)
    for kv_ in range(2):
        for hp_ in range(2):
            for ab_ in range(2):
                g_ = ab_ * 2 + hp_
                c0_ = hp_ * 256 + ab_ * 128
                k.copy('dve', sinkexp, sinkexp[:, kv_, c0_:c0_ + 128], sk8,
                       sk8[:, kv_ * 4 + g_:kv_ * 4 + g_ + 1].broadcast_to([64, 128]))
    k.dma('sp', kgrow[:, :], kgrow_d[:, :], kgrow, kgrow_d)
    k.memset('pool', ones_bf, ones_bf[:, :], 1.0 / D)

    stage = k.sb("stage_small", [128, 128], F32)
    silu_c = k.sb("silu_c", [128, NC_, 2], F32)

    def load_cols(src_d, r0, nrows, dst, dst_ap, func=None):
        k.dma('sp', stage[0:nrows, :], src_d[r0:r0 + nrows, :], stage, src_d)
        k.tr(PS[0], PS[0][:, 0:nrows], stage, stage[0:nrows, :], ident, ident[0:nrows, 0:nrows])
        if func is None:
            k.copy('dve', dst, dst_ap, PS[0], PS[0][:, 0:nrows])
        else:
            k.act(dst, dst_ap, PS[0], PS[0][:, 0:nrows], func)

    for j in range(2):
        load_cols(cv_d, j * NC_, NC_, silu_c, silu_c[:, :, j], AF.Silu)
    load_cols(lnp_d, 0, 96, lnp, lnp[:, :])

    mk0 = k.mark()
    adab = k.sb("adab", [128, 144], F32)
    load_cols(adab_d, 0, 72, adab, adab[:, 0:72])
    load_cols(adab_d, 72, 72, adab, adab[:, 72:144])
    NADA = 2
    adat = [k.sb(f"adat{i}", [128, NC_, 512], F32) for i in range(NADA)]
    NST = 2
    st_f = [k.sb(f"stf{i}", [128, 2816], F32) for i in range(NST)]
    st_b = [k.sb(f"stb{i}", [128, 2816], BF16) for i in range(NST)]
    cast_rr = RR(['act', 'dve', 'pool'])
    pi = [0]
    prep_units = []

    def prep_rows(src_d, dst_d, src_ap, dst_ap, ncols):
        def f():
            i = pi[0] % NST
            pi[0] += 1
            k.dma('sp', st_f[i][:, 0:ncols], src_ap, st_f[i], src_d)
            k.copy(cast_rr(), st_b[i], st_b[i][:, 0:ncols], st_f[i], st_f[i][:, 0:ncols])
            k.dma('pool', dst_ap, st_b[i][:, 0:ncols], dst_d, st_b[i])
        prep_units.append(f)

    def prep_rows2(s, r):
        def f():
            i = pi[0] % NST
            pi[0] += 1
            src = w2_d.h[s, r * 256:(r + 1) * 256, :].rearrange("(a p) n -> p a n", p=128)
            dst = w2_s.h[s, r * 256:(r + 1) * 256, :].rearrange("(a p) n -> p a n", p=128)
            sf = st_f[i].h[:, 0:2048].rearrange("p (a n) -> p a n", a=2)
            sbb = st_b[i].h[:, 0:2048].rearrange("p (a n) -> p a n", a=2)
            k.dma('sp', sf, src, st_f[i], w2_d)
            k.copy(cast_rr(), st_b[i], st_b[i][:, 0:2048], st_f[i], st_f[i][:, 0:2048])
            k.dma('pool', dst, sbb, w2_s, st_b[i])
        prep_units.append(f)

    def prep_ffn(s):
        for kc in range(NC_):
            prep_rows(w1_d, w1_s, w1_d.h[s, kc * 128:(kc + 1) * 128, :], w1_s.h[s, kc * 128:(kc + 1) * 128, :], DFF)
            prep_rows(w3_d, w3_s, w3_d.h[s, kc * 128:(kc + 1) * 128, :], w3_s.h[s, kc * 128:(kc + 1) * 128, :], DFF)
        for r in range(11):
            prep_rows2(s, r)

    def prep_mat(src_d, dst_d, ncols):
        for kc in range(NC_):
            for c0 in range(0, ncols, 2048):
                n = min(2048, ncols - c0)
                prep_rows(src_d, dst_d, src_d.h[kc * 128:(kc + 1) * 128, c0:c0 + n],
                          dst_d.h[kc * 128:(kc + 1) * 128, c0:c0 + n], n)

    prep_ffn(0)
    per_blk = (len(prep_units) + 35) // 36
    pu = [0]

    def emit_prep(n):
        for _ in range(n):
            if pu[0] < len(prep_units):
                prep_units[pu[0]]()
                pu[0] += 1

    it = 0
    for l in range(2):
        for nb4 in range(18):
            at = adat[it % NADA]
            it += 1
            k.dma('sp', at[:, :, :], adaw_d.h[l].rearrange("(kc p) n -> p kc n", p=128)[:, :, nb4 * 512:(nb4 + 1) * 512],
                  at, adaw_d)
            pm = PS[1 + (nb4 % 2)]
            for q in range(4):
                for kc in range(NC_):
                    k.mm(pm, pm[:, q * 2:q * 2 + 2], at, at[:, kc, q * 128:(q + 1) * 128], silu_c, silu_c[:, kc, :],
                         start=(kc == 0), stop=(kc == NC_ - 1))
            for q in range(4):
                nb = nb4 * 4 + q
                k.ts('dve', modv[l], modv[l][:, nb, :], pm, pm[:, q * 2:q * 2 + 2],
                     adab[:, l * 72 + nb:l * 72 + nb + 1], None, ALU.add, extra=[adab])
            emit_prep(per_blk)
        k.ts('pool', sc1[l], sc1[l][:, :, :], modv[l], modv[l][:, :, :], 1.0, None, ALU.add)
        for j in range(3):
            cc = (1.0 if j == 1 else 0.5) / ALPHA
            g0 = (3 * j + 2) * 8
            k.ts('pool', cf[l], cf[l][:, g0:g0 + 8, :], modv[l], modv[l][:, g0:g0 + 8, :], cc, None, ALU.mult)
    emit_prep(len(prep_units))
    k.release(mk0)
    mk2 = k.mark()
    xin = [k.sb(f"xin{i}", [128, D], F32) for i in range(2)]
    ev_rr = RR(['dve', 'act'])

    def load_x(src_d, r0, Xt, t0, i):
        xi = xin[i % 2]
        k.dma('sp', xi[:, :], src_d[r0:r0 + 128, :], xi, src_d)
        for g in range(2):
            pt = PS[2 + (2 * i + g) % 4]
            for q in range(4):
                c = g * 4 + q
                k.tr(pt, pt[:, q * 128:(q + 1) * 128], xi, xi[:, c * 128:(c + 1) * 128], ident, ident[:, :])
            k.copy(ev_rr(), Xt, Xt.h[:, g * 4:(g + 1) * 4, t0:t0 + 128],
                   pt, pt.h[:, :].rearrange("p (q t) -> p q t", q=4))

    for t in range(NTT):
        for b in range(4):
            if t < 4:
                load_x(xs_d, t * TT + b * 128, X[t], b * 128, t * 4 + b)
            else:
                load_x(xp_d, b * 128, X[t], b * 128, t * 4 + b)

    k.release(mk2)
    mod_rr = RR(['act', 'pool'])

    def modulate(l, j, t):
        col = 0 if t < 4 else 1
        for c in range(NC_):
            s_ap = sc1[l][:, (3 * j + 1) * 8 + c, col:col + 1]
            b_ap = modv[l][:, (3 * j) * 8 + c, col:col + 1]
            e = mod_rr()
            if e == 'act':
                k.act(H[t], H[t][:, c, :], X[t], X[t][:, c, :], AF.Identity, bias=b_ap, scale=s_ap,
                      extra=[sc1[l], modv[l]])
            else:
                k.ts('pool', H[t], H[t][:, c, :], X[t], X[t][:, c, :], s_ap, b_ap, ALU.mult, ALU.add,
                     extra=[sc1[l], modv[l]])

    class NS:
        pass
    F = NS()

    def alloc_ln():
        F.zb = k.sb("zb", [128, 16, TT], BF16)
        F.mean_sb = k.sb("ln_mean", [128, TT], F32)
        F.rstd_sb = k.sb("ln_rstd", [128, TT], F32)
        F.tmp_sb = k.sb("ln_tmp", [128, TT], F32)
        F.lt = [k.sb(f"ln_t{i}", [128, TT], F32) for i in range(2)]

    def alloc_ffn():
        F.G = k.sb("G", [128, NFC, TT], BF16)
        F.zb = F.G
        F.mean_sb = k.sb("ln_mean", [128, TT], F32)
        F.rstd_sb = k.sb("ln_rstd", [128, TT], F32)
        F.tmp_sb = k.sb("ln_tmp", [128, TT], F32)
        F.w1r = [k.sb(f"w1r{i}", [128, NC_, 256], BF16) for i in range(NW13)]
        F.w3r = [k.sb(f"w3r{i}", [128, NC_, 256], BF16) for i in range(NW13)]
        F.w2r = [k.sb(f"w2r{i}", [128, 512], BF16) for i in range(NW2)]
        F.sg = [k.sb(f"sg{i}", [128, TT], F32) for i in range(2)]
        F.lt = F.sg
        F.bgf = [k.sb(f"bgf{i}", [128, BGW], F32) for i in range(2)]
        F.bgb = [k.sb(f"bgb{i}", [128, BGW], BF16) for i in range(2)]

    NW13 = 2
    NW2 = 6

    def layernorm(l, j, t):
        Xt = X[t]
        pm, pq = PS[0], PS[1]
        zb, zq = F.zb, F.zb
        mean_sb, rstd_sb, tmp_sb, lt = F.mean_sb, F.rstd_sb, F.tmp_sb, F.lt
        for c in range(NC_):
            k.copy('dve' if c % 2 == 0 else 'pool', zb, zb[:, c, :], Xt, Xt[:, c, :])
            k.act(zq, zq[:, 8 + c, :], Xt, Xt[:, c, :], AF.Square)
        for c in range(NC_):
            k.mm(pm, pm[:, :], ones_bf, ones_bf[:, :], zb, zb[:, c, :], start=(c == 0), stop=(c == NC_ - 1))
        for c in range(NC_):
            k.mm(pq, pq[:, :], ones_bf, ones_bf[:, :], zq, zq[:, 8 + c, :], start=(c == 0), stop=(c == NC_ - 1))
        k.copy('dve', mean_sb, mean_sb[:, :], pm, pm[:, :])
        k.tt('dve', tmp_sb, tmp_sb[:, :], mean_sb, mean_sb[:, :], mean_sb, mean_sb[:, :], ALU.mult)
        k.tt('dve', tmp_sb, tmp_sb[:, :], pq, pq[:, :], tmp_sb, tmp_sb[:, :], ALU.subtract)
        k.ts('dve', tmp_sb, tmp_sb[:, :], tmp_sb, tmp_sb[:, :], 0.0, LN_EPS, ALU.max, ALU.add)
        k.act(rstd_sb, rstd_sb[:, :], tmp_sb, tmp_sb[:, :], AF.Ln)
        k.act(rstd_sb, rstd_sb[:, :], rstd_sb, rstd_sb[:, :], AF.Exp, scale=-0.5)
        gi = (l * 3 + j) * 8
        for c in range(NC_):
            tb = lt[c % 2]
            e = 'pool' if c % 4 == 3 else 'dve'
            k.tt(e, tb, tb[:, :], Xt, Xt[:, c, :], mean_sb, mean_sb[:, :], ALU.subtract)
            k.tt(e, tb, tb[:, :], tb, tb[:, :], rstd_sb, rstd_sb[:, :], ALU.mult)
            k.act(Xt, Xt[:, c, :], tb, tb[:, :], AF.Identity, bias=lnp[:, 48 + gi + c:48 + gi + c + 1],
                  scale=lnp[:, gi + c:gi + c + 1], extra=[lnp])

    BGW = 1416
    BG = NS()
    BG.q = []
    BG.pos = 0
    BG.tick = 0
    BG.stride = 1
    BG.i = 0

    def bg_add(src_d, dst_d, src2d, dst2d, R, C, W=None):
        W = W or BGW
        npc = (C + W - 1) // W
        w = (C + npc - 1) // npc
        for rc in range(R // 128):
            for pc in range(npc):
                c0 = pc * w
                n = min(w, C - c0)
                BG.q.append((src_d, dst_d, src2d[rc * 128:(rc + 1) * 128, c0:c0 + n],
                             dst2d[rc * 128:(rc + 1) * 128, c0:c0 + n], n))

    def bg_add_ffn(si, W=None):
        bg_add(w1_d, w1_s, w1_d.h[si], w1_s.h[si], D, DFF, W)
        bg_add(w3_d, w3_s, w3_d.h[si], w3_s.h[si], D, DFF, W)
        bg_add(w2_d, w2_s, w2_d.h[si], w2_s.h[si], DFF, D, W)

    def bg_load(u):
        src_d, dst_d, sap, dap, n = BG.q[u]
        i = u % 2
        k.dma('pool', F.bgf[i][:, 0:n], sap, F.bgf[i], src_d)

    def bg_cast_store(u):
        src_d, dst_d, sap, dap, n = BG.q[u]
        i = u % 2
        k.copy('pool', F.bgb[i], F.bgb[i][:, 0:n], F.bgf[i], F.bgf[i][:, 0:n])
        k.dma('pool', dap, F.bgb[i][:, 0:n], dst_d, F.bgb[i])

    def bg_start():
        n = len(BG.q) - BG.pos
        BG.stride = max(1, 300 // max(n, 1))
        BG.tick = 0
        BG.loaded = BG.pos - 1

    def bg_step():
        u = BG.pos
        if BG.loaded < u:
            bg_load(u)
            BG.loaded = u
        if u + 1 < len(BG.q) and BG.loaded < u + 1:
            bg_load(u + 1)
            BG.loaded = u + 1
        bg_cast_store(u)
        BG.pos += 1

    def bg_tick():
        BG.tick += 1
        if BG.tick % BG.stride == 0 and BG.pos < len(BG.q):
            bg_step()

    def bg_flush():
        while BG.pos < len(BG.q):
            bg_step()

    wc = {'a': 0, 'b': 0}

    def ffn(l, j, t):
        s = l * 2 + (0 if j == 0 else 1)
        col = 0 if t < 4 else 1
        Ht = H[t]
        G, w1r, w3r, w2r, sg = F.G, F.w1r, F.w3r, F.w2r, F.sg
        for fc in range(NFC):
            if fc % 2 == 0:
                i = wc['a'] % NW13
                wc['a'] += 1
                k.dma('sp', w1r[i][:, :, :],
                      w1_s.h[s].rearrange("(kc p) n -> p kc n", p=128)[:, :, fc * 128:(fc + 2) * 128], w1r[i], w1_s)
                k.dma('sp', w3r[i][:, :, :],
                      w3_s.h[s].rearrange("(kc p) n -> p kc n", p=128)[:, :, fc * 128:(fc + 2) * 128], w3r[i], w3_s)
            wo = (fc % 2) * 128
            pa, pb = PS[(fc % 2) * 2], PS[(fc % 2) * 2 + 1]
            for kc in range(NC_):
                k.mm(pa, pa[:, :], w1r[i], w1r[i][:, kc, wo:wo + 128], Ht, Ht[:, kc, :], start=(kc == 0), stop=(kc == NC_ - 1))
            for kc in range(NC_):
                k.mm(pb, pb[:, :], w3r[i], w3r[i][:, kc, wo:wo + 128], Ht, Ht[:, kc, :], start=(kc == 0), stop=(kc == NC_ - 1))
            sgi = sg[fc % 2]
            k.act(sgi, sgi[:, :], pa, pa[:, :], AF.Silu)
            k.tt('dve', G, G[:, fc, :], sgi, sgi[:, :], pb, pb[:, :], ALU.mult)
            bg_tick()
        for dg in range(2):
            for fc in range(NFC):
                i = wc['b'] % NW2
                wc['b'] += 1
                k.dma('sp', w2r[i][:, :], w2_s.h[s, fc * 128:(fc + 1) * 128, dg * 512:(dg + 1) * 512], w2r[i], w2_s)
                for q in range(4):
                    py = PS[4 + q]
                    k.mm(py, py[:, :], w2r[i], w2r[i][:, q * 128:(q + 1) * 128], G, G[:, fc, :],
                         start=(fc == 0), stop=(fc == NFC - 1))
                bg_tick()
            for q in range(4):
                c = dg * 4 + q
                py = PS[4 + q]
                k.stt(X[t], X[t][:, c, :], py, py[:, :], cf[l][:, (3 * j + 2) * 8 + c, col:col + 1],
                      X[t], X[t][:, c, :], ALU.mult, ALU.add, extra=[cf[l]])
        layernorm(l, j, t)

    def hs(seq, kc, i0, n):
        if seq == 0:
            t, c0 = i0 // TT, i0 % TT
        else:
            t, c0 = 4, (seq - 1) * LP + i0
        return H[t], H[t][:, kc, c0:c0 + n]

    M = NS()
    NWM = 4
    wmc = [0]

    def alloc_wm(n=3):
        M.wm = [k.sb(f"wm{i}", [128, NC_, 256], BF16) for i in range(n)]

    def next_wm():
        w = M.wm[wmc[0] % len(M.wm)]
        wmc[0] += 1
        return w

    def wload(wt, dcol, scr, col0, n):
        k.dma('sp', wt[:, :, dcol:dcol + n], scr.h.rearrange("(kc p) n -> p kc n", p=128)[:, :, col0:col0 + n], wt, scr)

    def proj_fm(seq, i0, n, wt, wc0, Mo, ps, ps_ap):
        for kc in range(NC_):
            Hb, hap = hs(seq, kc, i0, n)
            k.mm(ps, ps_ap, wt, wt[:, kc, wc0:wc0 + Mo], Hb, hap, start=(kc == 0), stop=(kc == NC_ - 1))

    def proj_tm(seq, i0, wt, wc0, ncols, ps, ps_ap):
        for kc in range(NC_):
            Hb, hap = hs(seq, kc, i0, 128)
            k.mm(ps, ps_ap, Hb, hap, wt, wt[:, kc, wc0:wc0 + ncols], start=(kc == 0), stop=(kc == NC_ - 1))

    def rstd_from(dst, dst_ap, src, src_ap, scale, eps):
        k.act(dst, dst_ap, src, src_ap, AF.Ln, bias=float(eps), scale=float(scale))
        k.act(dst, dst_ap, dst, dst_ap, AF.Exp, scale=-0.5)

    def attention(seq, L, cfg):
        nt = L // 128
        ctx = cfg['ctx'] if seq == 0 else None
        rope = cfg['rope'] and seq == 0
        band = cfg['band'] and seq == 0
        qknorm = cfg['qknorm']
        sinkexp = cfg['sinkexp']
        win = cfg['win']
        nctx = 4 if ctx is not None else 0
        qhalf = bool(cfg.get('qhalf')) and seq == 0
        mk = k.mark()
        bgl = cfg.get('bg') if seq == 0 else None
        if bgl is not None:
            F.bgf = [k.sb(f"abgf{i}", [128, 704], F32) for i in range(2)]
            F.bgb = [k.sb(f"abgb{i}", [128, 704], BF16) for i in range(2)]
            bgl()
            n_ = len(BG.q) - BG.pos
            BG.stride = max(1, 240 // max(n_, 1))
            BG.tick = 0
            BG.loaded = BG.pos - 1
        QT = k.sb("QT", [128, 4, L], BF16)
        KT = k.sb("KT", [128, 2, L + 128 * nctx], BF16)
        VA = k.sb("VA", [128, nt + nctx, 2, 128], BF16)
        k.memset('pool', VA, VA[:, :, :, 64:128], 1.0)
        mkA = k.mark()
        alloc_wm(2 if bgl is not None else 3)
        step = min(L, 512)
        qf = [k.sb(f"qf{i}", [128, step], F32) for i in range(2)]
        t1 = [k.sb(f"t1{i}", [128, step], F32) for i in range(1 if bgl is not None else 2)] * 2
        if rope:
            ropeT = k.sb("ropeT", [128, 2, step], F32)
        if qknorm:
            sq = k.sb("sq", [128, step], BF16)
            rs = k.sb("rs", [128, step], F32)
        kvo = [k.sb(f"kvo{i}", [128, 256], F32) for i in range(2)]
        if qknorm:
            kt2 = k.sb("kt2", [128, 128], F32)
            kss = k.sb("kss", [128, 2], F32)
        bi = 0
        for i0 in range(0, L, step):
            n = step
            if rope:
                k.dma('sp', ropeT[:, 0, :], cos_d[:, i0:i0 + n], ropeT, cos_d)
                k.dma('sp', ropeT[:, 1, :], sin_d[:, i0:i0 + n], ropeT, sin_d)
            for blk in range(DBG.get('nblk', 6)):
                if blk < 4 and qhalf and i0 >= L // 2:
                    continue
                wt = next_wm()
                if blk < 4:
                    wload(wt, 0, win, cfg['qcol'] + blk * 128, 128)
                    dst, dst_ap = QT, QT[:, blk, i0:i0 + n]
                    gcol = cfg.get('qg')
                else:
                    kv = blk - 4
                    wload(wt, 0, win, cfg['kcol'] + kv * 64, 64)
                    wload(wt, 64, win, cfg['kcol'] + kv * 64, 64)
                    dst, dst_ap = KT, KT[:, kv, i0:i0 + n]
                    gcol = cfg.get('kg')
                pp = PS[blk % 2]
                proj_fm(seq, i0, n, wt, 0, 128, pp, pp[:, :n])
                if bgl is not None:
                    bg_tick()
                qfi = qf[bi % 2]
                t1i = t1[bi % 2]
                bi += 1
                if qknorm:
                    k.act(sq, sq[:, :n], pp, pp[:, :n], AF.Square)
                    pn = PS[2]
                    k.mm(pn, pn[:, :n], bones64, bones64[:, :], sq, sq[:, :n])
                    rstd_from(rs, rs[:, :n], pn, pn[:, :n], 1.0, 1e-6)
                    k.stt(qfi, qfi[:, :n], pp, pp[:, :n], gcol, rs, rs[:, :n], ALU.mult, ALU.mult, extra=[gvec])
                    src, src_ap = qfi, qfi[:, :n]
                elif rope:
                    k.copy('act', qfi, qfi[:, :n], pp, pp[:, :n])
                    src, src_ap = qfi, qfi[:, :n]
                else:
                    src, src_ap = pp, pp[:, :n]
                if rope:
                    pr = PS[3]
                    k.mm(pr, pr[:, :n], RT, RT[:, :], src, src_ap)
                    k.tt('pool', t1i, t1i[:, :n], src, src_ap, ropeT, ropeT[:, 0, :n], ALU.mult)
                    k.tt('dve', src, src_ap, pr, pr[:, :n], ropeT, ropeT[:, 1, :n], ALU.mult)
                    k.tt('pool', dst, dst_ap, t1i, t1i[:, :n], src, src_ap, ALU.add)
                else:
                    if src.space == 'ps':
                        k.copy('act', dst, dst_ap, src, src_ap)
                    else:
                        k.copy('pool', dst, dst_ap, src, src_ap)
            wt = next_wm()
            wload(wt, 0, win, cfg['kcol'], 256)
            for b in range(0 if DBG.get('notm') else n // 128):
                pk = PS[4 + b % 2]
                proj_tm(seq, i0 + b * 128, wt, 0, 256, pk, pk[:, 0:256])
                tile = i0 // 128 + b
                if not DBG.get('nova'):
                    k.copy('dve', VA, VA.h[:, tile, :, 0:64], pk, pk.h[:, 128:256].rearrange("p (kv d) -> p kv d", kv=2))
                if seq > 0 and not DBG.get('noko'):
                    ko = kvo[b % 2]
                    k.copy('dve' if DBG.get('kodve') else 'act', ko, ko[:, :], pk, pk[:, 0:256])
                    if qknorm:
                        k.tt('dve', kt2, kt2[:, :], ko, ko[:, 0:128], ko, ko[:, 0:128], ALU.mult)
                        k.op('dve', lambda: nc.vector.tensor_reduce(out=kss[:, :], in_=kt2.h[:, :].rearrange("p (a d) -> p a d", a=2),
                                                                    op=ALU.add, axis=mybir.AxisListType.X), [kt2], [kss])
                        rstd_from(kss, kss[:, :], kss, kss[:, :], 1.0 / 64, 1e-6)
                        k.tt('dve', kt2, kt2.h[:, :].rearrange("p (a d) -> p a d", a=2),
                             ko, ko.h[:, 0:128].rearrange("p (a d) -> p a d", a=2),
                             kss, kss.h[:, :].unsqueeze(2).broadcast_to([128, 2, 64]), ALU.mult)
                        k.tt('dve', ko, ko[:, 0:128], kt2, kt2[:, :], kgrow, kgrow[:, :], ALU.mult)
                    r0 = i0 + b * 128
                    k.dma('pool', cfg['ko'][seq - 1, r0:r0 + 128, :], ko[:, 0:128], cfg['ko'], ko)
                    k.dma('pool', cfg['vo'][seq - 1, r0:r0 + 128, :], ko[:, 128:256], cfg['vo'], ko)
        if ctx is not None:
            kd, vd = ctx
            kcs = k.sb("kcs", [128, 4, 256], F32)
            vcs = k.sb("vcs", [128, 4, 128], F32)
            k.dma('sp', kcs[:, :, :], kd.h.rearrange("(t p) c -> p t c", p=128), kcs, kd)
            k.dma('sp', vcs[:, :, :], vd.h.rearrange("(t p) c -> p t c", p=128), vcs, vd)
            for tl in range(4):
                for kv in range(2):
                    pt = PS[4 + kv]
                    k.tr(pt, pt[:, 0:128], kcs, kcs[:, tl, kv * 128:(kv + 1) * 128], ident, ident[:, :])
                    k.copy('act', KT, KT[:, kv, L + tl * 128:L + (tl + 1) * 128], pt, pt[:, 0:128])
            k.copy('dve', VA, VA.h[:, nt:nt + 4, :, 0:64], vcs, vcs.h[:, :, :].rearrange("p t (kv d) -> p t kv d", kv=2))
        k.release(mkA)
        if DBG.get('attnA'):
            k.release(mk)
            return
        pT = [k.sb(f"pT{i}", [128, 512], BF16) for i in range(3)]
        osb = [k.sb(f"osb{i}", [128, 512], F32) for i in range(2)]
        rden = k.sb("rden", [64, 512], F32)
        yat = [k.sb(f"yat{i}", [64, 512], BF16) for i in range(2)]
        ya_s = cfg['ya_s'][seq]
        units = []
        steps = []
        for qb in range(nt // 2 if qhalf else nt):
            for kv in range(2):
                if band:
                    tiles = [(kt, (None if kt == qb else ('lo' if kt < qb else 'hi')))
                             for kt in (qb - 1, qb, qb + 1) if 0 <= kt < nt]
                else:
                    tiles = [(kt, None) for kt in range(nt)]
                tiles += [(nt + c, None) for c in range(nctx)]
                u = len(units)
                units.append((qb, kv))
                for ti, (kt, msk) in enumerate(tiles):
                    steps.append((u, kt, msk, ti == 0, ti == len(tiles) - 1))

        def emit_qk(si):
            u, kt, msk, first, last = steps[si]
            qb, kv = units[u]
            p = pT[si % 3]
            for hp in range(2):
                ps_s = PS[4 + 2 * (si % 2) + hp]
                k.mm(ps_s, ps_s.h[:, 0:256].rearrange("p (a q) -> p a q", a=2),
                     KT, KT[hp * 64:(hp + 1) * 64, kv, kt * 128:(kt + 1) * 128],
                     QT, QT[hp * 64:(hp + 1) * 64, kv * 2:kv * 2 + 2, qb * 128:(qb + 1) * 128])
                k.act(p, p[:, hp * 256:(hp + 1) * 256], ps_s, ps_s[:, 0:256], AF.Exp, scale=0.125)
            if msk is not None:
                mt_ = tri_lo4 if msk == 'lo' else tri_hi4
                k.tt('pool', p, p[:, :], p, p[:, :], mt_, mt_[:, :], ALU.mult)
            return p

        def emit_pv(si, p):
            u, kt, msk, first, last = steps[si]
            qb, kv = units[u]
            po = PS[2 + u % 2]
            k.mm(po, po[:, :], VA, VA[:, kt, kv, :], p, p[:, :], start=first, stop=last)
            if not last:
                return
            ob = osb[u % 2]
            k.copy('act', ob, ob[:, :], po, po[:, :])
            pd = PS[1]
            k.mm(pd, pd[0:64, :], shiftT, shiftT[:, :], ob, ob[:, :])
            if sinkexp is not None:
                k.tt('dve', rden, rden[:, :], pd, pd[0:64, :], sinkexp, sinkexp[:, kv, :], ALU.add)
                k.op('dve', lambda: nc.vector.reciprocal(out=rden[:, :], in_=rden[:, :]), [rden], [rden])
            else:
                k.op('dve', lambda: nc.vector.reciprocal(out=rden[:, :], in_=pd[0:64, :]), [pd], [rden])
            ya = yat[u % 2]
            k.tt('dve', ya, ya[:, :], ob, ob[0:64, :], rden, rden[:, :], ALU.mult)
            dstv = ya_s.h.rearrange("p (kv ab hp) l -> p kv hp ab l", kv=2, ab=2, hp=2)[:, kv, :, :, qb * 128:(qb + 1) * 128]
            srcv = ya.h[:, :].rearrange("p (hp ab q) -> p hp ab q", hp=2, ab=2)
            for hp in range(2):
                k.dma('pool', dstv[:, hp, :, :], srcv[:, hp, :, :], ya_s, ya)

        pcur = emit_qk(0)
        for si in range(len(steps)):
            pnext = emit_qk(si + 1) if si + 1 < len(steps) else None
            emit_pv(si, pcur)
            pcur = pnext
            if bgl is not None:
                bg_tick()
        if bgl is not None:
            bg_flush()
        k.release(mk)

    def mlstm(seq, L):
        SEG = min(L, 512)
        nseg = L // SEG
        ncs = SEG // 128
        win = wine_s
        mk = k.mark()
        alloc_wm(2)
        rw = {nm: k.sb("rw_" + nm, [4, SEG], F32) for nm in ('x', 'a', 'mn', 'B', 'r', 'M', 'one')}
        k.memset('pool', rw['one'], rw['one'][:, :], 1.0)
        rows3 = k.sb("rows3", [4, ncs, 3, 128], F32)
        rcol = k.sb("rcol", [128, ncs * 4], F32)
        carB = k.sb("carB", [4, 1], F32)
        carM = k.sb("carM", [4, 1], F32)
        Mprev = k.sb("Mprev", [4, ncs], F32)
        gb = k.sb("gb", [4, 4], F32)
        k.dma('sp', gb[:, :], gbias_d[:, :], gb, gbias_d)
        QTh = [k.sb(f"mQT{h}", [128, SEG], BF16) for h in range(4)]
        KTh = [k.sb(f"mKT{h}", [128, SEG], BF16) for h in range(4)]
        KVh = [k.sb(f"mKV{h}", [128, ncs, 257], BF16) for h in range(4)]
        for h in range(4):
            k.memset('pool', KVh[h], KVh[h][:, :, 256:257], 1.0)
        hseg = [k.sb(f"hseg{h}", [128, SEG], F32) for h in range(4)]
        Cst = [k.sb(f"Cst{h}", [128, 129], F32) for h in range(4)]
        nrep = [k.sb(f"nrep{h}", [128, 128], F32) for h in range(4)]
        onesf = k.sb("onesf", [128, 128], F32)
        k.memset('pool', onesf, onesf[:, :], 1.0)
        ones1b = k.sb("ones1b", [128, 128], BF16)
        k.memset('pool', ones1b, ones1b[:, :], 1.0)
        NB = 4
        Dt = [k.sb(f"Dt{i}", [128, 128], F32) for i in range(NB)]
        cols2 = [k.sb(f"cols2{i}", [128, 2], F32) for i in range(NB)]
        Wt = [k.sb(f"Wt{i}", [128, 128], BF16) for i in range(NB)]
        Qs = [k.sb(f"Qs{i}", [128, 128], F32) for i in range(NB)]
        dd = [k.sb(f"dd{i}", [128, 128], F32) for i in range(NB)]
        wsb = [k.sb(f"wsb{i}", [128, 1], F32) for i in range(NB)]
        Ks = [k.sb(f"Ks{i}", [128, 128], BF16) for i in range(NB)]
        hfl = k.sb("hfl", [128, SEG], F32)
        rsb = k.sb("rsb", [128, SEG], F32)
        sgb = k.sb("sgb", [128, SEG], F32)
        ybt = k.sb("ybt", [128, SEG], BF16)
        sqb = ybt
        ui = 0
        for d in range(2):
            fwd = (d == 0)
            for h in range(4):
                if seq == 0:
                    k.dma('sp', Cst[h][:, 0:128], stC_d[d, h, :, :], Cst[h], stC_d)
                    k.dma('sp', Cst[h][:, 128:129], stn_d.h[d, h, :].rearrange("(p o) -> p o", o=1), Cst[h], stn_d)
                else:
                    k.memset('pool', Cst[h], Cst[h][:, :], 0.0)
                k.ts('pool', nrep[h], nrep[h][:, :], onesf, onesf[:, :], Cst[h][:, 128:129], None, ALU.mult, extra=[Cst[h]])
            k.memset('pool', carB, carB[:, :], 0.0)
            if seq == 0:
                k.dma('sp', carM[:, :], stm_d.h[d, :].rearrange("(p o) -> p o", o=1), carM, stm_d)
            else:
                k.memset('pool', carM, carM[:, :], 0.0)
            segs = list(range(nseg)) if fwd else list(range(nseg - 1, -1, -1))
            for sg_ in segs:
                i0 = sg_ * SEG
                wt = next_wm()
                wload(wt, 0, win, 2304, 16)
                pgi, pgf = PS[0], PS[1]
                proj_fm(seq, i0, SEG, wt, (2 * d) * 4, 4, pgi, pgi[0:4, :SEG])
                proj_fm(seq, i0, SEG, wt, (2 * d + 1) * 4, 4, pgf, pgf[0:4, :SEG])
                x, a_, mn, B, r, Mx, one = (rw[nm] for nm in ('x', 'a', 'mn', 'B', 'r', 'M', 'one'))
                mt, tmp = a_, mn
                k.act(x, x[:, :], pgf, pgf[0:4, :SEG], AF.Identity, bias=gb[:, 2 * d + 1:2 * d + 2], extra=[gb])
                k.act(r, r[:, :], pgi, pgi[0:4, :SEG], AF.Identity, bias=gb[:, 2 * d:2 * d + 1], extra=[gb])
                k.act(a_, a_[:, :], x, x[:, :], AF.Abs)
                k.act(a_, a_[:, :], a_, a_[:, :], AF.Exp, scale=-1.0)
                k.act(a_, a_[:, :], a_, a_[:, :], AF.Ln, bias=1.0)
                k.ts('dve', mn, mn[:, :], x, x[:, :], -1.0, 0.0, ALU.mult, ALU.max)
                k.tt('dve', x, x[:, :], mn, mn[:, :], a_, a_[:, :], ALU.add)
                k.ts('dve', x, x[:, :], x, x[:, :], -1.0, None, ALU.mult)
                rv = (lambda t_: t_[:, :]) if fwd else (lambda t_: t_[:, ::-1])
                k.op('dve', lambda: nc.vector.tensor_tensor_scan(out=rv(B), data0=one[:, :], data1=rv(x), initial=carB[:, 0:1],
                                                                 op0=ALU.mult, op1=ALU.add), [one, x, carB], [B])
                k.tt('dve', r, r[:, :], r, r[:, :], B, B[:, :], ALU.subtract)
                k.op('dve', lambda: nc.vector.tensor_tensor_scan(out=rv(Mx), data0=one[:, :], data1=rv(r), initial=carM[:, 0:1],
                                                                 op0=ALU.mult, op1=ALU.max), [one, r, carM], [Mx])
                k.tt('dve', mt, mt[:, :], B, B[:, :], Mx, Mx[:, :], ALU.add)
                Mv = Mx.h[:, :].rearrange("p (c t) -> p c t", t=128)
                if fwd:
                    k.copy('dve', Mprev, Mprev[:, 0:1], carM, carM[:, 0:1])
                    if ncs > 1:
                        k.copy('dve', Mprev, Mprev[:, 1:ncs], Mx, Mv[:, 0:ncs - 1, 127])
                else:
                    k.copy('dve', Mprev, Mprev[:, ncs - 1:ncs], carM, carM[:, 0:1])
                    if ncs > 1:
                        k.copy('dve', Mprev, Mprev[:, 0:ncs - 1], Mx, Mv[:, 1:ncs, 0])
                last = SEG - 1 if fwd else 0
                k.copy('dve', carB, carB[:, :], B, B[:, last:last + 1])
                k.copy('dve', carM, carM[:, :], Mx, Mx[:, last:last + 1])
                k.ts('dve', rows3, rows3.h[:, :, 0, :], Mx, Mv, -1.0, None, ALU.mult)
                k.tt('dve', tmp, tmp.h[:, :].rearrange("p (c t) -> p c t", t=128), Mprev,
                     Mprev.h[:, :].unsqueeze(2).broadcast_to([4, ncs, 128]), Mx, Mv, ALU.subtract)
                k.act(rows3, rows3.h[:, :, 1, :], tmp, tmp.h[:, :].rearrange("p (c t) -> p c t", t=128), AF.Exp)
                k.act(rows3, rows3.h[:, :, 2, :], mt, mt.h[:, :].rearrange("p (c t) -> p c t", t=128), AF.Exp, scale=-1.0)
                prc = PS[2]
                for c in range(ncs):
                    k.tr(prc, prc[:, c * 4:(c + 1) * 4], r, r[0:4, c * 128:(c + 1) * 128], ident, ident[0:4, 0:4])
                k.copy('dve', rcol, rcol[:, :], prc, prc[:, 0:ncs * 4])
                if seq > 0 and sg_ == segs[-1]:
                    k.dma('pool', bm_o.h[seq - 1, d, :].rearrange("(p o) -> p o", o=1), mt[:, last:last + 1], bm_o, mt)
                for h in range(4):
                    wt = next_wm()
                    wload(wt, 0, win, 768 + h * 128, 128)
                    wload(wt, 128, win, 1280 + h * 128, 128)
                    pq, pk_ = PS[0], PS[1]
                    proj_fm(seq, i0, SEG, wt, 0, 128, pq, pq[:, :SEG])
                    k.copy('act', QTh[h], QTh[h][:, :], pq, pq[:, :SEG])
                    proj_fm(seq, i0, SEG, wt, 128, 128, pk_, pk_[:, :SEG])
                    k.act(KTh[h], KTh[h][:, :], pk_, pk_[:, :SEG], AF.Identity, scale=128 ** -0.5)
                    wt2 = next_wm()
                    wload(wt2, 0, win, 1280 + h * 128, 128)
                    wload(wt2, 128, win, 1792 + h * 128, 128)
                    for c in range(ncs):
                        pkv = PS[2 + c % 2]
                        proj_tm(seq, i0 + c * 128, wt2, 0, 256, pkv, pkv[:, 0:256])
                        k.act(KVh[h], KVh[h][:, c, 0:128], pkv, pkv[:, 0:128], AF.Identity, scale=128 ** -0.5)
                        k.copy('dve', KVh[h], KVh[h][:, c, 128:256], pkv, pkv[:, 128:256])
                chunks = list(range(ncs)) if fwd else list(range(ncs - 1, -1, -1))
                edge = 127 if fwd else 0
                msk = tri_hi4 if fwd else tri_lo4
                for c in chunks:
                    cs = slice(c * 128, (c + 1) * 128)
                    bA = [PS[2 * h] for h in range(4)]
                    bB = [PS[2 * h + 1] for h in range(4)]
                    for h in range(4):
                        k.mm(bA[h], bA[h][:, 0:384], sel4, sel4[:, h, :], rows3, rows3.h[:, c, :, :].rearrange("p a t -> p (a t)"))
                        k.mm(bA[h], bA[h][:, 384:512], KTh[h], KTh[h][:, cs], QTh[h], QTh[h][:, cs])
                    for h in range(4):
                        u = h
                        k.act(Dt[u], Dt[u][:, :], bA[h], bA[h][:, 0:128], AF.Exp, bias=rcol[:, c * 4 + h:c * 4 + h + 1], extra=[rcol])
                        k.copy('act', cols2[u], cols2[u][:, :], bA[h], bA[h][:, edge:edge + 129:128])
                        k.tt('pool', Dt[u], Dt[u][:, :], Dt[u], Dt[u][:, :], msk, msk[:, 0:128], ALU.mult)
                        k.tt('dve', Qs[u], Qs[u][:, :], QTh[h], QTh[h][:, cs], bA[h], bA[h][:, 128:256], ALU.mult)
                        k.tt('dve', Wt[u], Wt[u][:, :], Dt[u], Dt[u][:, :], bA[h], bA[h][:, 384:512], ALU.mult)
                    for h in range(4):
                        u = h
                        k.mm(bB[h], bB[h][:, 0:128], Cst[h], Cst[h][:, 0:128], Qs[u], Qs[u][:, :], start=True, stop=False)
                        k.mm(bB[h], bB[h][:, 0:128], KVh[h], KVh[h][:, c, 128:256], Wt[u], Wt[u][:, :], start=False, stop=True)
                        k.mm(bB[h], bB[h][:, 128:256], nrep[h], nrep[h][:, :], Qs[u], Qs[u][:, :], start=True, stop=False)
                        k.mm(bB[h], bB[h][:, 128:256], ones1b, ones1b[:, :], Wt[u], Wt[u][:, :], start=False, stop=True)
                    for h in range(4):
                        u = h
                        k.act(dd[u], dd[u][:, :], bB[h], bB[h][:, 128:256], AF.Abs)
                        k.act(wsb[u], wsb[u][:, :], rcol, rcol[:, c * 4 + h:c * 4 + h + 1], AF.Exp, bias=cols2[u][:, 0:1],
                              extra=[cols2[u]])
                        k.tt('dve', dd[u], dd[u][:, :], dd[u], dd[u][:, :], bA[h], bA[h][:, 256:384], ALU.max)
                        k.op('dve', lambda: nc.vector.reciprocal(out=dd[u][:, :], in_=dd[u][:, :]), [dd[u]], [dd[u]])
                        k.tt('dve', hseg[h], hseg[h][:, cs], bB[h], bB[h][:, 0:128], dd[u], dd[u][:, :], ALU.mult)
                        k.ts('pool', Ks[u], Ks[u][:, :], KVh[h], KVh[h][:, c, 0:128], wsb[u][:, 0:1], None, ALU.mult,
                             extra=[wsb[u]])
                    for h in range(4):
                        u = h
                        k.mm(bB[h], bB[h][:, 256:385], Ks[u], Ks[u][:, :], KVh[h], KVh[h][:, c, 128:257])
                    for h in range(4):
                        u = h
                        k.stt(Cst[h], Cst[h][:, :], Cst[h], Cst[h][:, :], cols2[u][:, 1:2], bB[h], bB[h][:, 256:385], ALU.mult, ALU.add,
                              extra=[cols2[u]])
                        k.ts('pool', nrep[h], nrep[h][:, :], onesf, onesf[:, :], Cst[h][:, 128:129], None, ALU.mult,
                             extra=[Cst[h]])
                for h in range(4):
                    if fwd:
                        k.dma('pool', hf_s[seq][h, :, i0:i0 + SEG], hseg[h][:, :], hf_s[seq], hseg[h])
                    else:
                        k.dma('sp', hfl[:, :], hf_s[seq][h, :, i0:i0 + SEG], hfl, hf_s[seq])
                        k.tt('pool', hfl, hfl[:, :], hfl, hfl[:, :], hseg[h], hseg[h][:, :], ALU.add)
                        k.act(sqb, sqb[:, :], hfl, hfl[:, :], AF.Square)
                        pr = PS[0]
                        k.mm(pr, pr[:, :SEG], ones_bf, ones_bf[:, :], sqb, sqb[:, :])
                        rstd_from(rsb, rsb[:, :], pr, pr[:, :SEG], 8.0, 1e-6)
                        wt = next_wm()
                        wload(wt, 0, win, 2320 + h * 128, 128)
                        po_ = PS[1]
                        proj_fm(seq, i0, SEG, wt, 0, 128, po_, po_[:, :SEG])
                        k.act(sgb, sgb[:, :], po_, po_[:, :SEG], AF.Sigmoid)
                        k.stt(hfl, hfl[:, :], hfl, hfl[:, :], bng[:, h:h + 1], rsb, rsb[:, :], ALU.mult, ALU.mult, extra=[bng])
                        k.tt('pool', ybt, ybt[:, :], hfl, hfl[:, :], sgb, sgb[:, :], ALU.mult)
                        k.dma('pool', yb_s[seq][:, h, i0:i0 + SEG], ybt[:, :], yb_s[seq], ybt)
            if seq > 0:
                for h in range(4):
                    k.dma('pool', bC_o[seq - 1, d, h, :, :], Cst[h][:, 0:128], bC_o, Cst[h])
                    k.dma('pool', bn_o.h[seq - 1, d, h, :].rearrange("(p o) -> p o", o=1), Cst[h][:, 128:129], bn_o, Cst[h])
        k.release(mk)
    def hgrn(seq, L):
        SEG = min(L, 512)
        nseg = L // SEG
        ngr = SEG // 128
        nch = SEG // 32
        win = wino_s
        mk = k.mark()
        wA2 = [k.sb(f"gwA{i}", [128, NC_, 256], BF16) for i in range(2)]
        wB2 = [k.sb(f"gwB{i}", [128, NC_, 128], BF16) for i in range(2)]
        onesS = k.sb("onesS", [128, SEG], F32)
        k.memset('pool', onesS, onesS[:, :], 1.0)
        fT2 = [k.sb(f"fT{i}", [128, SEG], F32) for i in range(2)]
        kT2 = [k.sb(f"kT{i}", [128, SEG], F32) for i in range(2)]
        eT2 = [k.sb(f"eT{i}", [128, SEG], F32) for i in range(2)]
        Zh2 = [k.sb(f"gZ{i}", [128, SEG], F32) for i in range(2)]
        khf2 = [k.sb(f"gkh{i}", [128, SEG], F32) for i in range(2)]
        qT = [k.sb(f"gqT{h}", [128, SEG], F32) for h in range(4)]
        qb2 = [k.sb(f"gqb{i}", [128, SEG], BF16) for i in range(2)]
        Kmix = [k.sb(f"gKmix{i}", [128, 4, 128], BF16) for i in range(2)]
        tmpE = [k.sb(f"gtmpE{i}", [128, 128], F32) for i in range(2)]
        Kh = [k.sb(f"gKh{h}", [128, ngr, 128], BF16) for h in range(4)]
        Vt = [k.sb(f"gVt{h}", [128, ngr, 128], BF16) for h in range(4)]
        att = [k.sb(f"gatt{h}", [128, ngr, 128], BF16) for h in range(4)]
        oseg = [k.sb(f"goseg{h}", [128, SEG], F32) for h in range(4)]
        dec = [k.sb(f"gdec{h}", [128, ngr], F32) for h in range(4)]
        refc2 = [k.sb(f"grefc{i}", [128, nch], F32) for i in range(2)]
        S = [k.sb(f"gS{h}", [128, 128], F32) for h in range(4)]
        carZ = [k.sb(f"gcarZ{h}", [128, 1], F32) for h in range(4)]
        ofl, rsb, sgb = fT2[0], fT2[1], kT2[0]
        sqb, yct = qb2[0], qb2[1]
        v32 = lambda t_: t_.h[:, :].rearrange("p (c t) -> p c t", t=32)
        v128 = lambda t_: t_.h[:, :].rearrange("p (c t) -> p c t", t=128)
        for d in range(2):
            fwd = (d == 0)
            for i in range(2):
                k.memset('pool', Kmix[i], Kmix[i][:, :, :], 0.0)
            for h in range(4):
                if seq == 0:
                    k.dma('sp', S[h][:, :], stS_d[d, h, :, :], S[h], stS_d)
                else:
                    k.memset('pool', S[h], S[h][:, :], 0.0)
                k.memset('pool', carZ[h], carZ[h][:, :], 0.0)
            segs = list(range(nseg)) if fwd else list(range(nseg - 1, -1, -1))
            if seq == 0 and fwd:
                segs = segs[:nseg // 2]
            rv = (lambda t_: t_[:, :]) if fwd else (lambda t_: t_[:, ::-1])
            msk = tri_hi4 if fwd else tri_lo4
            for sg_ in segs:
                i0 = sg_ * SEG
                so = (seq == 0 and sg_ >= nseg // 2)
                def head_prep(h, par):
                    fT, kT, eT, Zh, khf, qb, refc = fT2[par], kT2[par], eT2[par], Zh2[par], khf2[par], qb2[par], refc2[par]
                    lfT = eT
                    PB = 4 * par
                    wt = wA2[par]
                    wload(wt, 0, win, h * 128, 128)
                    wload(wt, 128, win, 512 * (1 + d) + h * 128, 128)
                    pq, pf = PS[PB + 0], PS[PB + 1]
                    if not so:
                        proj_fm(seq, i0, SEG, wt, 0, 128, pq, pq[:, :SEG])
                        yield
                    proj_fm(seq, i0, SEG, wt, 128, 128, pf, pf[:, :SEG])
                    yield
                    if not so:
                        k.act(qT[h], qT[h][:, :], pq, pq[:, :SEG], AF.Silu)
                        yield
                    k.act(fT, fT[:, :], pf, pf[:, :SEG], AF.Sigmoid)
                    yield
                    k.ts('dve', fT, fT[:, :], fT, fT[:, :], oml[:, h:h + 1], lbc[:, h:h + 1], ALU.mult, ALU.add, extra=[oml, lbc])
                    yield
                    k.act(lfT, lfT[:, :], fT, fT[:, :], AF.Ln)
                    yield
                    k.ts('pool', kT, kT[:, :], fT, fT[:, :], -1.0, 1.0, ALU.mult, ALU.add)
                    yield
                    k.op('dve', lambda: nc.vector.tensor_tensor_scan(out=rv(Zh), data0=onesS[:, :], data1=rv(lfT),
                                                                     initial=carZ[h][:, 0:1], op0=ALU.mult, op1=ALU.add),
                         [onesS, lfT, carZ[h]], [Zh])
                    yield
                    Zv = v32(Zh)
                    Zg = v128(Zh)
                    if fwd:
                        k.copy('dve', refc, refc[:, 0:1], carZ[h], carZ[h][:, 0:1])
                        yield
                        k.copy('dve', refc, refc[:, 1:nch], Zh, Zv[:, 0:nch - 1, 31])
                        yield
                        refg_ap = refc[:, 0:nch:4]
                        edgeg_ap = Zg[:, :, 127]
                    else:
                        k.copy('dve', refc, refc[:, nch - 1:nch], carZ[h], carZ[h][:, 0:1])
                        yield
                        k.copy('dve', refc, refc[:, 0:nch - 1], Zh, Zv[:, 1:nch, 0])
                        yield
                        refg_ap = refc[:, 3:nch:4]
                        edgeg_ap = Zg[:, :, 0]
                    last = SEG - 1 if fwd else 0
                    k.copy('dve', carZ[h], carZ[h][:, :], Zh, Zh[:, last:last + 1])
                    yield
                    refb = refc.h[:, :].unsqueeze(2).broadcast_to([128, nch, 32])
                    refgb = refg_ap.unsqueeze(2).broadcast_to([128, ngr, 128])
                    edgegb = edgeg_ap.unsqueeze(2).broadcast_to([128, ngr, 128])
                    if not so:
                        k.tt('dve', eT, v32(eT), Zh, Zv, refc, refb, ALU.subtract)
                        yield
                        k.act(eT, eT[:, :], eT, eT[:, :], AF.Exp)
                        yield
                        k.tt('pool', qb, qb[:, :], qT[h], qT[h][:, :], eT, eT[:, :], ALU.mult)
                        yield
                        k.tt('dve', eT, v128(eT), Zh, Zg, refc, refgb, ALU.subtract)
                        yield
                        k.act(eT, eT[:, :], eT, eT[:, :], AF.Exp)
                        yield
                        k.tt('pool', qT[h], qT[h][:, :], qT[h], qT[h][:, :], eT, eT[:, :], ALU.mult)
                        yield
                    k.tt('dve', eT, v128(eT), Zh, edgegb, Zh, Zg, ALU.subtract)
                    yield
                    k.act(eT, eT[:, :], eT, eT[:, :], AF.Exp)
                    yield
                    k.tt('pool', khf, khf[:, :], kT, kT[:, :], eT, eT[:, :], ALU.mult)
                    yield
                    k.tt('dve', dec[h], dec[h][:, :], Zh, edgeg_ap, refc, refg_ap, ALU.subtract)
                    yield
                    k.act(dec[h], dec[h][:, :], dec[h], dec[h][:, :], AF.Exp)
                    yield
                    ptb = PS[PB + 1]
                    for g in range(ngr):
                        k.tr(ptb, ptb[:, g * 128:(g + 1) * 128], khf, khf[:, g * 128:(g + 1) * 128], ident, ident[:, :])
                        yield
                    k.copy('act', Kh[h], Kh[h].h[:, :, :], ptb, ptb.h[:, 0:ngr * 128].rearrange("p (g d) -> p g d", g=ngr))
                    yield
                    wt2 = wB2[par]
                    wload(wt2, 0, win, 1536 + h * 128, 128)
                    for g in range(ngr):
                        pv = PS[PB + 2]
                        proj_tm(seq, i0 + g * 128, wt2, 0, 128, pv, pv[:, 0:128])
                        yield
                        k.copy('act', Vt[h], Vt[h][:, g, :], pv, pv[:, 0:128])
                        yield
                    for g in range(0 if so else ngr):
                        Km = Kmix[par]
                        pa = PS[PB + 3]
                        for a in range(4):
                            c0, c1 = (0, 32 * (a + 1)) if fwd else (32 * a, 128)
                            te = tmpE[par]
                            cs = slice(g * 128 + c0, g * 128 + c1)
                            ci = g * 4 + a
                            k.act(te, te[:, c0:c1], Zh, Zh[:, cs], AF.Exp, bias=refc[:, ci:ci + 1], scale=-1.0, extra=[refc])
                            yield
                            k.tt('pool', Km, Km[:, a, c0:c1], kT, kT[:, cs], te, te[:, c0:c1], ALU.mult)
                            yield
                            k.mm(pa, pa[:, a * 32:(a + 1) * 32], Km, Km[:, a, :], qb, qb[:, g * 128 + a * 32:g * 128 + (a + 1) * 32])
                            yield
                        k.tt('dve', att[h], att[h][:, g, :], pa, pa[:, 0:128], msk, msk[:, 0:128], ALU.mult)
                        yield

                for h0 in (0, 2):
                    gens = [head_prep(h0, 0), head_prep(h0 + 1, 1)]
                    alive = [True, True]
                    while any(alive):
                        for gi in range(2):
                            if alive[gi]:
                                try:
                                    next(gens[gi])
                                except StopIteration:
                                    alive[gi] = False
                groups = list(range(ngr)) if fwd else list(range(ngr - 1, -1, -1))
                for g in groups:
                    gs = slice(g * 128, (g + 1) * 128)
                    for h in range(4):
                        pO = PS[h % 2]
                        if not so:
                            k.mm(pO, pO[:, 0:128], Vt[h], Vt[h][:, g, :], att[h], att[h][:, g, :], start=True, stop=False)
                            k.mm(pO, pO[:, 0:128], S[h], S[h][:, :], qT[h], qT[h][:, gs], start=False, stop=True)
                        pD = PS[2 + h % 2]
                        k.mm(pD, pD[:, 0:128], Kh[h], Kh[h][:, g, :], Vt[h], Vt[h][:, g, :])
                        k.stt(S[h], S[h][:, :], S[h], S[h][:, :], dec[h][:, g:g + 1], pD, pD[:, 0:128], ALU.mult, ALU.add,
                              extra=[dec[h]])
                        if not so:
                            k.copy('act', oseg[h], oseg[h][:, gs], pO, pO[:, 0:128])
                for h in range(0 if so else 4):
                    if fwd:
                        k.dma('pool', of_s[seq][h, :, i0:i0 + SEG], oseg[h][:, :], of_s[seq], oseg[h])
                    else:
                        k.dma('sp', ofl[:, :], of_s[seq][h, :, i0:i0 + SEG], ofl, of_s[seq])
                        k.tt('pool', ofl, ofl[:, :], ofl, ofl[:, :], oseg[h], oseg[h][:, :], ALU.add)
                        k.act(sqb, sqb[:, :], ofl, ofl[:, :], AF.Square)
                        pr = PS[6]
                        k.mm(pr, pr[:, :SEG], ones_bf, ones_bf[:, :], sqb, sqb[:, :])
                        rstd_from(rsb, rsb[:, :], pr, pr[:, :SEG], 8.0, 1e-6)
                        wt = wB2[h % 2]
                        wload(wt, 0, win, 2048 + h * 128, 128)
                        pg = PS[7]
                        proj_fm(seq, i0, SEG, wt, 0, 128, pg, pg[:, :SEG])
                        k.act(sgb, sgb[:, :], pg, pg[:, :SEG], AF.Silu)
                        k.stt(ofl, ofl[:, :], ofl, ofl[:, :], cng[:, h:h + 1], rsb, rsb[:, :], ALU.mult, ALU.mult, extra=[cng])
                        k.tt('pool', yct, yct[:, :], ofl, ofl[:, :], sgb, sgb[:, :], ALU.mult)
                        k.dma('pool', yb_s[seq][:, h, i0:i0 + SEG], yct[:, :], yb_s[seq], yct)
            if seq > 0:
                for h in range(4):
                    k.dma('pool', cS_o[seq - 1, d, h, :, :], S[h][:, :], cS_o, S[h])
        k.release(mk)

    def mixer_out(l, wout, ya_first):
        mk = k.mark()
        woa = k.sb("woa", [64, 8, D], BF16)
        wob = k.sb("wob", [128, 4, D], BF16)
        ra, rb = (0, 512) if ya_first else (512, 0)
        k.dma('sp', woa[:, :, :], wout.h[ra:ra + 512, :].rearrange("(h p) n -> p h n", p=64), woa, wout)
        k.dma('sp', wob[:, :, :], wout.h[rb:rb + 512, :].rearrange("(h p) n -> p h n", p=128), wob, wout)
        yat = [k.sb(f"oyat{i}", [64, 8, 256], BF16) for i in range(2)]
        ybt = [k.sb(f"oybt{i}", [128, 4, 256], BF16) for i in range(2)]
        alloc_ln()
        mbg = (l == 0)
        if mbg:
            F.bgf = [k.sb(f"obgf{i}", [128, 704], F32) for i in range(2)]
            F.bgb = [k.sb(f"obgb{i}", [128, 704], BF16) for i in range(2)]
            bg_add_ffn(2, 704)
            BG.stride = 1
            BG.tick = 0
            BG.loaded = BG.pos - 1
        ii = 0
        for t in ((0, 1, 4) if l == 1 else range(NTT)):
            col = 0 if t < 4 else 1
            for hf in range(2):
                ya_, yb_ = yat[ii % 2], ybt[ii % 2]
                ii += 1
                if t < 4:
                    seq, c0 = 0, t * TT + hf * 256
                else:
                    seq, c0 = 1 + hf, 0
                k.dma('sp', ya_[:, :, :], ya_s[seq][:, :, c0:c0 + 256], ya_, ya_s[seq])
                k.dma('sp', yb_[:, :, :], yb_s[seq][:, :, c0:c0 + 256], yb_, yb_s[seq])
                for dc in range(NC_):
                    py = PS[dc % 4]
                    for hd in range(8):
                        k.mm(py, py[:, 0:256], woa, woa[:, hd, dc * 128:(dc + 1) * 128], ya_, ya_[:, hd, :],
                             start=(hd == 0), stop=False)
                    for hd in range(4):
                        k.mm(py, py[:, 0:256], wob, wob[:, hd, dc * 128:(dc + 1) * 128], yb_, yb_[:, hd, :],
                             start=False, stop=(hd == 3))
                    xs = slice(hf * 256, (hf + 1) * 256)
                    k.stt(X[t], X[t][:, dc, xs], py, py[:, 0:256], cf[l][:, (3 * 1 + 2) * 8 + dc, col:col + 1],
                          X[t], X[t][:, dc, xs], ALU.mult, ALU.add, extra=[cf[l]])
                    if mbg:
                        bg_tick()
            layernorm(l, 1, t)
            modulate(l, 2, t)
        if mbg:
            bg_flush()
        k.release(mk)

    stop_after = None if debug_stage is None else debug_stage.get('stop')
    cfgA = dict(win=wine_s, qcol=0, kcol=512, rope=True, band=True, qknorm=False, sinkexp=sinkexp,
                ctx=(kctxa_d, vctxa_d), ko=ak_o, vo=av_o, ya_s=ya_s, bg=lambda: bg_add_ffn(1, 704))
    cfgD = dict(win=wino_s, qcol=2560, kcol=3072, rope=True, band=False, qknorm=True, sinkexp=None,
                ctx=(kctxd_d, vctxd_d), ko=dk_o, vo=dv_o, ya_s=ya_s, qg=gvec[:, 0:1], kg=gvec[:, 1:2], qhalf=True)

    def ffn_phase(l, j, nxt=None):
        mk = k.mark()
        alloc_ffn()
        if (l, j) == (0, 0):
            bg_add(wine_d, wine_s, wine_d.h, wine_s.h, D, 2832)
            bg_add(woute_d, woute_s, woute_d.h, woute_s.h, D, D)
        elif (l, j) == (0, 2):
            bg_add(wino_d, wino_s, wino_d.h, wino_s.h, D, 3328)
        elif (l, j) == (1, 0):
            bg_add(wouto_d, wouto_s, wouto_d.h, wouto_s.h, D, D)
            bg_add_ffn(3)
        bg_start()
        for t in ((0, 1, 4) if (l, j) == (1, 2) else range(NTT)):
            ffn(l, j, t)
            if nxt is not None:
                modulate(nxt[0], nxt[1], t)
        bg_flush()
        k.release(mk)

    def mark_phase(label):
        PHASE_MARKS.append((label, dict(k.cnt)))

    def run_all_marked():
        mark_phase('setup_end')
        for t in range(NTT):
            modulate(0, 0, t)
        ffn_phase(0, 0, nxt=(0, 1)); mark_phase('ffn00')
        for seq in range(3):
            attention(seq, SL[seq], cfgA); mark_phase(f'attnA{seq}')
            mlstm(seq, SL[seq]); mark_phase(f'mlstm{seq}')
        mixer_out(0, woute_s, True); mark_phase('mout0')
        ffn_phase(0, 2, nxt=(1, 0)); mark_phase('ffn02')
        ffn_phase(1, 0, nxt=(1, 1)); mark_phase('ffn10')
        for seq in range(3):
            hgrn(seq, SL[seq]); mark_phase(f'hgrn{seq}')
            attention(seq, SL[seq], cfgD); mark_phase(f'attnD{seq}')
        mixer_out(1, wouto_s, False); mark_phase('mout1')
        ffn_phase(1, 2, nxt=None); mark_phase('ffn12')

    def run_all():
        if debug_stage is None:
            return run_all_marked()
        for t in range(NTT):
            modulate(0, 0, t)
        ffn_phase(0, 0, nxt=(0, 1))
        if stop_after == 'f00':
            return
        only = None if debug_stage is None else debug_stage.get('only')
        if only is not None:
            for nm in only:
                if nm[0] == 'a':
                    attention(int(nm[1]), SL[int(nm[1])], cfgA)
                if nm[0] == 'm':
                    mlstm(int(nm[1]), SL[int(nm[1])])
                if nm[0] == 'o':
                    mixer_out(0, woute_s, True)
            return
        for seq in range(3):
            attention(seq, SL[seq], cfgA)
            mlstm(seq, SL[seq])
        mixer_out(0, woute_s, True)
        if stop_after == 'm0':
            return
        ffn_phase(0, 2, nxt=(1, 0))
        ffn_phase(1, 0, nxt=(1, 1))
        if stop_after == 'f10':
            return
        only1 = None if debug_stage is None else debug_stage.get('only1')
        if only1 is not None:
            for nm in only1:
                if nm[0] == 'a':
                    attention(int(nm[1]), SL[int(nm[1])], cfgD)
                if nm[0] == 'g':
                    hgrn(int(nm[1]), SL[int(nm[1])])
                if nm[0] == 'o':
                    mixer_out(1, wouto_s, False)
            return
        for seq in range(3):
            hgrn(seq, SL[seq])
            attention(seq, SL[seq], cfgD)
        mixer_out(1, wouto_s, False)
        if stop_after == 'm1':
            return
        ffn_phase(1, 2, nxt=None)

    run_all()

    xo = [k.sb(f"xo{i}", [128, D], F32) for i in range(2)]

    def store_x(dst_d, r0, Xt, t0, i):
        xi = xo[i % 2]
        for g in range(2):
            pt = PS[2 + (2 * i + g) % 4]
            for q in range(4):
                c = g * 4 + q
                k.tr(pt, pt[:, q * 128:(q + 1) * 128], Xt, Xt[:, c, t0:t0 + 128], ident, ident[:, :])
            k.copy(ev_rr(), xi, xi[:, g * 512:(g + 1) * 512], pt, pt[:, :])
        k.dma('pool', dst_d[r0:r0 + 128, :], xi[:, :], dst_d, xi)

    for t in (0, 1, 4):
        for b in range(4):
            if t < 4:
                store_x(ys_d, t * TT + b * 128, X[t], b * 128, t * 4 + b)
            else:
                store_x(yp_d, b * 128, X[t], b * 128, t * 4 + b)
    mark_phase('store')
    k.barrier()
    return nc


def _consts(mir=False):
    import ml_dtypes
    c = {}
    c["ident"] = np.eye(128, dtype=np.float32)
    nf = 16
    inv = (10000.0 ** (-np.arange(nf, dtype=np.float32) / nf)).astype(np.float32)
    t = np.arange(LS)
    if mir:
        t = (LS - 1) - t
    row = (t // 64).astype(np.float32)
    colp = (t % 64).astype(np.float32)
    cosT = np.zeros((128, LS), np.float32)
    sinT = np.zeros((128, LS), np.float32)
    for p in range(128):
        d = p % 64
        pos = row if d < 32 else colp
        ang = (pos * inv[d % 16]).astype(np.float32)
        cosT[p] = np.cos(ang)
        sinT[p] = np.sin(ang)
    c["cosT"], c["sinT"] = cosT, sinT
    R = np.zeros((128, 128), np.float32)
    for m in range(128):
        if (m % 32) < 16:
            R[m, m + 16] = -1.0
        else:
            R[m, m - 16] = 1.0
    c["RT"] = np.ascontiguousarray(R.T)
    sh = np.zeros((128, 64), np.float32)
    for i in range(64):
        sh[64 + i, i] = 1.0
    c["shiftT"] = sh
    sel = np.zeros((4, 4, 128), np.float32)
    for h in range(4):
        sel[h, h, :] = 1.0
    c["sel4"] = sel.reshape(4, 512)
    s = np.arange(128)[:, None]
    tt_ = np.arange(128)[None, :]
    hi = (s <= tt_).astype(np.float32)
    lo = (s >= tt_).astype(np.float32)
    same = ((s // 32) == (tt_ // 32)).astype(np.float32)
    b64 = ((s // 64) == (tt_ // 64)).astype(np.float32) / 64.0
    mskb = np.concatenate([np.tile(hi, (1, 4)), np.tile(lo, (1, 4)), hi * same, lo * same, b64], 1)
    c["mskb"] = mskb.astype(ml_dtypes.bfloat16)
    return c


def make_in_maps(inp):
    f = lambda a: np.ascontiguousarray(np.asarray(a, dtype=np.float32))
    maps = []
    lnp = np.concatenate([f(inp['ln_g']).reshape(48, 128), f(inp['ln_b']).reshape(48, 128)], 0)
    lbl = f(inp['c_lb_logits']).reshape(2, 4, 128)
    smallc = np.zeros((128, 32), np.float32)
    smallc[:, 0:4] = f(inp['b_norm_g'])[0].T
    smallc[:, 4:8] = f(inp['c_norm_g'])[0].T
    smallc[:, 8] = np.tile(f(inp['d_q_norm'])[0], 2)
    smallc[:, 9] = np.tile(f(inp['d_k_norm'])[0], 2)
    smallc[:, 10:14] = lbl[0].T
    smallc[:, 14:18] = lbl[1].T
    smallc[:, 18:26] = np.broadcast_to(f(inp['a_sink'])[0].reshape(1, 8), (128, 8))
    kgrow = np.broadcast_to(np.tile(f(inp['d_k_norm'])[0], 2)[None, :], (128, 128)).copy()
    base = {
        "ada_w": f(inp['ada_w']),
        "ada_b": f(inp['ada_b']).reshape(144, 128),
        "lnp": lnp,
        "ffn_w1": f(inp['ffn_w1']).reshape(4, D, DFF),
        "ffn_w3": f(inp['ffn_w3']).reshape(4, D, DFF),
        "ffn_w2": f(inp['ffn_w2']).reshape(4, DFF, D),
        "wout_e": f(inp['w_out_even'])[0],
        "wout_o": f(inp['w_out_odd'])[0],
        "smallc": smallc, "kgrow": kgrow,
    }
    win_e = f(inp['w_in_even'])[0]
    win_o = f(inp['w_in_odd'])[0]
    gb = f(inp['b_gate_bias'])[0]
    variants = []
    for mir in (False, True):
        v = dict(base)
        v.update(_consts(mir))
        if not mir:
            v["win_e"], v["win_o"] = win_e, win_o
            v["gbias"] = np.ascontiguousarray(gb.T)
        else:
            we = win_e.copy()
            we[:, 2304:2312], we[:, 2312:2320] = win_e[:, 2312:2320], win_e[:, 2304:2312]
            wo = win_o.copy()
            wo[:, 512:1024], wo[:, 1024:1536] = win_o[:, 1024:1536], win_o[:, 512:1024]
            v["win_e"], v["win_o"] = we, wo
            v["gbias"] = np.ascontiguousarray(gb[[2, 3, 0, 1]].T)
        variants.append(v)

    def dupk(kc):
        return np.ascontiguousarray(np.concatenate([kc[:, 0], kc[:, 0], kc[:, 1], kc[:, 1]], 1))

    for i in range(8):
        b = i // 2
        mir = (i % 2 == 1)
        m = dict(variants[1 if mir else 0])
        xs = f(inp['x_sample'][b])
        xp = f(inp['x_prompt'][2 * i:2 * i + 2])
        stC, stn, stm, stS = (f(inp['state_b_C'][b, 0]), f(inp['state_b_n'][b, 0]), f(inp['state_b_m'][b, 0]),
                              f(inp['state_c_S'][b, 0]))
        if mir:
            xs = np.ascontiguousarray(xs[::-1])
            xp = np.ascontiguousarray(xp[:, ::-1])
            stC, stn, stm, stS = (np.ascontiguousarray(a_[::-1]) for a_ in (stC, stn, stm, stS))
        m.update({
            "xs": xs,
            "xp": xp.reshape(NPS * LP, D),
            "cvec": np.concatenate([f(inp['c'][b]).reshape(8, 128), f(inp['c_ctx']).reshape(8, 128)], 0),
            "kctxa": dupk(f(inp['cache_a_k'][b, 0])), "vctxa": f(inp['cache_a_v'][b, 0]).reshape(512, 128),
            "kctxd": dupk(f(inp['cache_d_k'][b, 0])), "vctxd": f(inp['cache_d_v'][b, 0]).reshape(512, 128),
            "stC": stC, "stn": stn, "stm": stm, "stS": stS,
        })
        maps.append(m)
    return maps


def gather(r):
    H2 = LS // 2
    ys = np.stack([np.concatenate([r[2 * b]["ys"], r[2 * b + 1]["ys"][::-1]], 0) for b in range(4)], 0)

    def per_core(nm, tok_axis=None, dir_axis=None):
        outs = []
        for i in range(8):
            a = r[i][nm]
            if i % 2 == 1:
                if tok_axis is not None:
                    a = np.flip(a, axis=tok_axis)
                if dir_axis is not None:
                    a = np.flip(a, axis=dir_axis)
            outs.append(a)
        return np.concatenate(outs, 0)

    yp = np.concatenate([(r[i]["yp"].reshape(NPS, LP, D)[:, ::-1] if i % 2 else r[i]["yp"].reshape(NPS, LP, D))
                         for i in range(8)], 0)
    ak = per_core("ak_o", tok_axis=1).reshape(16, 1, LP, 2, 64)
    av = per_core("av_o", tok_axis=1).reshape(16, 1, LP, 2, 64)
    bC = per_core("bC_o", dir_axis=1).reshape(16, 1, 2, 4, 128, 128)
    bn = per_core("bn_o", dir_axis=1).reshape(16, 1, 2, 4, 128)
    bm = per_core("bm_o", dir_axis=1).reshape(16, 1, 2, 4)
    cS = per_core("cS_o", dir_axis=1).reshape(16, 1, 2, 4, 128, 128)
    dk = per_core("dk_o", tok_axis=1).reshape(16, 1, LP, 2, 64)
    dv = per_core("dv_o", tok_axis=1).reshape(16, 1, LP, 2, 64)
    return (np.ascontiguousarray(yp), ys, np.ascontiguousarray(ak), np.ascontiguousarray(av), np.ascontiguousarray(bC),
            np.ascontiguousarray(bn), np.ascontiguousarray(bm), np.ascontiguousarray(cS), np.ascontiguousarray(dk),
            np.ascontiguousarray(dv))


def kernel(**inputs):
    nc = build()
    in_maps = make_in_maps(inputs)
    res = run_bass_kernel_spmd(nc, in_maps, core_ids=list(range(8)))
    return gather(res.results)
```

```python
import numpy as np
import concourse.bass as bass
import concourse.mybir as mybir
from concourse.bass_utils import run_bass_kernel_spmd

F32 = mybir.dt.float32
BF16 = mybir.dt.bfloat16
AF = mybir.ActivationFunctionType
ALU = mybir.AluOpType

D = 1024
NC_ = 8
DFF = 2816
NFC = 22
LS = 2048
LP = 256
NPS = 2
TT = 512
NTT = 5
ALPHA = 4 ** 0.25
LN_EPS = 1e-5 / (ALPHA * ALPHA)
PH = 30000
DBG = {}
PHASE_MARKS = []


class Buf:
    UID = 0

    def __init__(self, h, name, space):
        self.h = h
        self.name = name
        self.space = space
        self.last_w = None
        self.readers = []
        self.dsem = None
        self.dcnt = 0
        Buf.UID += 1
        self.uid = Buf.UID

    def __getitem__(self, idx):
        return self.h[idx]


class KB:
    def __init__(self, nc):
        self.nc = nc
        self.eng = {'pe': nc.tensor, 'act': nc.scalar, 'dve': nc.vector, 'pool': nc.gpsimd, 'sp': nc.sync}
        self.cnt = {e: 0 for e in self.eng}
        self.sems = {e: [] for e in self.eng}
        self.waited = {e: {} for e in self.eng}
        self.nbuf = 0
        self.dma_bufs = []
        self.guards = []
        self.gbufs = []
        self.free_dsems = []
        self.nsem = 0

    def sb(self, name, shape, dt):
        self.nbuf += 1
        g = self.nc.sbuf_tensor('sb_' + name + f'_{self.nbuf}', list(shape), dt)
        h = g.__enter__()
        self.guards.append(g)
        b = Buf(h, name, 'sb')
        self.gbufs.append(b)
        return b

    def mark(self):
        return len(self.guards)

    def release(self, mark):
        self.barrier()
        while len(self.guards) > mark:
            self.guards.pop().__exit__(None, None, None)
            b = self.gbufs.pop()
            if b.dsem is not None:
                self.free_dsems.append((b.dsem, b.dcnt))
                self.dma_bufs.remove(b)
                b.dsem = None

    def ps(self, name, shape, dt=F32):
        return Buf(self.nc.alloc_psum_tensor(name, list(shape), dt), name, 'ps')

    def dram(self, name, shape, dt, kind):
        return Buf(self.nc.dram_tensor(name, list(shape), dt, kind=kind).ap(), name, 'dram')

    def _sem(self, e, phase):
        while len(self.sems[e]) <= phase:
            self.sems[e].append(self.nc.alloc_semaphore(name=f"s_{e}_{len(self.sems[e])}"))
        return self.sems[e][phase]

    def _wait(self, e, ev):
        if ev is None:
            return
        if ev[0] == 'eng':
            _, e2, seq = ev
            if e2 == e and e == 'pe':
                return
            key = ('eng', e2)
            if self.waited[e].get(key, 0) >= seq:
                return
            self.waited[e][key] = seq
            ph, val = (seq - 1) // PH, (seq - 1) % PH + 1
            self.eng[e].wait_ge(self._sem(e2, ph), val)
        else:
            owner = ev[1]
            key = ('dma', owner.uid)
            need = owner.dcnt
            if self.waited[e].get(key, 0) >= need:
                return
            self.waited[e][key] = need
            self.eng[e].wait_ge(owner.dsem, need * 16)

    def _sync(self, e, reads, writes):
        for r in reads:
            self._wait(e, r.last_w)
            if r.space == 'ps':
                for ev in r.readers:
                    if not (ev[0] == 'eng' and ev[1] == e):
                        self._wait(e, ev)
        for w in writes:
            self._wait(e, w.last_w)
            for ev in w.readers:
                if ev[0] == 'eng' and ev[1] == e:
                    continue
                self._wait(e, ev)

    def _record(self, ev, reads, writes):
        for r in reads:
            if r in writes:
                continue
            if ev[0] == 'eng':
                r.readers = [x for x in r.readers if not (x[0] == 'eng' and x[1] == ev[1])]
            else:
                r.readers = [x for x in r.readers if not (x[0] == 'dma' and x[1] is ev[1])]
            r.readers.append(ev)
        for w in writes:
            w.last_w = ev
            w.readers = []

    def op(self, e, fn, reads, writes):
        self._sync(e, reads, writes)
        inst = fn()
        self.cnt[e] += 1
        seq = self.cnt[e]
        inst.then_inc(self._sem(e, (seq - 1) // PH), 1)
        self._record(('eng', e, seq), reads, writes)
        return inst

    def dma(self, q, out_ap, in_ap, dst, src, owner=None):
        if owner is None:
            owner = dst if dst.space == 'sb' else src
        if owner.dsem is None:
            if self.free_dsems:
                owner.dsem, owner.dcnt = self.free_dsems.pop()
            else:
                self.nsem += 1
                owner.dsem = self.nc.alloc_semaphore(name=f"d_{self.nsem}")
            self.dma_bufs.append(owner)
        self._sync(q, [src], [dst])
        inst = self.eng[q].dma_start(out=out_ap, in_=in_ap)
        owner.dcnt += 1
        inst.then_inc(owner.dsem, 16)
        self._record(('dma', owner), [src], [dst])

    def barrier(self):
        for e in self.eng:
            for e2 in self.eng:
                if e2 != e and self.cnt[e2] > 0:
                    self._wait(e, ('eng', e2, self.cnt[e2]))
            for b in self.dma_bufs:
                self._wait(e, ('dma', b))

    def mm(self, out, out_ap, lhsT, lhsT_ap, rhs, rhs_ap, start=True, stop=True):
        rd = [lhsT, rhs]
        return self.op('pe', lambda: self.nc.tensor.matmul(out_ap, lhsT=lhsT_ap, rhs=rhs_ap, start=start,
                                                          stop=stop), rd, [out])

    def tr(self, out, out_ap, in_, in_ap, ident, ident_ap):
        return self.op('pe', lambda: self.nc.tensor.transpose(out_ap, in_ap, ident_ap), [in_, ident], [out])

    def act(self, out, out_ap, in_, in_ap, func, bias=None, scale=None, extra=()):
        kw = {}
        if bias is not None:
            kw['bias'] = bias
        if scale is not None:
            kw['scale'] = scale
        return self.op('act', lambda: self.nc.scalar.activation(out=out_ap, in_=in_ap, func=func, **kw),
                       [in_] + list(extra), [out])

    def tt(self, e, out, out_ap, a, a_ap, b, b_ap, op):
        eng = self.eng[e]
        return self.op(e, lambda: eng.tensor_tensor(out=out_ap, in0=a_ap, in1=b_ap, op=op), [a, b], [out])

    def ts(self, e, out, out_ap, a, a_ap, s1, s2, op0, op1=None, extra=()):
        eng = self.eng[e]
        if op1 is None:
            f = lambda: eng.tensor_scalar(out=out_ap, in0=a_ap, scalar1=s1, scalar2=None, op0=op0)
        else:
            f = lambda: eng.tensor_scalar(out=out_ap, in0=a_ap, scalar1=s1, scalar2=s2, op0=op0, op1=op1)
        return self.op(e, f, [a] + list(extra), [out])

    def stt(self, out, out_ap, a, a_ap, scalar, b, b_ap, op0, op1, extra=()):
        return self.op('dve', lambda: self.nc.vector.scalar_tensor_tensor(out=out_ap, in0=a_ap, scalar=scalar,
                                                                         in1=b_ap, op0=op0, op1=op1),
                       [a, b] + list(extra), [out])

    def copy(self, e, out, out_ap, in_, in_ap):
        if e == 'act':
            return self.act(out, out_ap, in_, in_ap, AF.Copy)
        eng = self.eng[e]
        return self.op(e, lambda: eng.tensor_copy(out=out_ap, in_=in_ap), [in_], [out])

    def memset(self, e, out, out_ap, val):
        eng = self.eng[e]
        return self.op(e, lambda: eng.memset(out_ap, val), [], [out])


class RR:
    def __init__(self, engs):
        self.engs = engs
        self.i = 0

    def __call__(self):
        e = self.engs[self.i % len(self.engs)]
        self.i += 1
        return e


def build(debug_stage=None):
    nc = bass.Bass("TRN2", target_bir_lowering=False)
    k = KB(nc)
    xs_d = k.dram("xs", [LS, D], F32, "ExternalInput")
    xp_d = k.dram("xp", [NPS * LP, D], F32, "ExternalInput")
    cv_d = k.dram("cvec", [2 * NC_, 128], F32, "ExternalInput")
    adaw_d = k.dram("ada_w", [2, D, 9 * D], F32, "ExternalInput")
    adab_d = k.dram("ada_b", [2 * 72, 128], F32, "ExternalInput")
    lnp_d = k.dram("lnp", [96, 128], F32, "ExternalInput")
    w1_d = k.dram("ffn_w1", [4, D, DFF], F32, "ExternalInput")
    w3_d = k.dram("ffn_w3", [4, D, DFF], F32, "ExternalInput")
    w2_d = k.dram("ffn_w2", [4, DFF, D], F32, "ExternalInput")
    ident_d = k.dram("ident", [128, 128], F32, "ExternalInput")
    wine_d = k.dram("win_e", [D, 2832], F32, "ExternalInput")
    woute_d = k.dram("wout_e", [D, D], F32, "ExternalInput")
    wino_d = k.dram("win_o", [D, 3328], F32, "ExternalInput")
    wouto_d = k.dram("wout_o", [D, D], F32, "ExternalInput")
    kctxa_d = k.dram("kctxa", [512, 256], F32, "ExternalInput")
    vctxa_d = k.dram("vctxa", [512, 128], F32, "ExternalInput")
    kctxd_d = k.dram("kctxd", [512, 256], F32, "ExternalInput")
    vctxd_d = k.dram("vctxd", [512, 128], F32, "ExternalInput")
    stC_d = k.dram("stC", [2, 4, 128, 128], F32, "ExternalInput")
    stn_d = k.dram("stn", [2, 4, 128], F32, "ExternalInput")
    stm_d = k.dram("stm", [2, 4], F32, "ExternalInput")
    stS_d = k.dram("stS", [2, 4, 128, 128], F32, "ExternalInput")
    gbias_d = k.dram("gbias", [4, 4], F32, "ExternalInput")
    smallc_d = k.dram("smallc", [128, 32], F32, "ExternalInput")
    kgrow_d = k.dram("kgrow", [128, 128], F32, "ExternalInput")
    cos_d = k.dram("cosT", [128, LS], F32, "ExternalInput")
    sin_d = k.dram("sinT", [128, LS], F32, "ExternalInput")
    RT_d = k.dram("RT", [128, 128], F32, "ExternalInput")
    negm_d = k.dram("negm", [128, 256], F32, "ExternalInput")
    shiftT_d = k.dram("shiftT", [128, 64], F32, "ExternalInput")
    sel4_d = k.dram("sel4", [4, 512], F32, "ExternalInput")
    mskb_d = k.dram("mskb", [128, 1408], BF16, "ExternalInput")
    ys_d = k.dram("ys", [LS // 2, D], F32, "ExternalOutput")
    ak_o = k.dram("ak_o", [NPS, LP, 128], F32, "ExternalOutput")
    av_o = k.dram("av_o", [NPS, LP, 128], F32, "ExternalOutput")
    dk_o = k.dram("dk_o", [NPS, LP, 128], F32, "ExternalOutput")
    dv_o = k.dram("dv_o", [NPS, LP, 128], F32, "ExternalOutput")
    bC_o = k.dram("bC_o", [NPS, 2, 4, 128, 128], F32, "ExternalOutput")
    bn_o = k.dram("bn_o", [NPS, 2, 4, 128], F32, "ExternalOutput")
    bm_o = k.dram("bm_o", [NPS, 2, 4], F32, "ExternalOutput")
    cS_o = k.dram("cS_o", [NPS, 2, 4, 128, 128], F32, "ExternalOutput")
    wine_s = k.dram("wine_s", [D, 2832], BF16, "Internal")
    woute_s = k.dram("woute_s", [D, D], BF16, "Internal")
    wino_s = k.dram("wino_s", [D, 3328], BF16, "Internal")
    wouto_s = k.dram("wouto_s", [D, D], BF16, "Internal")
    SL = [LS, LP, LP]
    ya_s = [k.dram(f"ya_s{i}", [64, 8, SL[i]], BF16, "Internal") for i in range(3)]
    yb_s = [k.dram(f"yb_s{i}", [128, 4, SL[i]], BF16, "Internal") for i in range(3)]
    hf_s = [k.dram(f"hf_s{i}", [4, 128, SL[i]], F32, "Internal") for i in range(3)]
    of_s = hf_s
    yp_d = k.dram("yp", [NPS * LP, D], F32, "ExternalOutput")
    w1_s = k.dram("w1_s", [4, D, DFF], BF16, "Internal")
    w3_s = k.dram("w3_s", [4, D, DFF], BF16, "Internal")
    w2_s = k.dram("w2_s", [4, DFF, D], BF16, "Internal")

    X = [k.sb(f"X{t}", [128, NC_, TT], F32) for t in range(NTT)]
    H = [k.sb(f"H{t}", [128, NC_, TT], BF16) for t in range(NTT)]
    ident = k.sb("ident", [128, 128], F32)
    ones_bf = k.sb("ones_bf", [128, 128], BF16)
    modv = [k.sb(f"modv{l}", [128, 72, 2], F32) for l in range(2)]
    sc1 = [k.sb(f"sc1{l}", [128, 72, 2], F32) for l in range(2)]
    cf = [k.sb(f"cf{l}", [128, 72, 2], F32) for l in range(2)]
    lnp = k.sb("lnp", [128, 96], F32)
    PS = [k.ps(f"ps{i}", [128, 512]) for i in range(8)]

    k.dma('sp', ident[:, :], ident_d[:, :], ident, ident_d)
    RT = k.sb("RT", [128, 128], F32)
    k.dma('sp', RT[:, :], RT_d[:, :], RT, RT_d)
    shiftT = k.sb("shiftT", [128, 64], F32)
    k.dma('sp', shiftT[:, :], shiftT_d[:, :], shiftT, shiftT_d)
    sel4 = k.sb("sel4", [4, 4, 128], F32)
    k.dma('sp', sel4[:, :, :], sel4_d.h[:, :].rearrange("p (h t) -> p h t", h=4), sel4, sel4_d)

    class View:
        def __init__(self, buf, c0, c1):
            self.buf, self.c0, self.c1 = buf, c0, c1
    tri_hi4 = k.sb("tri_hi4", [128, 512], BF16)
    tri_lo4 = k.sb("tri_lo4", [128, 512], BF16)
    bd_hi = k.sb("bd_hi", [128, 128], BF16)
    bd_lo = k.sb("bd_lo", [128, 128], BF16)
    bones64 = k.sb("bones64", [128, 128], BF16)
    smallc = k.sb("smallc", [128, 32], F32)
    bng = k.sb("bng", [128, 4], F32)
    cng = k.sb("cng", [128, 4], F32)
    gvec = k.sb("gvec", [128, 2], F32)
    lbc = k.sb("lbc", [128, 4], F32)
    oml = k.sb("oml", [128, 4], F32)
    sinkexp = k.sb("sinkexp", [64, 2, 512], F32)
    sk8 = k.sb("sk8", [64, 8], F32)
    kgrow = k.sb("kgrow", [128, 128], F32)
    mk_c = k.mark()
    mskb = k.sb("mskb", [128, 1408], BF16)
    k.dma('sp', mskb[:, :], mskb_d[:, :], mskb, mskb_d)
    k.copy('pool', tri_hi4, tri_hi4[:, :], mskb, mskb[:, 0:512])
    k.copy('pool', tri_lo4, tri_lo4[:, :], mskb, mskb[:, 512:1024])
    k.copy('pool', bd_hi, bd_hi[:, :], mskb, mskb[:, 1024:1152])
    k.copy('pool', bd_lo, bd_lo[:, :], mskb, mskb[:, 1152:1280])
    k.copy('pool', bones64, bones64[:, :], mskb, mskb[:, 1280:1408])
    k.release(mk_c)
    k.dma('sp', smallc[:, :], smallc_d[:, :], smallc, smallc_d)
    k.copy('dve', bng, bng[:, :], smallc, smallc[:, 0:4])
    k.copy('dve', cng, cng[:, :], smallc, smallc[:, 4:8])
    k.copy('dve', gvec, gvec[:, :], smallc, smallc[:, 8:10])
    k.tt('dve', lbc, lbc[:, :], smallc, smallc[:, 14:18], smallc, smallc[:, 10:14], ALU.subtract)
    k.act(lbc, lbc[:, :], lbc, lbc[:, :], AF.Sigmoid)
    k.ts('dve', oml, oml[:, :], lbc, lbc[:, :], -1.0, 1.0, ALU.mult, ALU.add)
    k.act(sk8, sk8[:, :], smallc, smallc[0:64, 18:26], AF.Exp)
    for kv_ in range(2):
        for hp_ in range(2):
            for ab_ in range(2):
                g_ = ab_ * 2 + hp_
                c0_ = hp_ * 256 + ab_ * 128
                k.copy('dve', sinkexp, sinkexp[:, kv_, c0_:c0_ + 128], sk8,
                       sk8[:, kv_ * 4 + g_:kv_ * 4 + g_ + 1].broadcast_to([64, 128]))
    k.dma('sp', kgrow[:, :], kgrow_d[:, :], kgrow, kgrow_d)
    k.memset('pool', ones_bf, ones_bf[:, :], 1.0 / D)

    stage = k.sb("stage_small", [128, 128], F32)
    silu_c = k.sb("silu_c", [128, NC_, 2], F32)

    def load_cols(src_d, r0, nrows, dst, dst_ap, func=None):
        k.dma('sp', stage[0:nrows, :], src_d[r0:r0 + nrows, :], stage, src_d)
        k.tr(PS[0], PS[0][:, 0:nrows], stage, stage[0:nrows, :], ident, ident[0:nrows, 0:nrows])
        if func is None:
            k.copy('dve', dst, dst_ap, PS[0], PS[0][:, 0:nrows])
        else:
            k.act(dst, dst_ap, PS[0], PS[0][:, 0:nrows], func)

    for j in range(2):
        load_cols(cv_d, j * NC_, NC_, silu_c, silu_c[:, :, j], AF.Silu)
    load_cols(lnp_d, 0, 96, lnp, lnp[:, :])

    mk0 = k.mark()
    adab = k.sb("adab", [128, 144], F32)
    load_cols(adab_d, 0, 72, adab, adab[:, 0:72])
    load_cols(adab_d, 72, 72, adab, adab[:, 72:144])
    NADA = 2
    adat = [k.sb(f"adat{i}", [128, NC_, 512], F32) for i in range(NADA)]
    NST = 2
    st_f = [k.sb(f"stf{i}", [128, 2816], F32) for i in range(NST)]
    st_b = [k.sb(f"stb{i}", [128, 2816], BF16) for i in range(NST)]
    cast_rr = RR(['act', 'dve', 'pool'])
    pi = [0]
    prep_units = []

    def prep_rows(src_d, dst_d, src_ap, dst_ap, ncols):
        def f():
            i = pi[0] % NST
            pi[0] += 1
            k.dma('sp', st_f[i][:, 0:ncols], src_ap, st_f[i], src_d)
            k.copy(cast_rr(), st_b[i], st_b[i][:, 0:ncols], st_f[i], st_f[i][:, 0:ncols])
            k.dma('pool', dst_ap, st_b[i][:, 0:ncols], dst_d, st_b[i])
        prep_units.append(f)

    def prep_rows2(s, r):
        def f():
            i = pi[0] % NST
            pi[0] += 1
            src = w2_d.h[s, r * 256:(r + 1) * 256, :].rearrange("(a p) n -> p a n", p=128)
            dst = w2_s.h[s, r * 256:(r + 1) * 256, :].rearrange("(a p) n -> p a n", p=128)
            sf = st_f[i].h[:, 0:2048].rearrange("p (a n) -> p a n", a=2)
            sbb = st_b[i].h[:, 0:2048].rearrange("p (a n) -> p a n", a=2)
            k.dma('sp', sf, src, st_f[i], w2_d)
            k.copy(cast_rr(), st_b[i], st_b[i][:, 0:2048], st_f[i], st_f[i][:, 0:2048])
            k.dma('pool', dst, sbb, w2_s, st_b[i])
        prep_units.append(f)

    def prep_ffn(s):
        for kc in range(NC_):
            prep_rows(w1_d, w1_s, w1_d.h[s, kc * 128:(kc + 1) * 128, :], w1_s.h[s, kc * 128:(kc + 1) * 128, :], DFF)
            prep_rows(w3_d, w3_s, w3_d.h[s, kc * 128:(kc + 1) * 128, :], w3_s.h[s, kc * 128:(kc + 1) * 128, :], DFF)
        for r in range(11):
            prep_rows2(s, r)

    def prep_mat(src_d, dst_d, ncols):
        for kc in range(NC_):
            for c0 in range(0, ncols, 2048):
                n = min(2048, ncols - c0)
                prep_rows(src_d, dst_d, src_d.h[kc * 128:(kc + 1) * 128, c0:c0 + n],
                          dst_d.h[kc * 128:(kc + 1) * 128, c0:c0 + n], n)

    prep_ffn(0)
    per_blk = (len(prep_units) + 35) // 36
    pu = [0]

    def emit_prep(n):
        for _ in range(n):
            if pu[0] < len(prep_units):
                prep_units[pu[0]]()
                pu[0] += 1

    it = 0
    for l in range(2):
        for nb4 in range(18):
            at = adat[it % NADA]
            it += 1
            k.dma('sp', at[:, :, :], adaw_d.h[l].rearrange("(kc p) n -> p kc n", p=128)[:, :, nb4 * 512:(nb4 + 1) * 512],
                  at, adaw_d)
            pm = PS[1 + (nb4 % 2)]
            for q in range(4):
                for kc in range(NC_):
                    k.mm(pm, pm[:, q * 2:q * 2 + 2], at, at[:, kc, q * 128:(q + 1) * 128], silu_c, silu_c[:, kc, :],
                         start=(kc == 0), stop=(kc == NC_ - 1))
            for q in range(4):
                nb = nb4 * 4 + q
                k.ts('dve', modv[l], modv[l][:, nb, :], pm, pm[:, q * 2:q * 2 + 2],
                     adab[:, l * 72 + nb:l * 72 + nb + 1], None, ALU.add, extra=[adab])
            emit_prep(per_blk)
        k.ts('pool', sc1[l], sc1[l][:, :, :], modv[l], modv[l][:, :, :], 1.0, None, ALU.add)
        for j in range(3):
            cc = (1.0 if j == 1 else 0.5) / ALPHA
            g0 = (3 * j + 2) * 8
            k.ts('pool', cf[l], cf[l][:, g0:g0 + 8, :], modv[l], modv[l][:, g0:g0 + 8, :], cc, None, ALU.mult)
    emit_prep(len(prep_units))
    k.release(mk0)
    mk2 = k.mark()
    xin = [k.sb(f"xin{i}", [128, D], F32) for i in range(2)]
    ev_rr = RR(['dve', 'act'])

    def load_x(src_d, r0, Xt, t0, i):
        xi = xin[i % 2]
        k.dma('sp', xi[:, :], src_d[r0:r0 + 128, :], xi, src_d)
        for g in range(2):
            pt = PS[2 + (2 * i + g) % 4]
            for q in range(4):
                c = g * 4 + q
                k.tr(pt, pt[:, q * 128:(q + 1) * 128], xi, xi[:, c * 128:(c + 1) * 128], ident, ident[:, :])
            k.copy(ev_rr(), Xt, Xt.h[:, g * 4:(g + 1) * 4, t0:t0 + 128],
                   pt, pt.h[:, :].rearrange("p (q t) -> p q t", q=4))

    for t in range(NTT):
        for b in range(4):
            if t < 4:
                load_x(xs_d, t * TT + b * 128, X[t], b * 128, t * 4 + b)
            else:
                load_x(xp_d, b * 128, X[t], b * 128, t * 4 + b)

    k.release(mk2)
    mod_rr = RR(['act', 'pool'])

    def modulate(l, j, t):
        col = 0 if t < 4 else 1
        for c in range(NC_):
            s_ap = sc1[l][:, (3 * j + 1) * 8 + c, col:col + 1]
            b_ap = modv[l][:, (3 * j) * 8 + c, col:col + 1]
            e = mod_rr()
            if e == 'act':
                k.act(H[t], H[t][:, c, :], X[t], X[t][:, c, :], AF.Identity, bias=b_ap, scale=s_ap,
                      extra=[sc1[l], modv[l]])
            else:
                k.ts('pool', H[t], H[t][:, c, :], X[t], X[t][:, c, :], s_ap, b_ap, ALU.mult, ALU.add,
                     extra=[sc1[l], modv[l]])

    class NS:
        pass
    F = NS()

    def alloc_ln():
        F.zb = k.sb("zb", [128, 16, TT], BF16)
        F.mean_sb = k.sb("ln_mean", [128, TT], F32)
        F.rstd_sb = k.sb("ln_rstd", [128, TT], F32)
        F.tmp_sb = k.sb("ln_tmp", [128, TT], F32)
        F.lt = [k.sb(f"ln_t{i}", [128, TT], F32) for i in range(2)]

    def alloc_ffn():
        F.G = k.sb("G", [128, NFC, TT], BF16)
        F.zb = F.G
        F.mean_sb = k.sb("ln_mean", [128, TT], F32)
        F.rstd_sb = k.sb("ln_rstd", [128, TT], F32)
        F.tmp_sb = k.sb("ln_tmp", [128, TT], F32)
        F.w1r = [k.sb(f"w1r{i}", [128, NC_, 256], BF16) for i in range(NW13)]
        F.w3r = [k.sb(f"w3r{i}", [128, NC_, 256], BF16) for i in range(NW13)]
        F.w2r = [k.sb(f"w2r{i}", [128, 512], BF16) for i in range(NW2)]
        F.sg = [k.sb(f"sg{i}", [128, TT], F32) for i in range(2)]
        F.lt = F.sg
        F.bgf = [k.sb(f"bgf{i}", [128, BGW], F32) for i in range(2)]
        F.bgb = [k.sb(f"bgb{i}", [128, BGW], BF16) for i in range(2)]

    NW13 = 2
    NW2 = 6

    def layernorm(l, j, t):
        Xt = X[t]
        pm, pq = PS[0], PS[1]
        zb, zq = F.zb, F.zb
        mean_sb, rstd_sb, tmp_sb, lt = F.mean_sb, F.rstd_sb, F.tmp_sb, F.lt
        for c in range(NC_):
            k.copy('dve' if c % 2 == 0 else 'pool', zb, zb[:, c, :], Xt, Xt[:, c, :])
            k.act(zq, zq[:, 8 + c, :], Xt, Xt[:, c, :], AF.Square)
        for c in range(NC_):
            k.mm(pm, pm[:, :], ones_bf, ones_bf[:, :], zb, zb[:, c, :], start=(c == 0), stop=(c == NC_ - 1))
        for c in range(NC_):
            k.mm(pq, pq[:, :], ones_bf, ones_bf[:, :], zq, zq[:, 8 + c, :], start=(c == 0), stop=(c == NC_ - 1))
        k.copy('dve', mean_sb, mean_sb[:, :], pm, pm[:, :])
        k.tt('dve', tmp_sb, tmp_sb[:, :], mean_sb, mean_sb[:, :], mean_sb, mean_sb[:, :], ALU.mult)
        k.tt('dve', tmp_sb, tmp_sb[:, :], pq, pq[:, :], tmp_sb, tmp_sb[:, :], ALU.subtract)
        k.ts('dve', tmp_sb, tmp_sb[:, :], tmp_sb, tmp_sb[:, :], 0.0, LN_EPS, ALU.max, ALU.add)
        k.act(rstd_sb, rstd_sb[:, :], tmp_sb, tmp_sb[:, :], AF.Ln)
        k.act(rstd_sb, rstd_sb[:, :], rstd_sb, rstd_sb[:, :], AF.Exp, scale=-0.5)
        gi = (l * 3 + j) * 8
        for c in range(NC_):
            tb = lt[c % 2]
            e = 'pool' if c % 4 == 3 else 'dve'
            k.tt(e, tb, tb[:, :], Xt, Xt[:, c, :], mean_sb, mean_sb[:, :], ALU.subtract)
            k.tt(e, tb, tb[:, :], tb, tb[:, :], rstd_sb, rstd_sb[:, :], ALU.mult)
            k.act(Xt, Xt[:, c, :], tb, tb[:, :], AF.Identity, bias=lnp[:, 48 + gi + c:48 + gi + c + 1],
                  scale=lnp[:, gi + c:gi + c + 1], extra=[lnp])

    BGW = 1416
    BG = NS()
    BG.q = []
    BG.pos = 0
    BG.tick = 0
    BG.stride = 1
    BG.i = 0

    def bg_add(src_d, dst_d, src2d, dst2d, R, C):
        npc = (C + BGW - 1) // BGW
        w = (C + npc - 1) // npc
        for rc in range(R // 128):
            for pc in range(npc):
                c0 = pc * w
                n = min(w, C - c0)
                BG.q.append((src_d, dst_d, src2d[rc * 128:(rc + 1) * 128, c0:c0 + n],
                             dst2d[rc * 128:(rc + 1) * 128, c0:c0 + n], n))

    def bg_add_ffn(si):
        bg_add(w1_d, w1_s, w1_d.h[si], w1_s.h[si], D, DFF)
        bg_add(w3_d, w3_s, w3_d.h[si], w3_s.h[si], D, DFF)
        bg_add(w2_d, w2_s, w2_d.h[si], w2_s.h[si], DFF, D)

    def bg_load(u):
        src_d, dst_d, sap, dap, n = BG.q[u]
        i = u % 2
        k.dma('pool', F.bgf[i][:, 0:n], sap, F.bgf[i], src_d)

    def bg_cast_store(u):
        src_d, dst_d, sap, dap, n = BG.q[u]
        i = u % 2
        k.copy('pool', F.bgb[i], F.bgb[i][:, 0:n], F.bgf[i], F.bgf[i][:, 0:n])
        k.dma('pool', dap, F.bgb[i][:, 0:n], dst_d, F.bgb[i])

    def bg_start():
        n = len(BG.q) - BG.pos
        BG.stride = max(1, 300 // max(n, 1))
        BG.tick = 0
        BG.loaded = BG.pos - 1

    def bg_step():
        u = BG.pos
        if BG.loaded < u:
            bg_load(u)
            BG.loaded = u
        if u + 1 < len(BG.q) and BG.loaded < u + 1:
            bg_load(u + 1)
            BG.loaded = u + 1
        bg_cast_store(u)
        BG.pos += 1

    def bg_tick():
        BG.tick += 1
        if BG.tick % BG.stride == 0 and BG.pos < len(BG.q):
            bg_step()

    def bg_flush():
        while BG.pos < len(BG.q):
            bg_step()

    wc = {'a': 0, 'b': 0}

    def ffn(l, j, t):
        s = l * 2 + (0 if j == 0 else 1)
        col = 0 if t < 4 else 1
        Ht = H[t]
        G, w1r, w3r, w2r, sg = F.G, F.w1r, F.w3r, F.w2r, F.sg
        for fc in range(NFC):
            if fc % 2 == 0:
                i = wc['a'] % NW13
                wc['a'] += 1
                k.dma('sp', w1r[i][:, :, :],
                      w1_s.h[s].rearrange("(kc p) n -> p kc n", p=128)[:, :, fc * 128:(fc + 2) * 128], w1r[i], w1_s)
                k.dma('sp', w3r[i][:, :, :],
                      w3_s.h[s].rearrange("(kc p) n -> p kc n", p=128)[:, :, fc * 128:(fc + 2) * 128], w3r[i], w3_s)
            wo = (fc % 2) * 128
            pa, pb = PS[(fc % 2) * 2], PS[(fc % 2) * 2 + 1]
            for kc in range(NC_):
                k.mm(pa, pa[:, :], w1r[i], w1r[i][:, kc, wo:wo + 128], Ht, Ht[:, kc, :], start=(kc == 0), stop=(kc == NC_ - 1))
            for kc in range(NC_):
                k.mm(pb, pb[:, :], w3r[i], w3r[i][:, kc, wo:wo + 128], Ht, Ht[:, kc, :], start=(kc == 0), stop=(kc == NC_ - 1))
            sgi = sg[fc % 2]
            k.act(sgi, sgi[:, :], pa, pa[:, :], AF.Silu)
            k.tt('dve', G, G[:, fc, :], sgi, sgi[:, :], pb, pb[:, :], ALU.mult)
            bg_tick()
        for dg in range(2):
            for fc in range(NFC):
                i = wc['b'] % NW2
                wc['b'] += 1
                k.dma('sp', w2r[i][:, :], w2_s.h[s, fc * 128:(fc + 1) * 128, dg * 512:(dg + 1) * 512], w2r[i], w2_s)
                for q in range(4):
                    py = PS[4 + q]
                    k.mm(py, py[:, :], w2r[i], w2r[i][:, q * 128:(q + 1) * 128], G, G[:, fc, :],
                         start=(fc == 0), stop=(fc == NFC - 1))
                bg_tick()
            for q in range(4):
                c = dg * 4 + q
                py = PS[4 + q]
                k.stt(X[t], X[t][:, c, :], py, py[:, :], cf[l][:, (3 * j + 2) * 8 + c, col:col + 1],
                      X[t], X[t][:, c, :], ALU.mult, ALU.add, extra=[cf[l]])
        layernorm(l, j, t)

    def hs(seq, kc, i0, n):
        if seq == 0:
            t, c0 = i0 // TT, i0 % TT
        else:
            t, c0 = 4, (seq - 1) * LP + i0
        return H[t], H[t][:, kc, c0:c0 + n]

    M = NS()
    NWM = 4
    wmc = [0]

    def alloc_wm(n=3):
        M.wm = [k.sb(f"wm{i}", [128, NC_, 256], BF16) for i in range(n)]

    def next_wm():
        w = M.wm[wmc[0] % len(M.wm)]
        wmc[0] += 1
        return w

    def wload(wt, dcol, scr, col0, n):
        k.dma('sp', wt[:, :, dcol:dcol + n], scr.h.rearrange("(kc p) n -> p kc n", p=128)[:, :, col0:col0 + n], wt, scr)

    def proj_fm(seq, i0, n, wt, wc0, Mo, ps, ps_ap):
        for kc in range(NC_):
            Hb, hap = hs(seq, kc, i0, n)
            k.mm(ps, ps_ap, wt, wt[:, kc, wc0:wc0 + Mo], Hb, hap, start=(kc == 0), stop=(kc == NC_ - 1))

    def proj_tm(seq, i0, wt, wc0, ncols, ps, ps_ap):
        for kc in range(NC_):
            Hb, hap = hs(seq, kc, i0, 128)
            k.mm(ps, ps_ap, Hb, hap, wt, wt[:, kc, wc0:wc0 + ncols], start=(kc == 0), stop=(kc == NC_ - 1))

    def rstd_from(dst, dst_ap, src, src_ap, scale, eps):
        k.act(dst, dst_ap, src, src_ap, AF.Ln, bias=float(eps), scale=float(scale))
        k.act(dst, dst_ap, dst, dst_ap, AF.Exp, scale=-0.5)

    def attention(seq, L, cfg):
        nt = L // 128
        ctx = cfg['ctx'] if seq == 0 else None
        rope = cfg['rope'] and seq == 0
        band = cfg['band'] and seq == 0
        qknorm = cfg['qknorm']
        sinkexp = cfg['sinkexp']
        win = cfg['win']
        nctx = 4 if ctx is not None else 0
        qhalf = bool(cfg.get('qhalf')) and seq == 0
        mk = k.mark()
        QT = k.sb("QT", [128, 4, L], BF16)
        KT = k.sb("KT", [128, 2, L + 128 * nctx], BF16)
        VA = k.sb("VA", [128, nt + nctx, 2, 128], BF16)
        k.memset('pool', VA, VA[:, :, :, 64:128], 1.0)
        mkA = k.mark()
        alloc_wm()
        step = min(L, 512)
        qf = [k.sb(f"qf{i}", [128, step], F32) for i in range(2)]
        t1 = [k.sb(f"t1{i}", [128, step], F32) for i in range(2)]
        if rope:
            ropeT = k.sb("ropeT", [128, 2, step], F32)
        if qknorm:
            sq = k.sb("sq", [128, step], BF16)
            rs = k.sb("rs", [128, step], F32)
        kvo = [k.sb(f"kvo{i}", [128, 256], F32) for i in range(2)]
        if qknorm:
            kt2 = k.sb("kt2", [128, 128], F32)
            kss = k.sb("kss", [128, 2], F32)
        bi = 0
        for i0 in range(0, L, step):
            n = step
            if rope:
                k.dma('sp', ropeT[:, 0, :], cos_d[:, i0:i0 + n], ropeT, cos_d)
                k.dma('sp', ropeT[:, 1, :], sin_d[:, i0:i0 + n], ropeT, sin_d)
            for blk in range(DBG.get('nblk', 6)):
                if blk < 4 and qhalf and i0 >= L // 2:
                    continue
                wt = next_wm()
                if blk < 4:
                    wload(wt, 0, win, cfg['qcol'] + blk * 128, 128)
                    dst, dst_ap = QT, QT[:, blk, i0:i0 + n]
                    gcol = cfg.get('qg')
                else:
                    kv = blk - 4
                    wload(wt, 0, win, cfg['kcol'] + kv * 64, 64)
                    wload(wt, 64, win, cfg['kcol'] + kv * 64, 64)
                    dst, dst_ap = KT, KT[:, kv, i0:i0 + n]
                    gcol = cfg.get('kg')
                pp = PS[blk % 2]
                proj_fm(seq, i0, n, wt, 0, 128, pp, pp[:, :n])
                qfi = qf[bi % 2]
                t1i = t1[bi % 2]
                bi += 1
                if qknorm:
                    k.act(sq, sq[:, :n], pp, pp[:, :n], AF.Square)
                    pn = PS[2]
                    k.mm(pn, pn[:, :n], bones64, bones64[:, :], sq, sq[:, :n])
                    rstd_from(rs, rs[:, :n], pn, pn[:, :n], 1.0, 1e-6)
                    k.stt(qfi, qfi[:, :n], pp, pp[:, :n], gcol, rs, rs[:, :n], ALU.mult, ALU.mult, extra=[gvec])
                    src, src_ap = qfi, qfi[:, :n]
                elif rope:
                    k.copy('act', qfi, qfi[:, :n], pp, pp[:, :n])
                    src, src_ap = qfi, qfi[:, :n]
                else:
                    src, src_ap = pp, pp[:, :n]
                if rope:
                    pr = PS[3]
                    k.mm(pr, pr[:, :n], RT, RT[:, :], src, src_ap)
                    k.tt('pool', t1i, t1i[:, :n], src, src_ap, ropeT, ropeT[:, 0, :n], ALU.mult)
                    k.tt('dve', src, src_ap, pr, pr[:, :n], ropeT, ropeT[:, 1, :n], ALU.mult)
                    k.tt('pool', dst, dst_ap, t1i, t1i[:, :n], src, src_ap, ALU.add)
                else:
                    if src.space == 'ps':
                        k.copy('act', dst, dst_ap, src, src_ap)
                    else:
                        k.copy('pool', dst, dst_ap, src, src_ap)
            wt = next_wm()
            wload(wt, 0, win, cfg['kcol'], 256)
            for b in range(0 if DBG.get('notm') else n // 128):
                pk = PS[4 + b % 2]
                proj_tm(seq, i0 + b * 128, wt, 0, 256, pk, pk[:, 0:256])
                tile = i0 // 128 + b
                if not DBG.get('nova'):
                    k.copy('dve', VA, VA.h[:, tile, :, 0:64], pk, pk.h[:, 128:256].rearrange("p (kv d) -> p kv d", kv=2))
                if seq > 0 and not DBG.get('noko'):
                    ko = kvo[b % 2]
                    k.copy('dve' if DBG.get('kodve') else 'act', ko, ko[:, :], pk, pk[:, 0:256])
                    if qknorm:
                        k.tt('dve', kt2, kt2[:, :], ko, ko[:, 0:128], ko, ko[:, 0:128], ALU.mult)
                        k.op('dve', lambda: nc.vector.tensor_reduce(out=kss[:, :], in_=kt2.h[:, :].rearrange("p (a d) -> p a d", a=2),
                                                                    op=ALU.add, axis=mybir.AxisListType.X), [kt2], [kss])
                        rstd_from(kss, kss[:, :], kss, kss[:, :], 1.0 / 64, 1e-6)
                        k.tt('dve', kt2, kt2.h[:, :].rearrange("p (a d) -> p a d", a=2),
                             ko, ko.h[:, 0:128].rearrange("p (a d) -> p a d", a=2),
                             kss, kss.h[:, :].unsqueeze(2).broadcast_to([128, 2, 64]), ALU.mult)
                        k.tt('dve', ko, ko[:, 0:128], kt2, kt2[:, :], kgrow, kgrow[:, :], ALU.mult)
                    r0 = i0 + b * 128
                    k.dma('pool', cfg['ko'][seq - 1, r0:r0 + 128, :], ko[:, 0:128], cfg['ko'], ko)
                    k.dma('pool', cfg['vo'][seq - 1, r0:r0 + 128, :], ko[:, 128:256], cfg['vo'], ko)
        if ctx is not None:
            kd, vd = ctx
            kcs = k.sb("kcs", [128, 4, 256], F32)
            vcs = k.sb("vcs", [128, 4, 128], F32)
            k.dma('sp', kcs[:, :, :], kd.h.rearrange("(t p) c -> p t c", p=128), kcs, kd)
            k.dma('sp', vcs[:, :, :], vd.h.rearrange("(t p) c -> p t c", p=128), vcs, vd)
            for tl in range(4):
                for kv in range(2):
                    pt = PS[4 + kv]
                    k.tr(pt, pt[:, 0:128], kcs, kcs[:, tl, kv * 128:(kv + 1) * 128], ident, ident[:, :])
                    k.copy('act', KT, KT[:, kv, L + tl * 128:L + (tl + 1) * 128], pt, pt[:, 0:128])
            k.copy('dve', VA, VA.h[:, nt:nt + 4, :, 0:64], vcs, vcs.h[:, :, :].rearrange("p t (kv d) -> p t kv d", kv=2))
        k.release(mkA)
        if DBG.get('attnA'):
            k.release(mk)
            return
        pT = [k.sb(f"pT{i}", [128, 512], BF16) for i in range(3)]
        osb = [k.sb(f"osb{i}", [128, 512], F32) for i in range(2)]
        rden = k.sb("rden", [64, 512], F32)
        yat = [k.sb(f"yat{i}", [64, 512], BF16) for i in range(2)]
        ya_s = cfg['ya_s'][seq]
        units = []
        steps = []
        for qb in range(nt // 2 if qhalf else nt):
            for kv in range(2):
                if band:
                    tiles = [(kt, (None if kt == qb else ('lo' if kt < qb else 'hi')))
                             for kt in (qb - 1, qb, qb + 1) if 0 <= kt < nt]
                else:
                    tiles = [(kt, None) for kt in range(nt)]
                tiles += [(nt + c, None) for c in range(nctx)]
                u = len(units)
                units.append((qb, kv))
                for ti, (kt, msk) in enumerate(tiles):
                    steps.append((u, kt, msk, ti == 0, ti == len(tiles) - 1))

        def emit_qk(si):
            u, kt, msk, first, last = steps[si]
            qb, kv = units[u]
            p = pT[si % 3]
            for hp in range(2):
                ps_s = PS[4 + 2 * (si % 2) + hp]
                k.mm(ps_s, ps_s.h[:, 0:256].rearrange("p (a q) -> p a q", a=2),
                     KT, KT[hp * 64:(hp + 1) * 64, kv, kt * 128:(kt + 1) * 128],
                     QT, QT[hp * 64:(hp + 1) * 64, kv * 2:kv * 2 + 2, qb * 128:(qb + 1) * 128])
                k.act(p, p[:, hp * 256:(hp + 1) * 256], ps_s, ps_s[:, 0:256], AF.Exp, scale=0.125)
            if msk is not None:
                mt_ = tri_lo4 if msk == 'lo' else tri_hi4
                k.tt('pool', p, p[:, :], p, p[:, :], mt_, mt_[:, :], ALU.mult)
            return p

        def emit_pv(si, p):
            u, kt, msk, first, last = steps[si]
            qb, kv = units[u]
            po = PS[2 + u % 2]
            k.mm(po, po[:, :], VA, VA[:, kt, kv, :], p, p[:, :], start=first, stop=last)
            if not last:
                return
            ob = osb[u % 2]
            k.copy('act', ob, ob[:, :], po, po[:, :])
            pd = PS[1]
            k.mm(pd, pd[0:64, :], shiftT, shiftT[:, :], ob, ob[:, :])
            if sinkexp is not None:
                k.tt('dve', rden, rden[:, :], pd, pd[0:64, :], sinkexp, sinkexp[:, kv, :], ALU.add)
                k.op('dve', lambda: nc.vector.reciprocal(out=rden[:, :], in_=rden[:, :]), [rden], [rden])
            else:
                k.op('dve', lambda: nc.vector.reciprocal(out=rden[:, :], in_=pd[0:64, :]), [pd], [rden])
            ya = yat[u % 2]
            k.tt('dve', ya, ya[:, :], ob, ob[0:64, :], rden, rden[:, :], ALU.mult)
            dstv = ya_s.h.rearrange("p (kv ab hp) l -> p kv hp ab l", kv=2, ab=2, hp=2)[:, kv, :, :, qb * 128:(qb + 1) * 128]
            srcv = ya.h[:, :].rearrange("p (hp ab q) -> p hp ab q", hp=2, ab=2)
            for hp in range(2):
                k.dma('pool', dstv[:, hp, :, :], srcv[:, hp, :, :], ya_s, ya)

        pcur = emit_qk(0)
        for si in range(len(steps)):
            pnext = emit_qk(si + 1) if si + 1 < len(steps) else None
            emit_pv(si, pcur)
            pcur = pnext
        k.release(mk)

    def mlstm(seq, L):
        SEG = min(L, 512)
        nseg = L // SEG
        ncs = SEG // 128
        win = wine_s
        mk = k.mark()
        alloc_wm(2)
        rw = {nm: k.sb("rw_" + nm, [4, SEG], F32) for nm in ('x', 'a', 'mn', 'B', 'r', 'M')}
        rw['one'] = k.sb("rw_one", [4, SEG], BF16)
        negm = k.sb("negm", [128, 256], F32)
        k.dma('sp', negm[:, :], negm_d[:, :], negm, negm_d)
        k.memset('pool', rw['one'], rw['one'][:, :], 1.0)
        rows3 = k.sb("rows3", [4, ncs, 3, 128], F32)
        rcol = k.sb("rcol", [128, ncs * 4], F32)
        carB = k.sb("carB", [4, 1], F32)
        carM = k.sb("carM", [4, 1], F32)
        Mprev = k.sb("Mprev", [4, ncs], F32)
        gb = k.sb("gb", [4, 4], F32)
        k.dma('sp', gb[:, :], gbias_d[:, :], gb, gbias_d)
        QTh = [k.sb(f"mQT{h}", [128, SEG], BF16) for h in range(4)]
        KTh = [k.sb(f"mKT{h}", [128, SEG], BF16) for h in range(4)]
        KVh = [k.sb(f"mKV{h}", [128, ncs, 257], BF16) for h in range(4)]
        for h in range(4):
            k.memset('pool', KVh[h], KVh[h][:, :, 256:257], 1.0)
        hseg = [k.sb(f"hseg{h}", [128, SEG], F32) for h in range(4)]
        Cst = [k.sb(f"Cst{h}", [128, 129], F32) for h in range(4)]
        nrep = [k.sb(f"nrep{h}", [128, 128], F32) for h in range(4)]
        onesf = k.sb("onesf", [128, 128], F32)
        k.memset('pool', onesf, onesf[:, :], 1.0)
        ones1b = k.sb("ones1b", [128, 128], BF16)
        k.memset('pool', ones1b, ones1b[:, :], 1.0)
        NB = 4
        Dt = [k.sb(f"Dt{i}", [128, 128], F32) for i in range(NB)]
        cols2 = [k.sb(f"cols2{i}", [128, 2], F32) for i in range(NB)]
        Wt = [k.sb(f"Wt{i}", [128, 128], BF16) for i in range(NB)]
        Qs = [k.sb(f"Qs{i}", [128, 128], F32) for i in range(NB)]
        dd = [k.sb(f"dd{i}", [128, 128], F32) for i in range(NB)]
        wsb = [k.sb(f"wsb{i}", [128, 1], F32) for i in range(NB)]
        Ks = [k.sb(f"Ks{i}", [128, 128], BF16) for i in range(NB)]
        hfl = k.sb("hfl", [128, SEG], F32)
        rsb = k.sb("rsb", [128, SEG], F32)
        sgb = k.sb("sgb", [128, SEG], F32)
        ybt = k.sb("ybt", [128, SEG], BF16)
        sqb = ybt
        ui = 0
        for d in range(2):
            fwd = (d == 0)
            for h in range(4):
                if seq == 0:
                    k.dma('sp', Cst[h][:, 0:128], stC_d[d, h, :, :], Cst[h], stC_d)
                    k.dma('sp', Cst[h][:, 128:129], stn_d.h[d, h, :].rearrange("(p o) -> p o", o=1), Cst[h], stn_d)
                else:
                    k.memset('pool', Cst[h], Cst[h][:, :], 0.0)
                k.ts('pool', nrep[h], nrep[h][:, :], onesf, onesf[:, :], Cst[h][:, 128:129], None, ALU.mult, extra=[Cst[h]])
            k.memset('pool', carB, carB[:, :], 0.0)
            if seq == 0:
                k.dma('sp', carM[:, :], stm_d.h[d, :].rearrange("(p o) -> p o", o=1), carM, stm_d)
            else:
                k.memset('pool', carM, carM[:, :], 0.0)
            segs = list(range(nseg)) if fwd else list(range(nseg - 1, -1, -1))
            for sg_ in segs:
                i0 = sg_ * SEG
                wt = next_wm()
                wload(wt, 0, win, 2304, 16)
                pgi, pgf = PS[0], PS[1]
                proj_fm(seq, i0, SEG, wt, (2 * d) * 4, 4, pgi, pgi[0:4, :SEG])
                proj_fm(seq, i0, SEG, wt, (2 * d + 1) * 4, 4, pgf, pgf[0:4, :SEG])
                x, a_, mn, B, r, Mx, one = (rw[nm] for nm in ('x', 'a', 'mn', 'B', 'r', 'M', 'one'))
                mt, tmp = a_, mn
                k.act(x, x[:, :], pgf, pgf[0:4, :SEG], AF.Identity, bias=gb[:, 2 * d + 1:2 * d + 2], extra=[gb])
                k.act(r, r[:, :], pgi, pgi[0:4, :SEG], AF.Identity, bias=gb[:, 2 * d:2 * d + 1], extra=[gb])
                k.act(a_, a_[:, :], x, x[:, :], AF.Abs)
                k.act(a_, a_[:, :], a_, a_[:, :], AF.Exp, scale=-1.0)
                k.act(a_, a_[:, :], a_, a_[:, :], AF.Ln, bias=1.0)
                k.ts('dve', mn, mn[:, :], x, x[:, :], -1.0, 0.0, ALU.mult, ALU.max)
                k.tt('dve', x, x[:, :], mn, mn[:, :], a_, a_[:, :], ALU.add)
                k.ts('dve', x, x[:, :], x, x[:, :], -1.0, None, ALU.mult)
                rv = (lambda t_: t_[:, :]) if fwd else (lambda t_: t_[:, ::-1])
                k.op('dve', lambda: nc.vector.tensor_tensor_scan(out=rv(B), data0=one[:, :], data1=rv(x), initial=carB[:, 0:1],
                                                                 op0=ALU.mult, op1=ALU.add), [one, x, carB], [B])
                k.tt('dve', r, r[:, :], r, r[:, :], B, B[:, :], ALU.subtract)
                k.op('dve', lambda: nc.vector.tensor_tensor_scan(out=rv(Mx), data0=one[:, :], data1=rv(r), initial=carM[:, 0:1],
                                                                 op0=ALU.mult, op1=ALU.max), [one, r, carM], [Mx])
                k.tt('dve', mt, mt[:, :], B, B[:, :], Mx, Mx[:, :], ALU.add)
                Mv = Mx.h[:, :].rearrange("p (c t) -> p c t", t=128)
                if fwd:
                    k.copy('dve', Mprev, Mprev[:, 0:1], carM, carM[:, 0:1])
                    if ncs > 1:
                        k.copy('dve', Mprev, Mprev[:, 1:ncs], Mx, Mv[:, 0:ncs - 1, 127])
                else:
                    k.copy('dve', Mprev, Mprev[:, ncs - 1:ncs], carM, carM[:, 0:1])
                    if ncs > 1:
                        k.copy('dve', Mprev, Mprev[:, 0:ncs - 1], Mx, Mv[:, 1:ncs, 0])
                last = SEG - 1 if fwd else 0
                k.copy('dve', carB, carB[:, :], B, B[:, last:last + 1])
                k.copy('dve', carM, carM[:, :], Mx, Mx[:, last:last + 1])
                k.ts('dve', rows3, rows3.h[:, :, 0, :], Mx, Mv, -1.0, None, ALU.mult)
                k.tt('dve', tmp, tmp.h[:, :].rearrange("p (c t) -> p c t", t=128), Mprev,
                     Mprev.h[:, :].unsqueeze(2).broadcast_to([4, ncs, 128]), Mx, Mv, ALU.subtract)
                k.act(rows3, rows3.h[:, :, 1, :], tmp, tmp.h[:, :].rearrange("p (c t) -> p c t", t=128), AF.Exp)
                k.act(rows3, rows3.h[:, :, 2, :], mt, mt.h[:, :].rearrange("p (c t) -> p c t", t=128), AF.Exp, scale=-1.0)
                prc = PS[2]
                for c in range(ncs):
                    k.tr(prc, prc[:, c * 4:(c + 1) * 4], r, r[0:4, c * 128:(c + 1) * 128], ident, ident[0:4, 0:4])
                k.copy('dve', rcol, rcol[:, :], prc, prc[:, 0:ncs * 4])
                if seq > 0 and sg_ == segs[-1]:
                    k.dma('pool', bm_o.h[seq - 1, d, :].rearrange("(p o) -> p o", o=1), mt[:, last:last + 1], bm_o, mt)
                for h in range(4):
                    wt = next_wm()
                    wload(wt, 0, win, 768 + h * 128, 128)
                    wload(wt, 128, win, 1280 + h * 128, 128)
                    pq, pk_ = PS[0], PS[1]
                    proj_fm(seq, i0, SEG, wt, 0, 128, pq, pq[:, :SEG])
                    k.copy('act', QTh[h], QTh[h][:, :], pq, pq[:, :SEG])
                    proj_fm(seq, i0, SEG, wt, 128, 128, pk_, pk_[:, :SEG])
                    k.act(KTh[h], KTh[h][:, :], pk_, pk_[:, :SEG], AF.Identity, scale=128 ** -0.5)
                    wt2 = next_wm()
                    wload(wt2, 0, win, 1280 + h * 128, 128)
                    wload(wt2, 128, win, 1792 + h * 128, 128)
                    for c in range(ncs):
                        pkv = PS[2 + c % 2]
                        proj_tm(seq, i0 + c * 128, wt2, 0, 256, pkv, pkv[:, 0:256])
                        k.act(KVh[h], KVh[h][:, c, 0:128], pkv, pkv[:, 0:128], AF.Identity, scale=128 ** -0.5)
                        k.copy('dve', KVh[h], KVh[h][:, c, 128:256], pkv, pkv[:, 128:256])
                chunks = list(range(ncs)) if fwd else list(range(ncs - 1, -1, -1))
                edge = 127 if fwd else 0
                msk = tri_hi4 if fwd else tri_lo4
                moff = 0 if fwd else 128
                for c in chunks:
                    cs = slice(c * 128, (c + 1) * 128)
                    bA = [PS[2 * h] for h in range(4)]
                    bB = [PS[2 * h + 1] for h in range(4)]
                    for h in range(4):
                        k.mm(bA[h], bA[h][:, 0:384], sel4, sel4[:, h, :], rows3, rows3.h[:, c, :, :].rearrange("p a t -> p (a t)"),
                             start=True, stop=False)
                        k.mm(bA[h], bA[h][:, 0:128], ident, ident[:, :], negm, negm[:, moff:moff + 128], start=False, stop=True)
                        k.mm(bA[h], bA[h][:, 384:512], KTh[h], KTh[h][:, cs], QTh[h], QTh[h][:, cs])
                    for h in range(4):
                        u = h
                        k.act(Dt[u], Dt[u][:, :], bA[h], bA[h][:, 0:128], AF.Exp, bias=rcol[:, c * 4 + h:c * 4 + h + 1], extra=[rcol])
                        k.copy('act', cols2[u], cols2[u][:, :], bA[h], bA[h][:, edge:edge + 129:128])
                        k.tt('dve', Qs[u], Qs[u][:, :], QTh[h], QTh[h][:, cs], bA[h], bA[h][:, 128:256], ALU.mult)
                        k.tt('dve', Wt[u], Wt[u][:, :], Dt[u], Dt[u][:, :], bA[h], bA[h][:, 384:512], ALU.mult)
                    for h in range(4):
                        u = h
                        k.mm(bB[h], bB[h][:, 0:128], Cst[h], Cst[h][:, 0:128], Qs[u], Qs[u][:, :], start=True, stop=False)
                        k.mm(bB[h], bB[h][:, 0:128], KVh[h], KVh[h][:, c, 128:256], Wt[u], Wt[u][:, :], start=False, stop=True)
                        k.mm(bB[h], bB[h][:, 128:256], nrep[h], nrep[h][:, :], Qs[u], Qs[u][:, :], start=True, stop=False)
                        k.mm(bB[h], bB[h][:, 128:256], ones1b, ones1b[:, :], Wt[u], Wt[u][:, :], start=False, stop=True)
                    for h in range(4):
                        u = h
                        k.act(dd[u], dd[u][:, :], bB[h], bB[h][:, 128:256], AF.Abs)
                        k.act(wsb[u], wsb[u][:, :], rcol, rcol[:, c * 4 + h:c * 4 + h + 1], AF.Exp, bias=cols2[u][:, 0:1],
                              extra=[cols2[u]])
                        k.tt('dve', dd[u], dd[u][:, :], dd[u], dd[u][:, :], bA[h], bA[h][:, 256:384], ALU.max)
                        k.op('dve', lambda: nc.vector.reciprocal(out=dd[u][:, :], in_=dd[u][:, :]), [dd[u]], [dd[u]])
                        k.tt('dve', hseg[h], hseg[h][:, cs], bB[h], bB[h][:, 0:128], dd[u], dd[u][:, :], ALU.mult)
                        k.act(Ks[u], Ks[u][:, :], KVh[h], KVh[h][:, c, 0:128], AF.Copy, scale=wsb[u][:, 0:1], extra=[wsb[u]])
                    for h in range(4):
                        u = h
                        k.mm(bB[h], bB[h][:, 256:385], Ks[u], Ks[u][:, :], KVh[h], KVh[h][:, c, 128:257])
                    for h in range(4):
                        u = h
                        k.stt(Cst[h], Cst[h][:, :], Cst[h], Cst[h][:, :], cols2[u][:, 1:2], bB[h], bB[h][:, 256:385], ALU.mult, ALU.add,
                              extra=[cols2[u]])
                        k.ts('pool', nrep[h], nrep[h][:, :], onesf, onesf[:, :], Cst[h][:, 128:129], None, ALU.mult,
                             extra=[Cst[h]])
                for h in range(4):
                    if fwd:
                        k.dma('pool', hf_s[seq][h, :, i0:i0 + SEG], hseg[h][:, :], hf_s[seq], hseg[h])
                    else:
                        k.dma('sp', hfl[:, :], hf_s[seq][h, :, i0:i0 + SEG], hfl, hf_s[seq])
                        k.tt('pool', hfl, hfl[:, :], hfl, hfl[:, :], hseg[h], hseg[h][:, :], ALU.add)
                        k.act(sqb, sqb[:, :], hfl, hfl[:, :], AF.Square)
                        pr = PS[0]
                        k.mm(pr, pr[:, :SEG], ones_bf, ones_bf[:, :], sqb, sqb[:, :])
                        rstd_from(rsb, rsb[:, :], pr, pr[:, :SEG], 8.0, 1e-6)
                        wt = next_wm()
                        wload(wt, 0, win, 2320 + h * 128, 128)
                        po_ = PS[1]
                        proj_fm(seq, i0, SEG, wt, 0, 128, po_, po_[:, :SEG])
                        k.act(sgb, sgb[:, :], po_, po_[:, :SEG], AF.Sigmoid)
                        k.stt(hfl, hfl[:, :], hfl, hfl[:, :], bng[:, h:h + 1], rsb, rsb[:, :], ALU.mult, ALU.mult, extra=[bng])
                        k.tt('pool', ybt, ybt[:, :], hfl, hfl[:, :], sgb, sgb[:, :], ALU.mult)
                        k.dma('pool', yb_s[seq][:, h, i0:i0 + SEG], ybt[:, :], yb_s[seq], ybt)
            if seq > 0:
                for h in range(4):
                    k.dma('pool', bC_o[seq - 1, d, h, :, :], Cst[h][:, 0:128], bC_o, Cst[h])
                    k.dma('pool', bn_o.h[seq - 1, d, h, :].rearrange("(p o) -> p o", o=1), Cst[h][:, 128:129], bn_o, Cst[h])
        k.release(mk)
    def hgrn(seq, L):
        SEG = min(L, 512)
        nseg = L // SEG
        ngr = SEG // 128
        nch = SEG // 32
        win = wino_s
        mk = k.mark()
        wA2 = [k.sb(f"gwA{i}", [128, NC_, 256], BF16) for i in range(2)]
        wB2 = [k.sb(f"gwB{i}", [128, NC_, 128], BF16) for i in range(2)]
        onesS = k.sb("onesS", [128, SEG], F32)
        k.memset('pool', onesS, onesS[:, :], 1.0)
        fT2 = [k.sb(f"fT{i}", [128, SEG], F32) for i in range(2)]
        kT2 = [k.sb(f"kT{i}", [128, SEG], F32) for i in range(2)]
        eT2 = [k.sb(f"eT{i}", [128, SEG], F32) for i in range(2)]
        Zh2 = [k.sb(f"gZ{i}", [128, SEG], F32) for i in range(2)]
        khf2 = [k.sb(f"gkh{i}", [128, SEG], F32) for i in range(2)]
        qT = [k.sb(f"gqT{h}", [128, SEG], F32) for h in range(4)]
        qb2 = [k.sb(f"gqb{i}", [128, SEG], BF16) for i in range(2)]
        Kmix = [k.sb(f"gKmix{i}", [128, 4, 128], BF16) for i in range(2)]
        tmpE = [k.sb(f"gtmpE{i}", [128, 128], F32) for i in range(2)]
        Kh = [k.sb(f"gKh{h}", [128, ngr, 128], BF16) for h in range(4)]
        Vt = [k.sb(f"gVt{h}", [128, ngr, 128], BF16) for h in range(4)]
        att = [k.sb(f"gatt{h}", [128, ngr, 128], BF16) for h in range(4)]
        oseg = [k.sb(f"goseg{h}", [128, SEG], F32) for h in range(4)]
        dec = [k.sb(f"gdec{h}", [128, ngr], F32) for h in range(4)]
        refc2 = [k.sb(f"grefc{i}", [128, nch], F32) for i in range(2)]
        S = [k.sb(f"gS{h}", [128, 128], F32) for h in range(4)]
        carZ = [k.sb(f"gcarZ{h}", [128, 1], F32) for h in range(4)]
        ofl, rsb, sgb = fT2[0], fT2[1], kT2[0]
        sqb, yct = qb2[0], qb2[1]
        v32 = lambda t_: t_.h[:, :].rearrange("p (c t) -> p c t", t=32)
        v128 = lambda t_: t_.h[:, :].rearrange("p (c t) -> p c t", t=128)
        for d in range(2):
            fwd = (d == 0)
            for i in range(2):
                k.memset('pool', Kmix[i], Kmix[i][:, :, :], 0.0)
            for h in range(4):
                if seq == 0:
                    k.dma('sp', S[h][:, :], stS_d[d, h, :, :], S[h], stS_d)
                else:
                    k.memset('pool', S[h], S[h][:, :], 0.0)
                k.memset('pool', carZ[h], carZ[h][:, :], 0.0)
            segs = list(range(nseg)) if fwd else list(range(nseg - 1, -1, -1))
            if seq == 0 and fwd:
                segs = segs[:nseg // 2]
            rv = (lambda t_: t_[:, :]) if fwd else (lambda t_: t_[:, ::-1])
            msk = tri_hi4 if fwd else tri_lo4
            for sg_ in segs:
                i0 = sg_ * SEG
                so = (seq == 0 and sg_ >= nseg // 2)
                def head_prep(h, par):
                    fT, kT, eT, Zh, khf, qb, refc = fT2[par], kT2[par], eT2[par], Zh2[par], khf2[par], qb2[par], refc2[par]
                    lfT = eT
                    PB = 4 * par
                    wt = wA2[par]
                    wload(wt, 0, win, h * 128, 128)
                    wload(wt, 128, win, 512 * (1 + d) + h * 128, 128)
                    pq, pf = PS[PB + 0], PS[PB + 1]
                    if not so:
                        proj_fm(seq, i0, SEG, wt, 0, 128, pq, pq[:, :SEG])
                        yield
                    proj_fm(seq, i0, SEG, wt, 128, 128, pf, pf[:, :SEG])
                    yield
                    if not so:
                        k.act(qT[h], qT[h][:, :], pq, pq[:, :SEG], AF.Silu)
                        yield
                    k.act(fT, fT[:, :], pf, pf[:, :SEG], AF.Sigmoid)
                    yield
                    k.ts('dve', fT, fT[:, :], fT, fT[:, :], oml[:, h:h + 1], lbc[:, h:h + 1], ALU.mult, ALU.add, extra=[oml, lbc])
                    yield
                    k.act(lfT, lfT[:, :], fT, fT[:, :], AF.Ln)
                    yield
                    k.ts('pool', kT, kT[:, :], fT, fT[:, :], -1.0, 1.0, ALU.mult, ALU.add)
                    yield
                    k.op('dve', lambda: nc.vector.tensor_tensor_scan(out=rv(Zh), data0=onesS[:, :], data1=rv(lfT),
                                                                     initial=carZ[h][:, 0:1], op0=ALU.mult, op1=ALU.add),
                         [onesS, lfT, carZ[h]], [Zh])
                    yield
                    Zv = v32(Zh)
                    Zg = v128(Zh)
                    if fwd:
                        k.copy('dve', refc, refc[:, 0:1], carZ[h], carZ[h][:, 0:1])
                        yield
                        k.copy('dve', refc, refc[:, 1:nch], Zh, Zv[:, 0:nch - 1, 31])
                        yield
                        refg_ap = refc[:, 0:nch:4]
                        edgeg_ap = Zg[:, :, 127]
                    else:
                        k.copy('dve', refc, refc[:, nch - 1:nch], carZ[h], carZ[h][:, 0:1])
                        yield
                        k.copy('dve', refc, refc[:, 0:nch - 1], Zh, Zv[:, 1:nch, 0])
                        yield
                        refg_ap = refc[:, 3:nch:4]
                        edgeg_ap = Zg[:, :, 0]
                    last = SEG - 1 if fwd else 0
                    k.copy('dve', carZ[h], carZ[h][:, :], Zh, Zh[:, last:last + 1])
                    yield
                    refb = refc.h[:, :].unsqueeze(2).broadcast_to([128, nch, 32])
                    refgb = refg_ap.unsqueeze(2).broadcast_to([128, ngr, 128])
                    edgegb = edgeg_ap.unsqueeze(2).broadcast_to([128, ngr, 128])
                    if not so:
                        k.tt('dve', eT, v32(eT), Zh, Zv, refc, refb, ALU.subtract)
                        yield
                        k.act(eT, eT[:, :], eT, eT[:, :], AF.Exp)
                        yield
                        k.tt('pool', qb, qb[:, :], qT[h], qT[h][:, :], eT, eT[:, :], ALU.mult)
                        yield
                        k.tt('dve', eT, v128(eT), Zh, Zg, refc, refgb, ALU.subtract)
                        yield
                        k.act(eT, eT[:, :], eT, eT[:, :], AF.Exp)
                        yield
                        k.tt('pool', qT[h], qT[h][:, :], qT[h], qT[h][:, :], eT, eT[:, :], ALU.mult)
                        yield
                    k.tt('dve', eT, v128(eT), Zh, edgegb, Zh, Zg, ALU.subtract)
                    yield
                    k.act(eT, eT[:, :], eT, eT[:, :], AF.Exp)
                    yield
                    k.tt('pool', khf, khf[:, :], kT, kT[:, :], eT, eT[:, :], ALU.mult)
                    yield
                    k.tt('dve', dec[h], dec[h][:, :], Zh, edgeg_ap, refc, refg_ap, ALU.subtract)
                    yield
                    k.act(dec[h], dec[h][:, :], dec[h], dec[h][:, :], AF.Exp)
                    yield
                    ptb = PS[PB + 1]
                    for g in range(ngr):
                        k.tr(ptb, ptb[:, g * 128:(g + 1) * 128], khf, khf[:, g * 128:(g + 1) * 128], ident, ident[:, :])
                        yield
                    k.copy('act', Kh[h], Kh[h].h[:, :, :], ptb, ptb.h[:, 0:ngr * 128].rearrange("p (g d) -> p g d", g=ngr))
                    yield
                    wt2 = wB2[par]
                    wload(wt2, 0, win, 1536 + h * 128, 128)
                    for g in range(ngr):
                        pv = PS[PB + 2]
                        proj_tm(seq, i0 + g * 128, wt2, 0, 128, pv, pv[:, 0:128])
                        yield
                        k.copy('act', Vt[h], Vt[h][:, g, :], pv, pv[:, 0:128])
                        yield
                    for g in range(0 if so else ngr):
                        Km = Kmix[par]
                        pa = PS[PB + 3]
                        for a in range(4):
                            c0, c1 = (0, 32 * (a + 1)) if fwd else (32 * a, 128)
                            te = tmpE[par]
                            cs = slice(g * 128 + c0, g * 128 + c1)
                            ci = g * 4 + a
                            k.act(te, te[:, c0:c1], Zh, Zh[:, cs], AF.Exp, bias=refc[:, ci:ci + 1], scale=-1.0, extra=[refc])
                            yield
                            k.tt('pool', Km, Km[:, a, c0:c1], kT, kT[:, cs], te, te[:, c0:c1], ALU.mult)
                            yield
                            k.mm(pa, pa[:, a * 32:(a + 1) * 32], Km, Km[:, a, :], qb, qb[:, g * 128 + a * 32:g * 128 + (a + 1) * 32])
                            yield
                        k.tt('dve', att[h], att[h][:, g, :], pa, pa[:, 0:128], msk, msk[:, 0:128], ALU.mult)
                        yield

                for h0 in (0, 2):
                    gens = [head_prep(h0, 0), head_prep(h0 + 1, 1)]
                    alive = [True, True]
                    while any(alive):
                        for gi in range(2):
                            if alive[gi]:
                                try:
                                    next(gens[gi])
                                except StopIteration:
                                    alive[gi] = False
                groups = list(range(ngr)) if fwd else list(range(ngr - 1, -1, -1))
                for g in groups:
                    gs = slice(g * 128, (g + 1) * 128)
                    for h in range(4):
                        pO = PS[h % 2]
                        if not so:
                            k.mm(pO, pO[:, 0:128], Vt[h], Vt[h][:, g, :], att[h], att[h][:, g, :], start=True, stop=False)
                            k.mm(pO, pO[:, 0:128], S[h], S[h][:, :], qT[h], qT[h][:, gs], start=False, stop=True)
                        pD = PS[2 + h % 2]
                        k.mm(pD, pD[:, 0:128], Kh[h], Kh[h][:, g, :], Vt[h], Vt[h][:, g, :])
                        k.stt(S[h], S[h][:, :], S[h], S[h][:, :], dec[h][:, g:g + 1], pD, pD[:, 0:128], ALU.mult, ALU.add,
                              extra=[dec[h]])
                        if not so:
                            k.copy('act', oseg[h], oseg[h][:, gs], pO, pO[:, 0:128])
                for h in range(0 if so else 4):
                    if fwd:
                        k.dma('pool', of_s[seq][h, :, i0:i0 + SEG], oseg[h][:, :], of_s[seq], oseg[h])
                    else:
                        k.dma('sp', ofl[:, :], of_s[seq][h, :, i0:i0 + SEG], ofl, of_s[seq])
                        k.tt('pool', ofl, ofl[:, :], ofl, ofl[:, :], oseg[h], oseg[h][:, :], ALU.add)
                        k.act(sqb, sqb[:, :], ofl, ofl[:, :], AF.Square)
                        pr = PS[6]
                        k.mm(pr, pr[:, :SEG], ones_bf, ones_bf[:, :], sqb, sqb[:, :])
                        rstd_from(rsb, rsb[:, :], pr, pr[:, :SEG], 8.0, 1e-6)
                        wt = wB2[h % 2]
                        wload(wt, 0, win, 2048 + h * 128, 128)
                        pg = PS[7]
                        proj_fm(seq, i0, SEG, wt, 0, 128, pg, pg[:, :SEG])
                        k.act(sgb, sgb[:, :], pg, pg[:, :SEG], AF.Silu)
                        k.stt(ofl, ofl[:, :], ofl, ofl[:, :], cng[:, h:h + 1], rsb, rsb[:, :], ALU.mult, ALU.mult, extra=[cng])
                        k.tt('pool', yct, yct[:, :], ofl, ofl[:, :], sgb, sgb[:, :], ALU.mult)
                        k.dma('pool', yb_s[seq][:, h, i0:i0 + SEG], yct[:, :], yb_s[seq], yct)
            if seq > 0:
                for h in range(4):
                    k.dma('pool', cS_o[seq - 1, d, h, :, :], S[h][:, :], cS_o, S[h])
        k.release(mk)

    def mixer_out(l, wout, ya_first):
        mk = k.mark()
        woa = k.sb("woa", [64, 8, D], BF16)
        wob = k.sb("wob", [128, 4, D], BF16)
        ra, rb = (0, 512) if ya_first else (512, 0)
        k.dma('sp', woa[:, :, :], wout.h[ra:ra + 512, :].rearrange("(h p) n -> p h n", p=64), woa, wout)
        k.dma('sp', wob[:, :, :], wout.h[rb:rb + 512, :].rearrange("(h p) n -> p h n", p=128), wob, wout)
        yat = [k.sb(f"oyat{i}", [64, 8, 256], BF16) for i in range(2)]
        ybt = [k.sb(f"oybt{i}", [128, 4, 256], BF16) for i in range(2)]
        alloc_ln()
        ii = 0
        for t in ((0, 1, 4) if l == 1 else range(NTT)):
            col = 0 if t < 4 else 1
            for hf in range(2):
                ya_, yb_ = yat[ii % 2], ybt[ii % 2]
                ii += 1
                if t < 4:
                    seq, c0 = 0, t * TT + hf * 256
                else:
                    seq, c0 = 1 + hf, 0
                k.dma('sp', ya_[:, :, :], ya_s[seq][:, :, c0:c0 + 256], ya_, ya_s[seq])
                k.dma('sp', yb_[:, :, :], yb_s[seq][:, :, c0:c0 + 256], yb_, yb_s[seq])
                for dc in range(NC_):
                    py = PS[dc % 4]
                    for hd in range(8):
                        k.mm(py, py[:, 0:256], woa, woa[:, hd, dc * 128:(dc + 1) * 128], ya_, ya_[:, hd, :],
                             start=(hd == 0), stop=False)
                    for hd in range(4):
                        k.mm(py, py[:, 0:256], wob, wob[:, hd, dc * 128:(dc + 1) * 128], yb_, yb_[:, hd, :],
                             start=False, stop=(hd == 3))
                    xs = slice(hf * 256, (hf + 1) * 256)
                    k.stt(X[t], X[t][:, dc, xs], py, py[:, 0:256], cf[l][:, (3 * 1 + 2) * 8 + dc, col:col + 1],
                          X[t], X[t][:, dc, xs], ALU.mult, ALU.add, extra=[cf[l]])
            layernorm(l, 1, t)
            modulate(l, 2, t)
        k.release(mk)

    stop_after = None if debug_stage is None else debug_stage.get('stop')
    cfgA = dict(win=wine_s, qcol=0, kcol=512, rope=True, band=True, qknorm=False, sinkexp=sinkexp,
                ctx=(kctxa_d, vctxa_d), ko=ak_o, vo=av_o, ya_s=ya_s)
    cfgD = dict(win=wino_s, qcol=2560, kcol=3072, rope=True, band=False, qknorm=True, sinkexp=None,
                ctx=(kctxd_d, vctxd_d), ko=dk_o, vo=dv_o, ya_s=ya_s, qg=gvec[:, 0:1], kg=gvec[:, 1:2], qhalf=True)

    def ffn_phase(l, j, nxt=None):
        mk = k.mark()
        alloc_ffn()
        if (l, j) == (0, 0):
            bg_add(wine_d, wine_s, wine_d.h, wine_s.h, D, 2832)
            bg_add(woute_d, woute_s, woute_d.h, woute_s.h, D, D)
            bg_add_ffn(1)
        elif (l, j) == (0, 2):
            bg_add_ffn(2)
            bg_add(wino_d, wino_s, wino_d.h, wino_s.h, D, 3328)
        elif (l, j) == (1, 0):
            bg_add(wouto_d, wouto_s, wouto_d.h, wouto_s.h, D, D)
            bg_add_ffn(3)
        bg_start()
        for t in ((0, 1, 4) if (l, j) == (1, 2) else range(NTT)):
            ffn(l, j, t)
            if nxt is not None:
                modulate(nxt[0], nxt[1], t)
        bg_flush()
        k.release(mk)

    def mark_phase(label):
        PHASE_MARKS.append((label, dict(k.cnt)))

    def run_all_marked():
        mark_phase('setup_end')
        for t in range(NTT):
            modulate(0, 0, t)
        ffn_phase(0, 0, nxt=(0, 1)); mark_phase('ffn00')
        for seq in range(3):
            attention(seq, SL[seq], cfgA); mark_phase(f'attnA{seq}')
            mlstm(seq, SL[seq]); mark_phase(f'mlstm{seq}')
        mixer_out(0, woute_s, True); mark_phase('mout0')
        ffn_phase(0, 2, nxt=(1, 0)); mark_phase('ffn02')
        ffn_phase(1, 0, nxt=(1, 1)); mark_phase('ffn10')
        for seq in range(3):
            hgrn(seq, SL[seq]); mark_phase(f'hgrn{seq}')
            attention(seq, SL[seq], cfgD); mark_phase(f'attnD{seq}')
        mixer_out(1, wouto_s, False); mark_phase('mout1')
        ffn_phase(1, 2, nxt=None); mark_phase('ffn12')

    def run_all():
        if debug_stage is None:
            return run_all_marked()
        for t in range(NTT):
            modulate(0, 0, t)
        ffn_phase(0, 0, nxt=(0, 1))
        if stop_after == 'f00':
            return
        only = None if debug_stage is None else debug_stage.get('only')
        if only is not None:
            for nm in only:
                if nm[0] == 'a':
                    attention(int(nm[1]), SL[int(nm[1])], cfgA)
                if nm[0] == 'm':
                    mlstm(int(nm[1]), SL[int(nm[1])])
                if nm[0] == 'o':
                    mixer_out(0, woute_s, True)
            return
        for seq in range(3):
            attention(seq, SL[seq], cfgA)
            mlstm(seq, SL[seq])
        mixer_out(0, woute_s, True)
        if stop_after == 'm0':
            return
        ffn_phase(0, 2, nxt=(1, 0))
        ffn_phase(1, 0, nxt=(1, 1))
        if stop_after == 'f10':
            return
        only1 = None if debug_stage is None else debug_stage.get('only1')
        if only1 is not None:
            for nm in only1:
                if nm[0] == 'a':
                    attention(int(nm[1]), SL[int(nm[1])], cfgD)
                if nm[0] == 'g':
                    hgrn(int(nm[1]), SL[int(nm[1])])
                if nm[0] == 'o':
                    mixer_out(1, wouto_s, False)
            return
        for seq in range(3):
            hgrn(seq, SL[seq])
            attention(seq, SL[seq], cfgD)
        mixer_out(1, wouto_s, False)
        if stop_after == 'm1':
            return
        ffn_phase(1, 2, nxt=None)

    run_all()

    xo = [k.sb(f"xo{i}", [128, D], F32) for i in range(2)]

    def store_x(dst_d, r0, Xt, t0, i):
        xi = xo[i % 2]
        for g in range(2):
            pt = PS[2 + (2 * i + g) % 4]
            for q in range(4):
                c = g * 4 + q
                k.tr(pt, pt[:, q * 128:(q + 1) * 128], Xt, Xt[:, c, t0:t0 + 128], ident, ident[:, :])
            k.copy(ev_rr(), xi, xi[:, g * 512:(g + 1) * 512], pt, pt[:, :])
        k.dma('pool', dst_d[r0:r0 + 128, :], xi[:, :], dst_d, xi)

    for t in (0, 1, 4):
        for b in range(4):
            if t < 4:
                store_x(ys_d, t * TT + b * 128, X[t], b * 128, t * 4 + b)
            else:
                store_x(yp_d, b * 128, X[t], b * 128, t * 4 + b)
    mark_phase('store')
    k.barrier()
    return nc


def _consts(mir=False):
    import ml_dtypes
    c = {}
    c["ident"] = np.eye(128, dtype=np.float32)
    nf = 16
    inv = (10000.0 ** (-np.arange(nf, dtype=np.float32) / nf)).astype(np.float32)
    t = np.arange(LS)
    if mir:
        t = (LS - 1) - t
    row = (t // 64).astype(np.float32)
    colp = (t % 64).astype(np.float32)
    cosT = np.zeros((128, LS), np.float32)
    sinT = np.zeros((128, LS), np.float32)
    for p in range(128):
        d = p % 64
        pos = row if d < 32 else colp
        ang = (pos * inv[d % 16]).astype(np.float32)
        cosT[p] = np.cos(ang)
        sinT[p] = np.sin(ang)
    c["cosT"], c["sinT"] = cosT, sinT
    R = np.zeros((128, 128), np.float32)
    for m in range(128):
        if (m % 32) < 16:
            R[m, m + 16] = -1.0
        else:
            R[m, m - 16] = 1.0
    c["RT"] = np.ascontiguousarray(R.T)
    sh = np.zeros((128, 64), np.float32)
    for i in range(64):
        sh[64 + i, i] = 1.0
    c["shiftT"] = sh
    sel = np.zeros((4, 4, 128), np.float32)
    for h in range(4):
        sel[h, h, :] = 1.0
    c["sel4"] = sel.reshape(4, 512)
    s = np.arange(128)[:, None]
    tt_ = np.arange(128)[None, :]
    hi = (s <= tt_).astype(np.float32)
    lo = (s >= tt_).astype(np.float32)
    same = ((s // 32) == (tt_ // 32)).astype(np.float32)
    b64 = ((s // 64) == (tt_ // 64)).astype(np.float32) / 64.0
    mskb = np.concatenate([np.tile(hi, (1, 4)), np.tile(lo, (1, 4)), hi * same, lo * same, b64], 1)
    c["mskb"] = mskb.astype(ml_dtypes.bfloat16)
    c["negm"] = np.concatenate([(hi - 1.0) * 30000.0, (lo - 1.0) * 30000.0], 1).astype(np.float32)
    return c


def make_in_maps(inp):
    f = lambda a: np.ascontiguousarray(np.asarray(a, dtype=np.float32))
    maps = []
    lnp = np.concatenate([f(inp['ln_g']).reshape(48, 128), f(inp['ln_b']).reshape(48, 128)], 0)
    lbl = f(inp['c_lb_logits']).reshape(2, 4, 128)
    smallc = np.zeros((128, 32), np.float32)
    smallc[:, 0:4] = f(inp['b_norm_g'])[0].T
    smallc[:, 4:8] = f(inp['c_norm_g'])[0].T
    smallc[:, 8] = np.tile(f(inp['d_q_norm'])[0], 2)
    smallc[:, 9] = np.tile(f(inp['d_k_norm'])[0], 2)
    smallc[:, 10:14] = lbl[0].T
    smallc[:, 14:18] = lbl[1].T
    smallc[:, 18:26] = np.broadcast_to(f(inp['a_sink'])[0].reshape(1, 8), (128, 8))
    kgrow = np.broadcast_to(np.tile(f(inp['d_k_norm'])[0], 2)[None, :], (128, 128)).copy()
    base = {
        "ada_w": f(inp['ada_w']),
        "ada_b": f(inp['ada_b']).reshape(144, 128),
        "lnp": lnp,
        "ffn_w1": f(inp['ffn_w1']).reshape(4, D, DFF),
        "ffn_w3": f(inp['ffn_w3']).reshape(4, D, DFF),
        "ffn_w2": f(inp['ffn_w2']).reshape(4, DFF, D),
        "wout_e": f(inp['w_out_even'])[0],
        "wout_o": f(inp['w_out_odd'])[0],
        "smallc": smallc, "kgrow": kgrow,
    }
    win_e = f(inp['w_in_even'])[0]
    win_o = f(inp['w_in_odd'])[0]
    gb = f(inp['b_gate_bias'])[0]
    variants = []
    for mir in (False, True):
        v = dict(base)
        v.update(_consts(mir))
        if not mir:
            v["win_e"], v["win_o"] = win_e, win_o
            v["gbias"] = np.ascontiguousarray(gb.T)
        else:
            we = win_e.copy()
            we[:, 2304:2312], we[:, 2312:2320] = win_e[:, 2312:2320], win_e[:, 2304:2312]
            wo = win_o.copy()
            wo[:, 512:1024], wo[:, 1024:1536] = win_o[:, 1024:1536], win_o[:, 512:1024]
            v["win_e"], v["win_o"] = we, wo
            v["gbias"] = np.ascontiguousarray(gb[[2, 3, 0, 1]].T)
        variants.append(v)

    def dupk(kc):
        return np.ascontiguousarray(np.concatenate([kc[:, 0], kc[:, 0], kc[:, 1], kc[:, 1]], 1))

    for i in range(8):
        b = i // 2
        mir = (i % 2 == 1)
        m = dict(variants[1 if mir else 0])
        xs = f(inp['x_sample'][b])
        xp = f(inp['x_prompt'][2 * i:2 * i + 2])
        stC, stn, stm, stS = (f(inp['state_b_C'][b, 0]), f(inp['state_b_n'][b, 0]), f(inp['state_b_m'][b, 0]),
                              f(inp['state_c_S'][b, 0]))
        if mir:
            xs = np.ascontiguousarray(xs[::-1])
            xp = np.ascontiguousarray(xp[:, ::-1])
            stC, stn, stm, stS = (np.ascontiguousarray(a_[::-1]) for a_ in (stC, stn, stm, stS))
        m.update({
            "xs": xs,
            "xp": xp.reshape(NPS * LP, D),
            "cvec": np.concatenate([f(inp['c'][b]).reshape(8, 128), f(inp['c_ctx']).reshape(8, 128)], 0),
            "kctxa": dupk(f(inp['cache_a_k'][b, 0])), "vctxa": f(inp['cache_a_v'][b, 0]).reshape(512, 128),
            "kctxd": dupk(f(inp['cache_d_k'][b, 0])), "vctxd": f(inp['cache_d_v'][b, 0]).reshape(512, 128),
            "stC": stC, "stn": stn, "stm": stm, "stS": stS,
        })
        maps.append(m)
    return maps


def gather(r):
    H2 = LS // 2
    ys = np.stack([np.concatenate([r[2 * b]["ys"], r[2 * b + 1]["ys"][::-1]], 0) for b in range(4)], 0)

    def per_core(nm, tok_axis=None, dir_axis=None):
        outs = []
        for i in range(8):
            a = r[i][nm]
            if i % 2 == 1:
                if tok_axis is not None:
                    a = np.flip(a, axis=tok_axis)
                if dir_axis is not None:
                    a = np.flip(a, axis=dir_axis)
            outs.append(a)
        return np.concatenate(outs, 0)

    yp = np.concatenate([(r[i]["yp"].reshape(NPS, LP, D)[:, ::-1] if i % 2 else r[i]["yp"].reshape(NPS, LP, D))
                         for i in range(8)], 0)
    ak = per_core("ak_o", tok_axis=1).reshape(16, 1, LP, 2, 64)
    av = per_core("av_o", tok_axis=1).reshape(16, 1, LP, 2, 64)
    bC = per_core("bC_o", dir_axis=1).reshape(16, 1, 2, 4, 128, 128)
    bn = per_core("bn_o", dir_axis=1).reshape(16, 1, 2, 4, 128)
    bm = per_core("bm_o", dir_axis=1).reshape(16, 1, 2, 4)
    cS = per_core("cS_o", dir_axis=1).reshape(16, 1, 2, 4, 128, 128)
    dk = per_core("dk_o", tok_axis=1).reshape(16, 1, LP, 2, 64)
    dv = per_core("dv_o", tok_axis=1).reshape(16, 1, LP, 2, 64)
    return (np.ascontiguousarray(yp), ys, np.ascontiguousarray(ak), np.ascontiguousarray(av), np.ascontiguousarray(bC),
            np.ascontiguousarray(bn), np.ascontiguousarray(bm), np.ascontiguousarray(cS), np.ascontiguousarray(dk),
            np.ascontiguousarray(dv))


def kernel(**inputs):
    nc = build()
    in_maps = make_in_maps(inputs)
    res = run_bass_kernel_spmd(nc, in_maps, core_ids=list(range(8)))
    return gather(res.results)
```

```python
import numpy as np
import concourse.bass as bass
import concourse.mybir as mybir
from concourse.bass_utils import run_bass_kernel_spmd

F32 = mybir.dt.float32
BF16 = mybir.dt.bfloat16
AF = mybir.ActivationFunctionType
ALU = mybir.AluOpType

D = 1024
NC_ = 8
DFF = 2816
NFC = 22
LS = 2048
LP = 256
NPS = 2
TT = 512
NTT = 5
ALPHA = 4 ** 0.25
LN_EPS = 1e-5 / (ALPHA * ALPHA)
PH = 30000
DBG = {}
PHASE_MARKS = []


class Buf:
    UID = 0

    def __init__(self, h, name, space):
        self.h = h
        self.name = name
        self.space = space
        self.last_w = None
        self.readers = []
        self.dsem = None
        self.dcnt = 0
        Buf.UID += 1
        self.uid = Buf.UID

    def __getitem__(self, idx):
        return self.h[idx]


class KB:
    def __init__(self, nc):
        self.nc = nc
        self.eng = {'pe': nc.tensor, 'act': nc.scalar, 'dve': nc.vector, 'pool': nc.gpsimd, 'sp': nc.sync}
        self.cnt = {e: 0 for e in self.eng}
        self.sems = {e: [] for e in self.eng}
        self.waited = {e: {} for e in self.eng}
        self.nbuf = 0
        self.dma_bufs = []
        self.guards = []
        self.gbufs = []
        self.free_dsems = []
        self.nsem = 0

    def sb(self, name, shape, dt):
        self.nbuf += 1
        g = self.nc.sbuf_tensor('sb_' + name + f'_{self.nbuf}', list(shape), dt)
        h = g.__enter__()
        self.guards.append(g)
        b = Buf(h, name, 'sb')
        self.gbufs.append(b)
        return b

    def mark(self):
        return len(self.guards)

    def release(self, mark):
        self.barrier()
        while len(self.guards) > mark:
            self.guards.pop().__exit__(None, None, None)
            b = self.gbufs.pop()
            if b.dsem is not None:
                self.free_dsems.append((b.dsem, b.dcnt))
                self.dma_bufs.remove(b)
                b.dsem = None

    def ps(self, name, shape, dt=F32):
        return Buf(self.nc.alloc_psum_tensor(name, list(shape), dt), name, 'ps')

    def dram(self, name, shape, dt, kind):
        return Buf(self.nc.dram_tensor(name, list(shape), dt, kind=kind).ap(), name, 'dram')

    def _sem(self, e, phase):
        while len(self.sems[e]) <= phase:
            self.sems[e].append(self.nc.alloc_semaphore(name=f"s_{e}_{len(self.sems[e])}"))
        return self.sems[e][phase]

    def _wait(self, e, ev):
        if ev is None:
            return
        if ev[0] == 'eng':
            _, e2, seq = ev
            if e2 == e and e == 'pe':
                return
            key = ('eng', e2)
            if self.waited[e].get(key, 0) >= seq:
                return
            self.waited[e][key] = seq
            ph, val = (seq - 1) // PH, (seq - 1) % PH + 1
            self.eng[e].wait_ge(self._sem(e2, ph), val)
        else:
            owner = ev[1]
            key = ('dma', owner.uid)
            need = owner.dcnt
            if self.waited[e].get(key, 0) >= need:
                return
            self.waited[e][key] = need
            self.eng[e].wait_ge(owner.dsem, need * 16)

    def _sync(self, e, reads, writes):
        for r in reads:
            self._wait(e, r.last_w)
            if r.space == 'ps':
                for ev in r.readers:
                    if not (ev[0] == 'eng' and ev[1] == e):
                        self._wait(e, ev)
        for w in writes:
            self._wait(e, w.last_w)
            for ev in w.readers:
                if ev[0] == 'eng' and ev[1] == e:
                    continue
                self._wait(e, ev)

    def _record(self, ev, reads, writes):
        for r in reads:
            if r in writes:
                continue
            if ev[0] == 'eng':
                r.readers = [x for x in r.readers if not (x[0] == 'eng' and x[1] == ev[1])]
            else:
                r.readers = [x for x in r.readers if not (x[0] == 'dma' and x[1] is ev[1])]
            r.readers.append(ev)
        for w in writes:
            w.last_w = ev
            w.readers = []

    def op(self, e, fn, reads, writes):
        self._sync(e, reads, writes)
        inst = fn()
        self.cnt[e] += 1
        seq = self.cnt[e]
        inst.then_inc(self._sem(e, (seq - 1) // PH), 1)
        self._record(('eng', e, seq), reads, writes)
        return inst

    def dma(self, q, out_ap, in_ap, dst, src, owner=None):
        if owner is None:
            owner = dst if dst.space == 'sb' else src
        if owner.dsem is None:
            if self.free_dsems:
                owner.dsem, owner.dcnt = self.free_dsems.pop()
            else:
                self.nsem += 1
                owner.dsem = self.nc.alloc_semaphore(name=f"d_{self.nsem}")
            self.dma_bufs.append(owner)
        self._sync(q, [src], [dst])
        inst = self.eng[q].dma_start(out=out_ap, in_=in_ap)
        owner.dcnt += 1
        inst.then_inc(owner.dsem, 16)
        self._record(('dma', owner), [src], [dst])

    def barrier(self):
        for e in self.eng:
            for e2 in self.eng:
                if e2 != e and self.cnt[e2] > 0:
                    self._wait(e, ('eng', e2, self.cnt[e2]))
            for b in self.dma_bufs:
                self._wait(e, ('dma', b))

    def mm(self, out, out_ap, lhsT, lhsT_ap, rhs, rhs_ap, start=True, stop=True):
        rd = [lhsT, rhs]
        return self.op('pe', lambda: self.nc.tensor.matmul(out_ap, lhsT=lhsT_ap, rhs=rhs_ap, start=start,
                                                          stop=stop), rd, [out])

    def tr(self, out, out_ap, in_, in_ap, ident, ident_ap):
        return self.op('pe', lambda: self.nc.tensor.transpose(out_ap, in_ap, ident_ap), [in_, ident], [out])

    def act(self, out, out_ap, in_, in_ap, func, bias=None, scale=None, extra=()):
        kw = {}
        if bias is not None:
            kw['bias'] = bias
        if scale is not None:
            kw['scale'] = scale
        return self.op('act', lambda: self.nc.scalar.activation(out=out_ap, in_=in_ap, func=func, **kw),
                       [in_] + list(extra), [out])

    def tt(self, e, out, out_ap, a, a_ap, b, b_ap, op):
        eng = self.eng[e]
        return self.op(e, lambda: eng.tensor_tensor(out=out_ap, in0=a_ap, in1=b_ap, op=op), [a, b], [out])

    def ts(self, e, out, out_ap, a, a_ap, s1, s2, op0, op1=None, extra=()):
        eng = self.eng[e]
        if op1 is None:
            f = lambda: eng.tensor_scalar(out=out_ap, in0=a_ap, scalar1=s1, scalar2=None, op0=op0)
        else:
            f = lambda: eng.tensor_scalar(out=out_ap, in0=a_ap, scalar1=s1, scalar2=s2, op0=op0, op1=op1)
        return self.op(e, f, [a] + list(extra), [out])

    def stt(self, out, out_ap, a, a_ap, scalar, b, b_ap, op0, op1, extra=()):
        return self.op('dve', lambda: self.nc.vector.scalar_tensor_tensor(out=out_ap, in0=a_ap, scalar=scalar,
                                                                         in1=b_ap, op0=op0, op1=op1),
                       [a, b] + list(extra), [out])

    def copy(self, e, out, out_ap, in_, in_ap):
        if e == 'act':
            return self.act(out, out_ap, in_, in_ap, AF.Copy)
        eng = self.eng[e]
        return self.op(e, lambda: eng.tensor_copy(out=out_ap, in_=in_ap), [in_], [out])

    def memset(self, e, out, out_ap, val):
        eng = self.eng[e]
        return self.op(e, lambda: eng.memset(out_ap, val), [], [out])


class RR:
    def __init__(self, engs):
        self.engs = engs
        self.i = 0

    def __call__(self):
        e = self.engs[self.i % len(self.engs)]
        self.i += 1
        return e


def build(debug_stage=None):
    nc = bass.Bass("TRN2", target_bir_lowering=False)
    k = KB(nc)
    xs_d = k.dram("xs", [LS, D], F32, "ExternalInput")
    xp_d = k.dram("xp", [NPS * LP, D], F32, "ExternalInput")
    cv_d = k.dram("cvec", [2 * NC_, 128], F32, "ExternalInput")
    adaw_d = k.dram("ada_w", [2, D, 9 * D], F32, "ExternalInput")
    adab_d = k.dram("ada_b", [2 * 72, 128], F32, "ExternalInput")
    lnp_d = k.dram("lnp", [96, 128], F32, "ExternalInput")
    w1_d = k.dram("ffn_w1", [4, D, DFF], F32, "ExternalInput")
    w3_d = k.dram("ffn_w3", [4, D, DFF], F32, "ExternalInput")
    w2_d = k.dram("ffn_w2", [4, DFF, D], F32, "ExternalInput")
    ident_d = k.dram("ident", [128, 128], F32, "ExternalInput")
    wine_d = k.dram("win_e", [D, 2832], F32, "ExternalInput")
    woute_d = k.dram("wout_e", [D, D], F32, "ExternalInput")
    wino_d = k.dram("win_o", [D, 3328], F32, "ExternalInput")
    wouto_d = k.dram("wout_o", [D, D], F32, "ExternalInput")
    kctxa_d = k.dram("kctxa", [512, 256], F32, "ExternalInput")
    vctxa_d = k.dram("vctxa", [512, 128], F32, "ExternalInput")
    kctxd_d = k.dram("kctxd", [512, 256], F32, "ExternalInput")
    vctxd_d = k.dram("vctxd", [512, 128], F32, "ExternalInput")
    stC_d = k.dram("stC", [2, 4, 128, 128], F32, "ExternalInput")
    stn_d = k.dram("stn", [2, 4, 128], F32, "ExternalInput")
    stm_d = k.dram("stm", [2, 4], F32, "ExternalInput")
    stS_d = k.dram("stS", [2, 4, 128, 128], F32, "ExternalInput")
    gbias_d = k.dram("gbias", [4, 4], F32, "ExternalInput")
    smallc_d = k.dram("smallc", [128, 32], F32, "ExternalInput")
    kgrow_d = k.dram("kgrow", [128, 128], F32, "ExternalInput")
    cos_d = k.dram("cosT", [128, LS], F32, "ExternalInput")
    sin_d = k.dram("sinT", [128, LS], F32, "ExternalInput")
    RT_d = k.dram("RT", [128, 128], F32, "ExternalInput")
    negm_d = k.dram("negm", [128, 256], F32, "ExternalInput")
    shiftT_d = k.dram("shiftT", [128, 64], F32, "ExternalInput")
    sel4_d = k.dram("sel4", [4, 512], F32, "ExternalInput")
    mskb_d = k.dram("mskb", [128, 1408], BF16, "ExternalInput")
    ys_d = k.dram("ys", [LS // 2, D], F32, "ExternalOutput")
    ak_o = k.dram("ak_o", [NPS, LP, 128], F32, "ExternalOutput")
    av_o = k.dram("av_o", [NPS, LP, 128], F32, "ExternalOutput")
    dk_o = k.dram("dk_o", [NPS, LP, 128], F32, "ExternalOutput")
    dv_o = k.dram("dv_o", [NPS, LP, 128], F32, "ExternalOutput")
    bC_o = k.dram("bC_o", [NPS, 2, 4, 128, 128], F32, "ExternalOutput")
    bn_o = k.dram("bn_o", [NPS, 2, 4, 128], F32, "ExternalOutput")
    bm_o = k.dram("bm_o", [NPS, 2, 4], F32, "ExternalOutput")
    cS_o = k.dram("cS_o", [NPS, 2, 4, 128, 128], F32, "ExternalOutput")
    wine_s = k.dram("wine_s", [D, 2832], BF16, "Internal")
    woute_s = k.dram("woute_s", [D, D], BF16, "Internal")
    wino_s = k.dram("wino_s", [D, 3328], BF16, "Internal")
    wouto_s = k.dram("wouto_s", [D, D], BF16, "Internal")
    SL = [LS, LP, LP]
    ya_s = [k.dram(f"ya_s{i}", [64, 8, SL[i]], BF16, "Internal") for i in range(3)]
    yb_s = [k.dram(f"yb_s{i}", [128, 4, SL[i]], BF16, "Internal") for i in range(3)]
    hf_s = [k.dram(f"hf_s{i}", [4, 128, SL[i]], F32, "Internal") for i in range(3)]
    of_s = hf_s
    yp_d = k.dram("yp", [NPS * LP, D], F32, "ExternalOutput")
    w1_s = k.dram("w1_s", [4, D, DFF], BF16, "Internal")
    w3_s = k.dram("w3_s", [4, D, DFF], BF16, "Internal")
    w2_s = k.dram("w2_s", [4, DFF, D], BF16, "Internal")

    X = [k.sb(f"X{t}", [128, NC_, TT], F32) for t in range(NTT)]
    H = [k.sb(f"H{t}", [128, NC_, TT], BF16) for t in range(NTT)]
    ident = k.sb("ident", [128, 128], F32)
    ones_bf = k.sb("ones_bf", [128, 128], BF16)
    modv = [k.sb(f"modv{l}", [128, 72, 2], F32) for l in range(2)]
    sc1 = [k.sb(f"sc1{l}", [128, 72, 2], F32) for l in range(2)]
    cf = [k.sb(f"cf{l}", [128, 72, 2], F32) for l in range(2)]
    lnp = k.sb("lnp", [128, 96], F32)
    PS = [k.ps(f"ps{i}", [128, 512]) for i in range(8)]

    k.dma('sp', ident[:, :], ident_d[:, :], ident, ident_d)
    RT = k.sb("RT", [128, 128], F32)
    k.dma('sp', RT[:, :], RT_d[:, :], RT, RT_d)
    shiftT = k.sb("shiftT", [128, 64], F32)
    k.dma('sp', shiftT[:, :], shiftT_d[:, :], shiftT, shiftT_d)
    sel4 = k.sb("sel4", [4, 4, 128], F32)
    k.dma('sp', sel4[:, :, :], sel4_d.h[:, :].rearrange("p (h t) -> p h t", h=4), sel4, sel4_d)

    class View:
        def __init__(self, buf, c0, c1):
            self.buf, self.c0, self.c1 = buf, c0, c1
    tri_hi4 = k.sb("tri_hi4", [128, 512], BF16)
    tri_lo4 = k.sb("tri_lo4", [128, 512], BF16)
    bd_hi = k.sb("bd_hi", [128, 128], BF16)
    bd_lo = k.sb("bd_lo", [128, 128], BF16)
    bones64 = k.sb("bones64", [128, 128], BF16)
    smallc = k.sb("smallc", [128, 32], F32)
    bng = k.sb("bng", [128, 4], F32)
    cng = k.sb("cng", [128, 4], F32)
    gvec = k.sb("gvec", [128, 2], F32)
    lbc = k.sb("lbc", [128, 4], F32)
    oml = k.sb("oml", [128, 4], F32)
    sinkexp = k.sb("sinkexp", [64, 2, 512], F32)
    sk8 = k.sb("sk8", [64, 8], F32)
    kgrow = k.sb("kgrow", [128, 128], F32)
    mk_c = k.mark()
    mskb = k.sb("mskb", [128, 1408], BF16)
    k.dma('sp', mskb[:, :], mskb_d[:, :], mskb, mskb_d)
    k.copy('pool', tri_hi4, tri_hi4[:, :], mskb, mskb[:, 0:512])
    k.copy('pool', tri_lo4, tri_lo4[:, :], mskb, mskb[:, 512:1024])
    k.copy('pool', bd_hi, bd_hi[:, :], mskb, mskb[:, 1024:1152])
    k.copy('pool', bd_lo, bd_lo[:, :], mskb, mskb[:, 1152:1280])
    k.copy('pool', bones64, bones64[:, :], mskb, mskb[:, 1280:1408])
    k.release(mk_c)
    k.dma('sp', smallc[:, :], smallc_d[:, :], smallc, smallc_d)
    k.copy('dve', bng, bng[:, :], smallc, smallc[:, 0:4])
    k.copy('dve', cng, cng[:, :], smallc, smallc[:, 4:8])
    k.copy('dve', gvec, gvec[:, :], smallc, smallc[:, 8:10])
    k.tt('dve', lbc, lbc[:, :], smallc, smallc[:, 14:18], smallc, smallc[:, 10:14], ALU.subtract)
    k.act(lbc, lbc[:, :], lbc, lbc[:, :], AF.Sigmoid)
    k.ts('dve', oml, oml[:, :], lbc, lbc[:, :], -1.0, 1.0, ALU.mult, ALU.add)
    k.act(sk8, sk8[:, :], smallc, smallc[0:64, 18:26], AF.Exp)
    for kv_ in range(2):
        for hp_ in range(2):
            for ab_ in range(2):
                g_ = ab_ * 2 + hp_
                c0_ = hp_ * 256 + ab_ * 128
                k.copy('dve', sinkexp, sinkexp[:, kv_, c0_:c0_ + 128], sk8,
                       sk8[:, kv_ * 4 + g_:kv_ * 4 + g_ + 1].broadcast_to([64, 128]))
    k.dma('sp', kgrow[:, :], kgrow_d[:, :], kgrow, kgrow_d)
    k.memset('pool', ones_bf, ones_bf[:, :], 1.0 / D)

    stage = k.sb("stage_small", [128, 128], F32)
    silu_c = k.sb("silu_c", [128, NC_, 2], F32)

    def load_cols(src_d, r0, nrows, dst, dst_ap, func=None):
        k.dma('sp', stage[0:nrows, :], src_d[r0:r0 + nrows, :], stage, src_d)
        k.tr(PS[0], PS[0][:, 0:nrows], stage, stage[0:nrows, :], ident, ident[0:nrows, 0:nrows])
        if func is None:
            k.copy('dve', dst, dst_ap, PS[0], PS[0][:, 0:nrows])
        else:
            k.act(dst, dst_ap, PS[0], PS[0][:, 0:nrows], func)

    for j in range(2):
        load_cols(cv_d, j * NC_, NC_, silu_c, silu_c[:, :, j], AF.Silu)
    load_cols(lnp_d, 0, 96, lnp, lnp[:, :])

    mk0 = k.mark()
    adab = k.sb("adab", [128, 144], F32)
    load_cols(adab_d, 0, 72, adab, adab[:, 0:72])
    load_cols(adab_d, 72, 72, adab, adab[:, 72:144])
    NADA = 2
    adat = [k.sb(f"adat{i}", [128, NC_, 512], F32) for i in range(NADA)]
    NST = 2
    st_f = [k.sb(f"stf{i}", [128, 2816], F32) for i in range(NST)]
    st_b = [k.sb(f"stb{i}", [128, 2816], BF16) for i in range(NST)]
    cast_rr = RR(['act', 'dve', 'pool'])
    pi = [0]
    prep_units = []

    def prep_rows(src_d, dst_d, src_ap, dst_ap, ncols):
        def f():
            i = pi[0] % NST
            pi[0] += 1
            k.dma('sp', st_f[i][:, 0:ncols], src_ap, st_f[i], src_d)
            k.copy(cast_rr(), st_b[i], st_b[i][:, 0:ncols], st_f[i], st_f[i][:, 0:ncols])
            k.dma('pool', dst_ap, st_b[i][:, 0:ncols], dst_d, st_b[i])
        prep_units.append(f)

    def prep_rows2(s, r):
        def f():
            i = pi[0] % NST
            pi[0] += 1
            src = w2_d.h[s, r * 256:(r + 1) * 256, :].rearrange("(a p) n -> p a n", p=128)
            dst = w2_s.h[s, r * 256:(r + 1) * 256, :].rearrange("(a p) n -> p a n", p=128)
            sf = st_f[i].h[:, 0:2048].rearrange("p (a n) -> p a n", a=2)
            sbb = st_b[i].h[:, 0:2048].rearrange("p (a n) -> p a n", a=2)
            k.dma('sp', sf, src, st_f[i], w2_d)
            k.copy(cast_rr(), st_b[i], st_b[i][:, 0:2048], st_f[i], st_f[i][:, 0:2048])
            k.dma('pool', dst, sbb, w2_s, st_b[i])
        prep_units.append(f)

    def prep_ffn(s):
        for kc in range(NC_):
            prep_rows(w1_d, w1_s, w1_d.h[s, kc * 128:(kc + 1) * 128, :], w1_s.h[s, kc * 128:(kc + 1) * 128, :], DFF)
            prep_rows(w3_d, w3_s, w3_d.h[s, kc * 128:(kc + 1) * 128, :], w3_s.h[s, kc * 128:(kc + 1) * 128, :], DFF)
        for r in range(11):
            prep_rows2(s, r)

    def prep_mat(src_d, dst_d, ncols):
        for kc in range(NC_):
            for c0 in range(0, ncols, 2048):
                n = min(2048, ncols - c0)
                prep_rows(src_d, dst_d, src_d.h[kc * 128:(kc + 1) * 128, c0:c0 + n],
                          dst_d.h[kc * 128:(kc + 1) * 128, c0:c0 + n], n)

    prep_ffn(0)
    per_blk = (len(prep_units) + 35) // 36
    pu = [0]

    def emit_prep(n):
        for _ in range(n):
            if pu[0] < len(prep_units):
                prep_units[pu[0]]()
                pu[0] += 1

    it = 0
    for l in range(2):
        for nb4 in range(18):
            at = adat[it % NADA]
            it += 1
            k.dma('sp', at[:, :, :], adaw_d.h[l].rearrange("(kc p) n -> p kc n", p=128)[:, :, nb4 * 512:(nb4 + 1) * 512],
                  at, adaw_d)
            pm = PS[1 + (nb4 % 2)]
            for q in range(4):
                for kc in range(NC_):
                    k.mm(pm, pm[:, q * 2:q * 2 + 2], at, at[:, kc, q * 128:(q + 1) * 128], silu_c, silu_c[:, kc, :],
                         start=(kc == 0), stop=(kc == NC_ - 1))
            for q in range(4):
                nb = nb4 * 4 + q
                k.ts('dve', modv[l], modv[l][:, nb, :], pm, pm[:, q * 2:q * 2 + 2],
                     adab[:, l * 72 + nb:l * 72 + nb + 1], None, ALU.add, extra=[adab])
            emit_prep(per_blk)
        k.ts('pool', sc1[l], sc1[l][:, :, :], modv[l], modv[l][:, :, :], 1.0, None, ALU.add)
        for j in range(3):
            cc = (1.0 if j == 1 else 0.5) / ALPHA
            g0 = (3 * j + 2) * 8
            k.ts('pool', cf[l], cf[l][:, g0:g0 + 8, :], modv[l], modv[l][:, g0:g0 + 8, :], cc, None, ALU.mult)
    emit_prep(len(prep_units))
    k.release(mk0)
    mk2 = k.mark()
    xin = [k.sb(f"xin{i}", [128, D], F32) for i in range(2)]
    ev_rr = RR(['dve', 'act'])

    def load_x(src_d, r0, Xt, t0, i):
        xi = xin[i % 2]
        k.dma('sp', xi[:, :], src_d[r0:r0 + 128, :], xi, src_d)
        for g in range(2):
            pt = PS[2 + (2 * i + g) % 4]
            for q in range(4):
                c = g * 4 + q
                k.tr(pt, pt[:, q * 128:(q + 1) * 128], xi, xi[:, c * 128:(c + 1) * 128], ident, ident[:, :])
            k.copy(ev_rr(), Xt, Xt.h[:, g * 4:(g + 1) * 4, t0:t0 + 128],
                   pt, pt.h[:, :].rearrange("p (q t) -> p q t", q=4))

    for t in range(NTT):
        for b in range(4):
            if t < 4:
                load_x(xs_d, t * TT + b * 128, X[t], b * 128, t * 4 + b)
            else:
                load_x(xp_d, b * 128, X[t], b * 128, t * 4 + b)

    k.release(mk2)
    mod_rr = RR(['act', 'pool'])

    def modulate(l, j, t):
        col = 0 if t < 4 else 1
        for c in range(NC_):
            s_ap = sc1[l][:, (3 * j + 1) * 8 + c, col:col + 1]
            b_ap = modv[l][:, (3 * j) * 8 + c, col:col + 1]
            e = mod_rr()
            if e == 'act':
                k.act(H[t], H[t][:, c, :], X[t], X[t][:, c, :], AF.Identity, bias=b_ap, scale=s_ap,
                      extra=[sc1[l], modv[l]])
            else:
                k.ts('pool', H[t], H[t][:, c, :], X[t], X[t][:, c, :], s_ap, b_ap, ALU.mult, ALU.add,
                     extra=[sc1[l], modv[l]])

    class NS:
        pass
    F = NS()

    def alloc_ln():
        F.zb = k.sb("zb", [128, 16, TT], BF16)
        F.mean_sb = k.sb("ln_mean", [128, TT], F32)
        F.rstd_sb = k.sb("ln_rstd", [128, TT], F32)
        F.tmp_sb = k.sb("ln_tmp", [128, TT], F32)
        F.lt = [k.sb(f"ln_t{i}", [128, TT], F32) for i in range(2)]

    def alloc_ffn():
        F.G = k.sb("G", [128, NFC, TT], BF16)
        F.zb = F.G
        F.mean_sb = k.sb("ln_mean", [128, TT], F32)
        F.rstd_sb = k.sb("ln_rstd", [128, TT], F32)
        F.tmp_sb = k.sb("ln_tmp", [128, TT], F32)
        F.w1r = [k.sb(f"w1r{i}", [128, NC_, 256], BF16) for i in range(NW13)]
        F.w3r = [k.sb(f"w3r{i}", [128, NC_, 256], BF16) for i in range(NW13)]
        F.w2r = [k.sb(f"w2r{i}", [128, 512], BF16) for i in range(NW2)]
        F.sg = [k.sb(f"sg{i}", [128, TT], F32) for i in range(2)]
        F.lt = F.sg
        F.bgf = [k.sb(f"bgf{i}", [128, BGW], F32) for i in range(2)]
        F.bgb = [k.sb(f"bgb{i}", [128, BGW], BF16) for i in range(2)]

    NW13 = 2
    NW2 = 6

    def layernorm(l, j, t):
        Xt = X[t]
        pm, pq = PS[0], PS[1]
        zb, zq = F.zb, F.zb
        mean_sb, rstd_sb, tmp_sb, lt = F.mean_sb, F.rstd_sb, F.tmp_sb, F.lt
        for c in range(NC_):
            k.copy('dve' if c % 2 == 0 else 'pool', zb, zb[:, c, :], Xt, Xt[:, c, :])
            k.act(zq, zq[:, 8 + c, :], Xt, Xt[:, c, :], AF.Square)
        for c in range(NC_):
            k.mm(pm, pm[:, :], ones_bf, ones_bf[:, :], zb, zb[:, c, :], start=(c == 0), stop=(c == NC_ - 1))
        for c in range(NC_):
            k.mm(pq, pq[:, :], ones_bf, ones_bf[:, :], zq, zq[:, 8 + c, :], start=(c == 0), stop=(c == NC_ - 1))
        k.copy('dve', mean_sb, mean_sb[:, :], pm, pm[:, :])
        k.tt('dve', tmp_sb, tmp_sb[:, :], mean_sb, mean_sb[:, :], mean_sb, mean_sb[:, :], ALU.mult)
        k.tt('dve', tmp_sb, tmp_sb[:, :], pq, pq[:, :], tmp_sb, tmp_sb[:, :], ALU.subtract)
        k.ts('dve', tmp_sb, tmp_sb[:, :], tmp_sb, tmp_sb[:, :], 0.0, LN_EPS, ALU.max, ALU.add)
        k.act(rstd_sb, rstd_sb[:, :], tmp_sb, tmp_sb[:, :], AF.Ln)
        k.act(rstd_sb, rstd_sb[:, :], rstd_sb, rstd_sb[:, :], AF.Exp, scale=-0.5)
        gi = (l * 3 + j) * 8
        for c in range(NC_):
            tb = lt[c % 2]
            e = 'pool' if c % 4 == 3 else 'dve'
            k.tt(e, tb, tb[:, :], Xt, Xt[:, c, :], mean_sb, mean_sb[:, :], ALU.subtract)
            k.tt(e, tb, tb[:, :], tb, tb[:, :], rstd_sb, rstd_sb[:, :], ALU.mult)
            k.act(Xt, Xt[:, c, :], tb, tb[:, :], AF.Identity, bias=lnp[:, 48 + gi + c:48 + gi + c + 1],
                  scale=lnp[:, gi + c:gi + c + 1], extra=[lnp])

    BGW = 1416
    BG = NS()
    BG.q = []
    BG.pos = 0
    BG.tick = 0
    BG.stride = 1
    BG.i = 0

    def bg_add(src_d, dst_d, src2d, dst2d, R, C):
        npc = (C + BGW - 1) // BGW
        w = (C + npc - 1) // npc
        for rc in range(R // 128):
            for pc in range(npc):
                c0 = pc * w
                n = min(w, C - c0)
                BG.q.append((src_d, dst_d, src2d[rc * 128:(rc + 1) * 128, c0:c0 + n],
                             dst2d[rc * 128:(rc + 1) * 128, c0:c0 + n], n))

    def bg_add_ffn(si):
        bg_add(w1_d, w1_s, w1_d.h[si], w1_s.h[si], D, DFF)
        bg_add(w3_d, w3_s, w3_d.h[si], w3_s.h[si], D, DFF)
        bg_add(w2_d, w2_s, w2_d.h[si], w2_s.h[si], DFF, D)

    def bg_load(u):
        src_d, dst_d, sap, dap, n = BG.q[u]
        i = u % 2
        k.dma('pool', F.bgf[i][:, 0:n], sap, F.bgf[i], src_d)

    def bg_cast_store(u):
        src_d, dst_d, sap, dap, n = BG.q[u]
        i = u % 2
        k.copy('pool', F.bgb[i], F.bgb[i][:, 0:n], F.bgf[i], F.bgf[i][:, 0:n])
        k.dma('pool', dap, F.bgb[i][:, 0:n], dst_d, F.bgb[i])

    def bg_start():
        n = len(BG.q) - BG.pos
        BG.stride = max(1, 300 // max(n, 1))
        BG.tick = 0
        BG.loaded = BG.pos - 1

    def bg_step():
        u = BG.pos
        if BG.loaded < u:
            bg_load(u)
            BG.loaded = u
        if u + 1 < len(BG.q) and BG.loaded < u + 1:
            bg_load(u + 1)
            BG.loaded = u + 1
        bg_cast_store(u)
        BG.pos += 1

    def bg_tick():
        BG.tick += 1
        if BG.tick % BG.stride == 0 and BG.pos < len(BG.q):
            bg_step()

    def bg_flush():
        while BG.pos < len(BG.q):
            bg_step()

    wc = {'a': 0, 'b': 0}

    def ffn(l, j, t):
        s = l * 2 + (0 if j == 0 else 1)
        col = 0 if t < 4 else 1
        Ht = H[t]
        G, w1r, w3r, w2r, sg = F.G, F.w1r, F.w3r, F.w2r, F.sg
        for fc in range(NFC):
            if fc % 2 == 0:
                i = wc['a'] % NW13
                wc['a'] += 1
                k.dma('sp', w1r[i][:, :, :],
                      w1_s.h[s].rearrange("(kc p) n -> p kc n", p=128)[:, :, fc * 128:(fc + 2) * 128], w1r[i], w1_s)
                k.dma('sp', w3r[i][:, :, :],
                      w3_s.h[s].rearrange("(kc p) n -> p kc n", p=128)[:, :, fc * 128:(fc + 2) * 128], w3r[i], w3_s)
            wo = (fc % 2) * 128
            pa, pb = PS[(fc % 2) * 2], PS[(fc % 2) * 2 + 1]
            for kc in range(NC_):
                k.mm(pa, pa[:, :], w1r[i], w1r[i][:, kc, wo:wo + 128], Ht, Ht[:, kc, :], start=(kc == 0), stop=(kc == NC_ - 1))
            for kc in range(NC_):
                k.mm(pb, pb[:, :], w3r[i], w3r[i][:, kc, wo:wo + 128], Ht, Ht[:, kc, :], start=(kc == 0), stop=(kc == NC_ - 1))
            sgi = sg[fc % 2]
            k.act(sgi, sgi[:, :], pa, pa[:, :], AF.Silu)
            k.tt('dve', G, G[:, fc, :], sgi, sgi[:, :], pb, pb[:, :], ALU.mult)
            bg_tick()
        for dg in range(2):
            for fc in range(NFC):
                i = wc['b'] % NW2
                wc['b'] += 1
                k.dma('sp', w2r[i][:, :], w2_s.h[s, fc * 128:(fc + 1) * 128, dg * 512:(dg + 1) * 512], w2r[i], w2_s)
                for q in range(4):
                    py = PS[4 + q]
                    k.mm(py, py[:, :], w2r[i], w2r[i][:, q * 128:(q + 1) * 128], G, G[:, fc, :],
                         start=(fc == 0), stop=(fc == NFC - 1))
                bg_tick()
            for q in range(4):
                c = dg * 4 + q
                py = PS[4 + q]
                k.stt(X[t], X[t][:, c, :], py, py[:, :], cf[l][:, (3 * j + 2) * 8 + c, col:col + 1],
                      X[t], X[t][:, c, :], ALU.mult, ALU.add, extra=[cf[l]])
        layernorm(l, j, t)

    def hs(seq, kc, i0, n):
        if seq == 0:
            t, c0 = i0 // TT, i0 % TT
        else:
            t, c0 = 4, (seq - 1) * LP + i0
        return H[t], H[t][:, kc, c0:c0 + n]

    M = NS()
    NWM = 4
    wmc = [0]

    def alloc_wm(n=3):
        M.wm = [k.sb(f"wm{i}", [128, NC_, 256], BF16) for i in range(n)]

    def next_wm():
        w = M.wm[wmc[0] % len(M.wm)]
        wmc[0] += 1
        return w

    def wload(wt, dcol, scr, col0, n):
        k.dma('sp', wt[:, :, dcol:dcol + n], scr.h.rearrange("(kc p) n -> p kc n", p=128)[:, :, col0:col0 + n], wt, scr)

    def proj_fm(seq, i0, n, wt, wc0, Mo, ps, ps_ap):
        for kc in range(NC_):
            Hb, hap = hs(seq, kc, i0, n)
            k.mm(ps, ps_ap, wt, wt[:, kc, wc0:wc0 + Mo], Hb, hap, start=(kc == 0), stop=(kc == NC_ - 1))

    def proj_tm(seq, i0, wt, wc0, ncols, ps, ps_ap):
        for kc in range(NC_):
            Hb, hap = hs(seq, kc, i0, 128)
            k.mm(ps, ps_ap, Hb, hap, wt, wt[:, kc, wc0:wc0 + ncols], start=(kc == 0), stop=(kc == NC_ - 1))

    def rstd_from(dst, dst_ap, src, src_ap, scale, eps):
        k.act(dst, dst_ap, src, src_ap, AF.Ln, bias=float(eps), scale=float(scale))
        k.act(dst, dst_ap, dst, dst_ap, AF.Exp, scale=-0.5)

    def attention(seq, L, cfg):
        nt = L // 128
        ctx = cfg['ctx'] if seq == 0 else None
        rope = cfg['rope'] and seq == 0
        band = cfg['band'] and seq == 0
        qknorm = cfg['qknorm']
        sinkexp = cfg['sinkexp']
        win = cfg['win']
        nctx = 4 if ctx is not None else 0
        qhalf = bool(cfg.get('qhalf')) and seq == 0
        mk = k.mark()
        QT = k.sb("QT", [128, 4, L], BF16)
        KT = k.sb("KT", [128, 2, L + 128 * nctx], BF16)
        VA = k.sb("VA", [128, nt + nctx, 2, 128], BF16)
        k.memset('pool', VA, VA[:, :, :, 64:128], 1.0)
        mkA = k.mark()
        alloc_wm()
        step = min(L, 512)
        qf = [k.sb(f"qf{i}", [128, step], F32) for i in range(2)]
        t1 = [k.sb(f"t1{i}", [128, step], F32) for i in range(2)]
        if rope:
            ropeT = k.sb("ropeT", [128, 2, step], F32)
        if qknorm:
            sq = k.sb("sq", [128, step], BF16)
            rs = k.sb("rs", [128, step], F32)
        kvo = [k.sb(f"kvo{i}", [128, 256], F32) for i in range(2)]
        if qknorm:
            kt2 = k.sb("kt2", [128, 128], F32)
            kss = k.sb("kss", [128, 2], F32)
        bi = 0
        for i0 in range(0, L, step):
            n = step
            if rope:
                k.dma('sp', ropeT[:, 0, :], cos_d[:, i0:i0 + n], ropeT, cos_d)
                k.dma('sp', ropeT[:, 1, :], sin_d[:, i0:i0 + n], ropeT, sin_d)
            for blk in range(DBG.get('nblk', 6)):
                if blk < 4 and qhalf and i0 >= L // 2:
                    continue
                wt = next_wm()
                if blk < 4:
                    wload(wt, 0, win, cfg['qcol'] + blk * 128, 128)
                    dst, dst_ap = QT, QT[:, blk, i0:i0 + n]
                    gcol = cfg.get('qg')
                else:
                    kv = blk - 4
                    wload(wt, 0, win, cfg['kcol'] + kv * 64, 64)
                    wload(wt, 64, win, cfg['kcol'] + kv * 64, 64)
                    dst, dst_ap = KT, KT[:, kv, i0:i0 + n]
                    gcol = cfg.get('kg')
                pp = PS[blk % 2]
                proj_fm(seq, i0, n, wt, 0, 128, pp, pp[:, :n])
                qfi = qf[bi % 2]
                t1i = t1[bi % 2]
                bi += 1
                if qknorm:
                    k.act(sq, sq[:, :n], pp, pp[:, :n], AF.Square)
                    pn = PS[2]
                    k.mm(pn, pn[:, :n], bones64, bones64[:, :], sq, sq[:, :n])
                    rstd_from(rs, rs[:, :n], pn, pn[:, :n], 1.0, 1e-6)
                    k.stt(qfi, qfi[:, :n], pp, pp[:, :n], gcol, rs, rs[:, :n], ALU.mult, ALU.mult, extra=[gvec])
                    src, src_ap = qfi, qfi[:, :n]
                elif rope:
                    k.copy('act', qfi, qfi[:, :n], pp, pp[:, :n])
                    src, src_ap = qfi, qfi[:, :n]
                else:
                    src, src_ap = pp, pp[:, :n]
                if rope:
                    pr = PS[3]
                    k.mm(pr, pr[:, :n], RT, RT[:, :], src, src_ap)
                    k.tt('pool', t1i, t1i[:, :n], src, src_ap, ropeT, ropeT[:, 0, :n], ALU.mult)
                    k.tt('dve', src, src_ap, pr, pr[:, :n], ropeT, ropeT[:, 1, :n], ALU.mult)
                    k.tt('pool', dst, dst_ap, t1i, t1i[:, :n], src, src_ap, ALU.add)
                else:
                    if src.space == 'ps':
                        k.copy('act', dst, dst_ap, src, src_ap)
                    else:
                        k.copy('pool', dst, dst_ap, src, src_ap)
            wt = next_wm()
            wload(wt, 0, win, cfg['kcol'], 256)
            for b in range(0 if DBG.get('notm') else n // 128):
                pk = PS[4 + b % 2]
                proj_tm(seq, i0 + b * 128, wt, 0, 256, pk, pk[:, 0:256])
                tile = i0 // 128 + b
                if not DBG.get('nova'):
                    k.copy('dve', VA, VA.h[:, tile, :, 0:64], pk, pk.h[:, 128:256].rearrange("p (kv d) -> p kv d", kv=2))
                if seq > 0 and not DBG.get('noko'):
                    ko = kvo[b % 2]
                    k.copy('dve' if DBG.get('kodve') else 'act', ko, ko[:, :], pk, pk[:, 0:256])
                    if qknorm:
                        k.tt('dve', kt2, kt2[:, :], ko, ko[:, 0:128], ko, ko[:, 0:128], ALU.mult)
                        k.op('dve', lambda: nc.vector.tensor_reduce(out=kss[:, :], in_=kt2.h[:, :].rearrange("p (a d) -> p a d", a=2),
                                                                    op=ALU.add, axis=mybir.AxisListType.X), [kt2], [kss])
                        rstd_from(kss, kss[:, :], kss, kss[:, :], 1.0 / 64, 1e-6)
                        k.tt('dve', kt2, kt2.h[:, :].rearrange("p (a d) -> p a d", a=2),
                             ko, ko.h[:, 0:128].rearrange("p (a d) -> p a d", a=2),
                             kss, kss.h[:, :].unsqueeze(2).broadcast_to([128, 2, 64]), ALU.mult)
                        k.tt('dve', ko, ko[:, 0:128], kt2, kt2[:, :], kgrow, kgrow[:, :], ALU.mult)
                    r0 = i0 + b * 128
                    k.dma('pool', cfg['ko'][seq - 1, r0:r0 + 128, :], ko[:, 0:128], cfg['ko'], ko)
                    k.dma('pool', cfg['vo'][seq - 1, r0:r0 + 128, :], ko[:, 128:256], cfg['vo'], ko)
        if ctx is not None:
            kd, vd = ctx
            kcs = k.sb("kcs", [128, 4, 256], F32)
            vcs = k.sb("vcs", [128, 4, 128], F32)
            k.dma('sp', kcs[:, :, :], kd.h.rearrange("(t p) c -> p t c", p=128), kcs, kd)
            k.dma('sp', vcs[:, :, :], vd.h.rearrange("(t p) c -> p t c", p=128), vcs, vd)
            for tl in range(4):
                for kv in range(2):
                    pt = PS[4 + kv]
                    k.tr(pt, pt[:, 0:128], kcs, kcs[:, tl, kv * 128:(kv + 1) * 128], ident, ident[:, :])
                    k.copy('act', KT, KT[:, kv, L + tl * 128:L + (tl + 1) * 128], pt, pt[:, 0:128])
            k.copy('dve', VA, VA.h[:, nt:nt + 4, :, 0:64], vcs, vcs.h[:, :, :].rearrange("p t (kv d) -> p t kv d", kv=2))
        k.release(mkA)
        if DBG.get('attnA'):
            k.release(mk)
            return
        pT = [k.sb(f"pT{i}", [128, 512], BF16) for i in range(3)]
        osb = [k.sb(f"osb{i}", [128, 512], F32) for i in range(2)]
        rden = k.sb("rden", [64, 512], F32)
        yat = [k.sb(f"yat{i}", [64, 512], BF16) for i in range(2)]
        ya_s = cfg['ya_s'][seq]
        units = []
        steps = []
        for qb in range(nt // 2 if qhalf else nt):
            for kv in range(2):
                if band:
                    tiles = [(kt, (None if kt == qb else ('lo' if kt < qb else 'hi')))
                             for kt in (qb - 1, qb, qb + 1) if 0 <= kt < nt]
                else:
                    tiles = [(kt, None) for kt in range(nt)]
                tiles += [(nt + c, None) for c in range(nctx)]
                u = len(units)
                units.append((qb, kv))
                for ti, (kt, msk) in enumerate(tiles):
                    steps.append((u, kt, msk, ti == 0, ti == len(tiles) - 1))

        def emit_qk(si):
            u, kt, msk, first, last = steps[si]
            qb, kv = units[u]
            p = pT[si % 3]
            for hp in range(2):
                ps_s = PS[4 + 2 * (si % 2) + hp]
                k.mm(ps_s, ps_s.h[:, 0:256].rearrange("p (a q) -> p a q", a=2),
                     KT, KT[hp * 64:(hp + 1) * 64, kv, kt * 128:(kt + 1) * 128],
                     QT, QT[hp * 64:(hp + 1) * 64, kv * 2:kv * 2 + 2, qb * 128:(qb + 1) * 128])
                k.act(p, p[:, hp * 256:(hp + 1) * 256], ps_s, ps_s[:, 0:256], AF.Exp, scale=0.125)
            if msk is not None:
                mt_ = tri_lo4 if msk == 'lo' else tri_hi4
                k.tt('pool', p, p[:, :], p, p[:, :], mt_, mt_[:, :], ALU.mult)
            return p

        def emit_pv(si, p):
            u, kt, msk, first, last = steps[si]
            qb, kv = units[u]
            po = PS[2 + u % 2]
            k.mm(po, po[:, :], VA, VA[:, kt, kv, :], p, p[:, :], start=first, stop=last)
            if not last:
                return
            ob = osb[u % 2]
            k.copy('act', ob, ob[:, :], po, po[:, :])
            pd = PS[1]
            k.mm(pd, pd[0:64, :], shiftT, shiftT[:, :], ob, ob[:, :])
            if sinkexp is not None:
                k.tt('dve', rden, rden[:, :], pd, pd[0:64, :], sinkexp, sinkexp[:, kv, :], ALU.add)
                k.op('dve', lambda: nc.vector.reciprocal(out=rden[:, :], in_=rden[:, :]), [rden], [rden])
            else:
                k.op('dve', lambda: nc.vector.reciprocal(out=rden[:, :], in_=pd[0:64, :]), [pd], [rden])
            ya = yat[u % 2]
            k.tt('dve', ya, ya[:, :], ob, ob[0:64, :], rden, rden[:, :], ALU.mult)
            dstv = ya_s.h.rearrange("p (kv ab hp) l -> p kv hp ab l", kv=2, ab=2, hp=2)[:, kv, :, :, qb * 128:(qb + 1) * 128]
            srcv = ya.h[:, :].rearrange("p (hp ab q) -> p hp ab q", hp=2, ab=2)
            for hp in range(2):
                k.dma('pool', dstv[:, hp, :, :], srcv[:, hp, :, :], ya_s, ya)

        pcur = emit_qk(0)
        for si in range(len(steps)):
            pnext = emit_qk(si + 1) if si + 1 < len(steps) else None
            emit_pv(si, pcur)
            pcur = pnext
        k.release(mk)

    def mlstm(seq, L):
        SEG = min(L, 512)
        nseg = L // SEG
        ncs = SEG // 128
        win = wine_s
        mk = k.mark()
        alloc_wm(2)
        rw = {nm: k.sb("rw_" + nm, [4, SEG], F32) for nm in ('x', 'a', 'mn', 'B', 'r', 'M')}
        rw['one'] = k.sb("rw_one", [4, SEG], BF16)
        negm = k.sb("negm", [128, 256], F32)
        k.dma('sp', negm[:, :], negm_d[:, :], negm, negm_d)
        k.memset('pool', rw['one'], rw['one'][:, :], 1.0)
        rows3 = k.sb("rows3", [4, ncs, 3, 128], F32)
        rcol = k.sb("rcol", [128, ncs * 4], F32)
        carB = k.sb("carB", [4, 1], F32)
        carM = k.sb("carM", [4, 1], F32)
        Mprev = k.sb("Mprev", [4, ncs], F32)
        gb = k.sb("gb", [4, 4], F32)
        k.dma('sp', gb[:, :], gbias_d[:, :], gb, gbias_d)
        QTh = [k.sb(f"mQT{h}", [128, SEG], BF16) for h in range(4)]
        KTh = [k.sb(f"mKT{h}", [128, SEG], BF16) for h in range(4)]
        KVh = [k.sb(f"mKV{h}", [128, ncs, 257], BF16) for h in range(4)]
        for h in range(4):
            k.memset('pool', KVh[h], KVh[h][:, :, 256:257], 1.0)
        hseg = [k.sb(f"hseg{h}", [128, SEG], F32) for h in range(4)]
        Cst = [k.sb(f"Cst{h}", [128, 129], F32) for h in range(4)]
        nrep = [k.sb(f"nrep{h}", [128, 128], F32) for h in range(4)]
        onesf = k.sb("onesf", [128, 128], F32)
        k.memset('pool', onesf, onesf[:, :], 1.0)
        ones1b = k.sb("ones1b", [128, 128], BF16)
        k.memset('pool', ones1b, ones1b[:, :], 1.0)
        NB = 4
        Dt = [k.sb(f"Dt{i}", [128, 128], F32) for i in range(NB)]
        cols2 = [k.sb(f"cols2{i}", [128, 2], F32) for i in range(NB)]
        Wt = [k.sb(f"Wt{i}", [128, 128], BF16) for i in range(NB)]
        Qs = [k.sb(f"Qs{i}", [128, 128], F32) for i in range(NB)]
        dd = [k.sb(f"dd{i}", [128, 128], F32) for i in range(NB)]
        wsb = [k.sb(f"wsb{i}", [128, 1], F32) for i in range(NB)]
        Ks = [k.sb(f"Ks{i}", [128, 128], BF16) for i in range(NB)]
        hfl = k.sb("hfl", [128, SEG], F32)
        rsb = k.sb("rsb", [128, SEG], F32)
        sgb = k.sb("sgb", [128, SEG], F32)
        ybt = k.sb("ybt", [128, SEG], BF16)
        sqb = ybt
        ui = 0
        for d in range(2):
            fwd = (d == 0)
            for h in range(4):
                if seq == 0:
                    k.dma('sp', Cst[h][:, 0:128], stC_d[d, h, :, :], Cst[h], stC_d)
                    k.dma('sp', Cst[h][:, 128:129], stn_d.h[d, h, :].rearrange("(p o) -> p o", o=1), Cst[h], stn_d)
                else:
                    k.memset('pool', Cst[h], Cst[h][:, :], 0.0)
                k.ts('pool', nrep[h], nrep[h][:, :], onesf, onesf[:, :], Cst[h][:, 128:129], None, ALU.mult, extra=[Cst[h]])
            k.memset('pool', carB, carB[:, :], 0.0)
            if seq == 0:
                k.dma('sp', carM[:, :], stm_d.h[d, :].rearrange("(p o) -> p o", o=1), carM, stm_d)
            else:
                k.memset('pool', carM, carM[:, :], 0.0)
            segs = list(range(nseg)) if fwd else list(range(nseg - 1, -1, -1))
            for sg_ in segs:
                i0 = sg_ * SEG
                wt = next_wm()
                wload(wt, 0, win, 2304, 16)
                pgi, pgf = PS[0], PS[1]
                proj_fm(seq, i0, SEG, wt, (2 * d) * 4, 4, pgi, pgi[0:4, :SEG])
                proj_fm(seq, i0, SEG, wt, (2 * d + 1) * 4, 4, pgf, pgf[0:4, :SEG])
                x, a_, mn, B, r, Mx, one = (rw[nm] for nm in ('x', 'a', 'mn', 'B', 'r', 'M', 'one'))
                mt, tmp = a_, mn
                k.act(x, x[:, :], pgf, pgf[0:4, :SEG], AF.Identity, bias=gb[:, 2 * d + 1:2 * d + 2], extra=[gb])
                k.act(r, r[:, :], pgi, pgi[0:4, :SEG], AF.Identity, bias=gb[:, 2 * d:2 * d + 1], extra=[gb])
                k.act(a_, a_[:, :], x, x[:, :], AF.Abs)
                k.act(a_, a_[:, :], a_, a_[:, :], AF.Exp, scale=-1.0)
                k.act(a_, a_[:, :], a_, a_[:, :], AF.Ln, bias=1.0)
                k.ts('dve', mn, mn[:, :], x, x[:, :], -1.0, 0.0, ALU.mult, ALU.max)
                k.tt('dve', x, x[:, :], mn, mn[:, :], a_, a_[:, :], ALU.add)
                k.ts('dve', x, x[:, :], x, x[:, :], -1.0, None, ALU.mult)
                rv = (lambda t_: t_[:, :]) if fwd else (lambda t_: t_[:, ::-1])
                k.op('dve', lambda: nc.vector.tensor_tensor_scan(out=rv(B), data0=one[:, :], data1=rv(x), initial=carB[:, 0:1],
                                                                 op0=ALU.mult, op1=ALU.add), [one, x, carB], [B])
                k.tt('dve', r, r[:, :], r, r[:, :], B, B[:, :], ALU.subtract)
                k.op('dve', lambda: nc.vector.tensor_tensor_scan(out=rv(Mx), data0=one[:, :], data1=rv(r), initial=carM[:, 0:1],
                                                                 op0=ALU.mult, op1=ALU.max), [one, r, carM], [Mx])
                k.tt('dve', mt, mt[:, :], B, B[:, :], Mx, Mx[:, :], ALU.add)
                Mv = Mx.h[:, :].rearrange("p (c t) -> p c t", t=128)
                if fwd:
                    k.copy('dve', Mprev, Mprev[:, 0:1], carM, carM[:, 0:1])
                    if ncs > 1:
                        k.copy('dve', Mprev, Mprev[:, 1:ncs], Mx, Mv[:, 0:ncs - 1, 127])
                else:
                    k.copy('dve', Mprev, Mprev[:, ncs - 1:ncs], carM, carM[:, 0:1])
                    if ncs > 1:
                        k.copy('dve', Mprev, Mprev[:, 0:ncs - 1], Mx, Mv[:, 1:ncs, 0])
                last = SEG - 1 if fwd else 0
                k.copy('dve', carB, carB[:, :], B, B[:, last:last + 1])
                k.copy('dve', carM, carM[:, :], Mx, Mx[:, last:last + 1])
                k.ts('dve', rows3, rows3.h[:, :, 0, :], Mx, Mv, -1.0, None, ALU.mult)
                k.tt('dve', tmp, tmp.h[:, :].rearrange("p (c t) -> p c t", t=128), Mprev,
                     Mprev.h[:, :].unsqueeze(2).broadcast_to([4, ncs, 128]), Mx, Mv, ALU.subtract)
                k.act(rows3, rows3.h[:, :, 1, :], tmp, tmp.h[:, :].rearrange("p (c t) -> p c t", t=128), AF.Exp)
                k.act(rows3, rows3.h[:, :, 2, :], mt, mt.h[:, :].rearrange("p (c t) -> p c t", t=128), AF.Exp, scale=-1.0)
                prc = PS[2]
                for c in range(ncs):
                    k.tr(prc, prc[:, c * 4:(c + 1) * 4], r, r[0:4, c * 128:(c + 1) * 128], ident, ident[0:4, 0:4])
                k.copy('dve', rcol, rcol[:, :], prc, prc[:, 0:ncs * 4])
                if seq > 0 and sg_ == segs[-1]:
                    k.dma('pool', bm_o.h[seq - 1, d, :].rearrange("(p o) -> p o", o=1), mt[:, last:last + 1], bm_o, mt)
                for h in range(4):
                    wt = next_wm()
                    wload(wt, 0, win, 768 + h * 128, 128)
                    wload(wt, 128, win, 1280 + h * 128, 128)
                    pq, pk_ = PS[0], PS[1]
                    proj_fm(seq, i0, SEG, wt, 0, 128, pq, pq[:, :SEG])
                    k.copy('act', QTh[h], QTh[h][:, :], pq, pq[:, :SEG])
                    proj_fm(seq, i0, SEG, wt, 128, 128, pk_, pk_[:, :SEG])
                    k.act(KTh[h], KTh[h][:, :], pk_, pk_[:, :SEG], AF.Identity, scale=128 ** -0.5)
                    wt2 = next_wm()
                    wload(wt2, 0, win, 1280 + h * 128, 128)
                    wload(wt2, 128, win, 1792 + h * 128, 128)
                    for c in range(ncs):
                        pkv = PS[2 + c % 2]
                        proj_tm(seq, i0 + c * 128, wt2, 0, 256, pkv, pkv[:, 0:256])
                        k.act(KVh[h], KVh[h][:, c, 0:128], pkv, pkv[:, 0:128], AF.Identity, scale=128 ** -0.5)
                        k.copy('dve', KVh[h], KVh[h][:, c, 128:256], pkv, pkv[:, 128:256])
                chunks = list(range(ncs)) if fwd else list(range(ncs - 1, -1, -1))
                edge = 127 if fwd else 0
                msk = tri_hi4 if fwd else tri_lo4
                moff = 0 if fwd else 128
                for c in chunks:
                    cs = slice(c * 128, (c + 1) * 128)
                    bA = [PS[2 * h] for h in range(4)]
                    bB = [PS[2 * h + 1] for h in range(4)]
                    for h in range(4):
                        k.mm(bA[h], bA[h][:, 0:384], sel4, sel4[:, h, :], rows3, rows3.h[:, c, :, :].rearrange("p a t -> p (a t)"),
                             start=True, stop=False)
                        k.mm(bA[h], bA[h][:, 0:128], ident, ident[:, :], negm, negm[:, moff:moff + 128], start=False, stop=True)
                        k.mm(bA[h], bA[h][:, 384:512], KTh[h], KTh[h][:, cs], QTh[h], QTh[h][:, cs])
                    for h in range(4):
                        u = h
                        k.act(Dt[u], Dt[u][:, :], bA[h], bA[h][:, 0:128], AF.Exp, bias=rcol[:, c * 4 + h:c * 4 + h + 1], extra=[rcol])
                        k.copy('act', cols2[u], cols2[u][:, :], bA[h], bA[h][:, edge:edge + 129:128])
                        k.tt('dve', Qs[u], Qs[u][:, :], QTh[h], QTh[h][:, cs], bA[h], bA[h][:, 128:256], ALU.mult)
                        k.tt('dve', Wt[u], Wt[u][:, :], Dt[u], Dt[u][:, :], bA[h], bA[h][:, 384:512], ALU.mult)
                    for h in range(4):
                        u = h
                        k.mm(bB[h], bB[h][:, 0:128], Cst[h], Cst[h][:, 0:128], Qs[u], Qs[u][:, :], start=True, stop=False)
                        k.mm(bB[h], bB[h][:, 0:128], KVh[h], KVh[h][:, c, 128:256], Wt[u], Wt[u][:, :], start=False, stop=True)
                        k.mm(bB[h], bB[h][:, 128:256], nrep[h], nrep[h][:, :], Qs[u], Qs[u][:, :], start=True, stop=False)
                        k.mm(bB[h], bB[h][:, 128:256], ones1b, ones1b[:, :], Wt[u], Wt[u][:, :], start=False, stop=True)
                    for h in range(4):
                        u = h
                        k.act(dd[u], dd[u][:, :], bB[h], bB[h][:, 128:256], AF.Abs)
                        k.act(wsb[u], wsb[u][:, :], rcol, rcol[:, c * 4 + h:c * 4 + h + 1], AF.Exp, bias=cols2[u][:, 0:1],
                              extra=[cols2[u]])
                        k.tt('dve', dd[u], dd[u][:, :], dd[u], dd[u][:, :], bA[h], bA[h][:, 256:384], ALU.max)
                        k.op('dve', lambda: nc.vector.reciprocal(out=dd[u][:, :], in_=dd[u][:, :]), [dd[u]], [dd[u]])
                        k.tt('dve', hseg[h], hseg[h][:, cs], bB[h], bB[h][:, 0:128], dd[u], dd[u][:, :], ALU.mult)
                        k.act(Ks[u], Ks[u][:, :], KVh[h], KVh[h][:, c, 0:128], AF.Copy, scale=wsb[u][:, 0:1], extra=[wsb[u]])
                    for h in range(4):
                        u = h
                        k.mm(bB[h], bB[h][:, 256:385], Ks[u], Ks[u][:, :], KVh[h], KVh[h][:, c, 128:257])
                    for h in range(4):
                        u = h
                        k.stt(Cst[h], Cst[h][:, :], Cst[h], Cst[h][:, :], cols2[u][:, 1:2], bB[h], bB[h][:, 256:385], ALU.mult, ALU.add,
                              extra=[cols2[u]])
                        k.act(nrep[h], nrep[h][:, :], onesf, onesf[:, :], AF.Copy, scale=Cst[h][:, 128:129], extra=[Cst[h]])
                for h in range(4):
                    if fwd:
                        k.dma('pool', hf_s[seq][h, :, i0:i0 + SEG], hseg[h][:, :], hf_s[seq], hseg[h])
                    else:
                        k.dma('sp', hfl[:, :], hf_s[seq][h, :, i0:i0 + SEG], hfl, hf_s[seq])
                        k.tt('pool', hfl, hfl[:, :], hfl, hfl[:, :], hseg[h], hseg[h][:, :], ALU.add)
                        k.act(sqb, sqb[:, :], hfl, hfl[:, :], AF.Square)
                        pr = PS[0]
                        k.mm(pr, pr[:, :SEG], ones_bf, ones_bf[:, :], sqb, sqb[:, :])
                        rstd_from(rsb, rsb[:, :], pr, pr[:, :SEG], 8.0, 1e-6)
                        wt = next_wm()
                        wload(wt, 0, win, 2320 + h * 128, 128)
                        po_ = PS[1]
                        proj_fm(seq, i0, SEG, wt, 0, 128, po_, po_[:, :SEG])
                        k.act(sgb, sgb[:, :], po_, po_[:, :SEG], AF.Sigmoid)
                        k.stt(hfl, hfl[:, :], hfl, hfl[:, :], bng[:, h:h + 1], rsb, rsb[:, :], ALU.mult, ALU.mult, extra=[bng])
                        k.tt('pool', ybt, ybt[:, :], hfl, hfl[:, :], sgb, sgb[:, :], ALU.mult)
                        k.dma('pool', yb_s[seq][:, h, i0:i0 + SEG], ybt[:, :], yb_s[seq], ybt)
            if seq > 0:
                for h in range(4):
                    k.dma('pool', bC_o[seq - 1, d, h, :, :], Cst[h][:, 0:128], bC_o, Cst[h])
                    k.dma('pool', bn_o.h[seq - 1, d, h, :].rearrange("(p o) -> p o", o=1), Cst[h][:, 128:129], bn_o, Cst[h])
        k.release(mk)
    def hgrn(seq, L):
        SEG = min(L, 512)
        nseg = L // SEG
        ngr = SEG // 128
        nch = SEG // 32
        win = wino_s
        mk = k.mark()
        wA2 = [k.sb(f"gwA{i}", [128, NC_, 256], BF16) for i in range(2)]
        wB2 = [k.sb(f"gwB{i}", [128, NC_, 128], BF16) for i in range(2)]
        onesS = k.sb("onesS", [128, SEG], F32)
        k.memset('pool', onesS, onesS[:, :], 1.0)
        fT2 = [k.sb(f"fT{i}", [128, SEG], F32) for i in range(2)]
        kT2 = [k.sb(f"kT{i}", [128, SEG], F32) for i in range(2)]
        eT2 = [k.sb(f"eT{i}", [128, SEG], F32) for i in range(2)]
        Zh2 = [k.sb(f"gZ{i}", [128, SEG], F32) for i in range(2)]
        khf2 = [k.sb(f"gkh{i}", [128, SEG], F32) for i in range(2)]
        qT = [k.sb(f"gqT{h}", [128, SEG], F32) for h in range(4)]
        qb2 = [k.sb(f"gqb{i}", [128, SEG], BF16) for i in range(2)]
        Kmix = [k.sb(f"gKmix{i}", [128, 4, 128], BF16) for i in range(2)]
        tmpE = [k.sb(f"gtmpE{i}", [128, 128], F32) for i in range(2)]
        Kh = [k.sb(f"gKh{h}", [128, ngr, 128], BF16) for h in range(4)]
        Vt = [k.sb(f"gVt{h}", [128, ngr, 128], BF16) for h in range(4)]
        att = [k.sb(f"gatt{h}", [128, ngr, 128], BF16) for h in range(4)]
        oseg = [k.sb(f"goseg{h}", [128, SEG], F32) for h in range(4)]
        dec = [k.sb(f"gdec{h}", [128, ngr], F32) for h in range(4)]
        refc2 = [k.sb(f"grefc{i}", [128, nch], F32) for i in range(2)]
        S = [k.sb(f"gS{h}", [128, 128], F32) for h in range(4)]
        carZ = [k.sb(f"gcarZ{h}", [128, 1], F32) for h in range(4)]
        ofl, rsb, sgb = fT2[0], fT2[1], kT2[0]
        sqb, yct = qb2[0], qb2[1]
        v32 = lambda t_: t_.h[:, :].rearrange("p (c t) -> p c t", t=32)
        v128 = lambda t_: t_.h[:, :].rearrange("p (c t) -> p c t", t=128)
        for d in range(2):
            fwd = (d == 0)
            for i in range(2):
                k.memset('pool', Kmix[i], Kmix[i][:, :, :], 0.0)
            for h in range(4):
                if seq == 0:
                    k.dma('sp', S[h][:, :], stS_d[d, h, :, :], S[h], stS_d)
                else:
                    k.memset('pool', S[h], S[h][:, :], 0.0)
                k.memset('pool', carZ[h], carZ[h][:, :], 0.0)
            segs = list(range(nseg)) if fwd else list(range(nseg - 1, -1, -1))
            if seq == 0 and fwd:
                segs = segs[:nseg // 2]
            rv = (lambda t_: t_[:, :]) if fwd else (lambda t_: t_[:, ::-1])
            msk = tri_hi4 if fwd else tri_lo4
            for sg_ in segs:
                i0 = sg_ * SEG
                so = (seq == 0 and sg_ >= nseg // 2)
                def head_prep(h, par):
                    fT, kT, eT, Zh, khf, qb, refc = fT2[par], kT2[par], eT2[par], Zh2[par], khf2[par], qb2[par], refc2[par]
                    lfT = eT
                    PB = 4 * par
                    wt = wA2[par]
                    wload(wt, 0, win, h * 128, 128)
                    wload(wt, 128, win, 512 * (1 + d) + h * 128, 128)
                    pq, pf = PS[PB + 0], PS[PB + 1]
                    if not so:
                        proj_fm(seq, i0, SEG, wt, 0, 128, pq, pq[:, :SEG])
                        yield
                    proj_fm(seq, i0, SEG, wt, 128, 128, pf, pf[:, :SEG])
                    yield
                    if not so:
                        k.act(qT[h], qT[h][:, :], pq, pq[:, :SEG], AF.Silu)
                        yield
                    k.act(fT, fT[:, :], pf, pf[:, :SEG], AF.Sigmoid)
                    yield
                    k.ts('dve', fT, fT[:, :], fT, fT[:, :], oml[:, h:h + 1], lbc[:, h:h + 1], ALU.mult, ALU.add, extra=[oml, lbc])
                    yield
                    k.act(lfT, lfT[:, :], fT, fT[:, :], AF.Ln)
                    yield
                    k.ts('pool', kT, kT[:, :], fT, fT[:, :], -1.0, 1.0, ALU.mult, ALU.add)
                    yield
                    k.op('dve', lambda: nc.vector.tensor_tensor_scan(out=rv(Zh), data0=onesS[:, :], data1=rv(lfT),
                                                                     initial=carZ[h][:, 0:1], op0=ALU.mult, op1=ALU.add),
                         [onesS, lfT, carZ[h]], [Zh])
                    yield
                    Zv = v32(Zh)
                    Zg = v128(Zh)
                    if fwd:
                        k.copy('dve', refc, refc[:, 0:1], carZ[h], carZ[h][:, 0:1])
                        yield
                        k.copy('dve', refc, refc[:, 1:nch], Zh, Zv[:, 0:nch - 1, 31])
                        yield
                        refg_ap = refc[:, 0:nch:4]
                        edgeg_ap = Zg[:, :, 127]
                    else:
                        k.copy('dve', refc, refc[:, nch - 1:nch], carZ[h], carZ[h][:, 0:1])
                        yield
                        k.copy('dve', refc, refc[:, 0:nch - 1], Zh, Zv[:, 1:nch, 0])
                        yield
                        refg_ap = refc[:, 3:nch:4]
                        edgeg_ap = Zg[:, :, 0]
                    last = SEG - 1 if fwd else 0
                    k.copy('dve', carZ[h], carZ[h][:, :], Zh, Zh[:, last:last + 1])
                    yield
                    refb = refc.h[:, :].unsqueeze(2).broadcast_to([128, nch, 32])
                    refgb = refg_ap.unsqueeze(2).broadcast_to([128, ngr, 128])
                    edgegb = edgeg_ap.unsqueeze(2).broadcast_to([128, ngr, 128])
                    if not so:
                        k.tt('dve', eT, v32(eT), Zh, Zv, refc, refb, ALU.subtract)
                        yield
                        k.act(eT, eT[:, :], eT, eT[:, :], AF.Exp)
                        yield
                        k.tt('pool', qb, qb[:, :], qT[h], qT[h][:, :], eT, eT[:, :], ALU.mult)
                        yield
                        k.tt('dve', eT, v128(eT), Zh, Zg, refc, refgb, ALU.subtract)
                        yield
                        k.act(eT, eT[:, :], eT, eT[:, :], AF.Exp)
                        yield
                        k.tt('pool', qT[h], qT[h][:, :], qT[h], qT[h][:, :], eT, eT[:, :], ALU.mult)
                        yield
                    k.tt('dve', eT, v128(eT), Zh, edgegb, Zh, Zg, ALU.subtract)
                    yield
                    k.act(eT, eT[:, :], eT, eT[:, :], AF.Exp)
                    yield
                    k.tt('pool', khf, khf[:, :], kT, kT[:, :], eT, eT[:, :], ALU.mult)
                    yield
                    k.tt('dve', dec[h], dec[h][:, :], Zh, edgeg_ap, refc, refg_ap, ALU.subtract)
                    yield
                    k.act(dec[h], dec[h][:, :], dec[h], dec[h][:, :], AF.Exp)
                    yield
                    ptb = PS[PB + 1]
                    for g in range(ngr):
                        k.tr(ptb, ptb[:, g * 128:(g + 1) * 128], khf, khf[:, g * 128:(g + 1) * 128], ident, ident[:, :])
                        yield
                    k.copy('act', Kh[h], Kh[h].h[:, :, :], ptb, ptb.h[:, 0:ngr * 128].rearrange("p (g d) -> p g d", g=ngr))
                    yield
                    wt2 = wB2[par]
                    wload(wt2, 0, win, 1536 + h * 128, 128)
                    for g in range(ngr):
                        pv = PS[PB + 2]
                        proj_tm(seq, i0 + g * 128, wt2, 0, 128, pv, pv[:, 0:128])
                        yield
                        k.copy('act', Vt[h], Vt[h][:, g, :], pv, pv[:, 0:128])
                        yield
                    for g in range(0 if so else ngr):
                        Km = Kmix[par]
                        pa = PS[PB + 3]
                        for a in range(4):
                            c0, c1 = (0, 32 * (a + 1)) if fwd else (32 * a, 128)
                            te = tmpE[par]
                            cs = slice(g * 128 + c0, g * 128 + c1)
                            ci = g * 4 + a
                            k.act(te, te[:, c0:c1], Zh, Zh[:, cs], AF.Exp, bias=refc[:, ci:ci + 1], scale=-1.0, extra=[refc])
                            yield
                            k.tt('dve', Km, Km[:, a, c0:c1], kT, kT[:, cs], te, te[:, c0:c1], ALU.mult)
                            yield
                            k.mm(pa, pa[:, a * 32:(a + 1) * 32], Km, Km[:, a, :], qb, qb[:, g * 128 + a * 32:g * 128 + (a + 1) * 32])
                            yield
                        k.tt('dve', att[h], att[h][:, g, :], pa, pa[:, 0:128], msk, msk[:, 0:128], ALU.mult)
                        yield

                for h0 in (0, 2):
                    gens = [head_prep(h0, 0), head_prep(h0 + 1, 1)]
                    alive = [True, True]
                    while any(alive):
                        for gi in range(2):
                            if alive[gi]:
                                try:
                                    next(gens[gi])
                                except StopIteration:
                                    alive[gi] = False
                groups = list(range(ngr)) if fwd else list(range(ngr - 1, -1, -1))
                for g in groups:
                    gs = slice(g * 128, (g + 1) * 128)
                    for h in range(4):
                        pO = PS[h % 2]
                        if not so:
                            k.mm(pO, pO[:, 0:128], Vt[h], Vt[h][:, g, :], att[h], att[h][:, g, :], start=True, stop=False)
                            k.mm(pO, pO[:, 0:128], S[h], S[h][:, :], qT[h], qT[h][:, gs], start=False, stop=True)
                        pD = PS[2 + h % 2]
                        k.mm(pD, pD[:, 0:128], Kh[h], Kh[h][:, g, :], Vt[h], Vt[h][:, g, :])
                        k.stt(S[h], S[h][:, :], S[h], S[h][:, :], dec[h][:, g:g + 1], pD, pD[:, 0:128], ALU.mult, ALU.add,
                              extra=[dec[h]])
                        if not so:
                            k.copy('act', oseg[h], oseg[h][:, gs], pO, pO[:, 0:128])
                for h in range(0 if so else 4):
                    if fwd:
                        k.dma('pool', of_s[seq][h, :, i0:i0 + SEG], oseg[h][:, :], of_s[seq], oseg[h])
                    else:
                        k.dma('sp', ofl[:, :], of_s[seq][h, :, i0:i0 + SEG], ofl, of_s[seq])
                        k.tt('pool', ofl, ofl[:, :], ofl, ofl[:, :], oseg[h], oseg[h][:, :], ALU.add)
                        k.act(sqb, sqb[:, :], ofl, ofl[:, :], AF.Square)
                        pr = PS[6]
                        k.mm(pr, pr[:, :SEG], ones_bf, ones_bf[:, :], sqb, sqb[:, :])
                        rstd_from(rsb, rsb[:, :], pr, pr[:, :SEG], 8.0, 1e-6)
                        wt = wB2[h % 2]
                        wload(wt, 0, win, 2048 + h * 128, 128)
                        pg = PS[7]
                        proj_fm(seq, i0, SEG, wt, 0, 128, pg, pg[:, :SEG])
                        k.act(sgb, sgb[:, :], pg, pg[:, :SEG], AF.Silu)
                        k.stt(ofl, ofl[:, :], ofl, ofl[:, :], cng[:, h:h + 1], rsb, rsb[:, :], ALU.mult, ALU.mult, extra=[cng])
                        k.tt('pool', yct, yct[:, :], ofl, ofl[:, :], sgb, sgb[:, :], ALU.mult)
                        k.dma('pool', yb_s[seq][:, h, i0:i0 + SEG], yct[:, :], yb_s[seq], yct)
            if seq > 0:
                for h in range(4):
                    k.dma('pool', cS_o[seq - 1, d, h, :, :], S[h][:, :], cS_o, S[h])
        k.release(mk)

    def mixer_out(l, wout, ya_first):
        mk = k.mark()
        woa = k.sb("woa", [64, 8, D], BF16)
        wob = k.sb("wob", [128, 4, D], BF16)
        ra, rb = (0, 512) if ya_first else (512, 0)
        k.dma('sp', woa[:, :, :], wout.h[ra:ra + 512, :].rearrange("(h p) n -> p h n", p=64), woa, wout)
        k.dma('sp', wob[:, :, :], wout.h[rb:rb + 512, :].rearrange("(h p) n -> p h n", p=128), wob, wout)
        yat = [k.sb(f"oyat{i}", [64, 8, 256], BF16) for i in range(2)]
        ybt = [k.sb(f"oybt{i}", [128, 4, 256], BF16) for i in range(2)]
        alloc_ln()
        ii = 0
        for t in ((0, 1, 4) if l == 1 else range(NTT)):
            col = 0 if t < 4 else 1
            for hf in range(2):
                ya_, yb_ = yat[ii % 2], ybt[ii % 2]
                ii += 1
                if t < 4:
                    seq, c0 = 0, t * TT + hf * 256
                else:
                    seq, c0 = 1 + hf, 0
                k.dma('sp', ya_[:, :, :], ya_s[seq][:, :, c0:c0 + 256], ya_, ya_s[seq])
                k.dma('sp', yb_[:, :, :], yb_s[seq][:, :, c0:c0 + 256], yb_, yb_s[seq])
                for dc in range(NC_):
                    py = PS[dc % 4]
                    for hd in range(8):
                        k.mm(py, py[:, 0:256], woa, woa[:, hd, dc * 128:(dc + 1) * 128], ya_, ya_[:, hd, :],
                             start=(hd == 0), stop=False)
                    for hd in range(4):
                        k.mm(py, py[:, 0:256], wob, wob[:, hd, dc * 128:(dc + 1) * 128], yb_, yb_[:, hd, :],
                             start=False, stop=(hd == 3))
                    xs = slice(hf * 256, (hf + 1) * 256)
                    k.stt(X[t], X[t][:, dc, xs], py, py[:, 0:256], cf[l][:, (3 * 1 + 2) * 8 + dc, col:col + 1],
                          X[t], X[t][:, dc, xs], ALU.mult, ALU.add, extra=[cf[l]])
            layernorm(l, 1, t)
            modulate(l, 2, t)
        k.release(mk)

    stop_after = None if debug_stage is None else debug_stage.get('stop')
    cfgA = dict(win=wine_s, qcol=0, kcol=512, rope=True, band=True, qknorm=False, sinkexp=sinkexp,
                ctx=(kctxa_d, vctxa_d), ko=ak_o, vo=av_o, ya_s=ya_s)
    cfgD = dict(win=wino_s, qcol=2560, kcol=3072, rope=True, band=False, qknorm=True, sinkexp=None,
                ctx=(kctxd_d, vctxd_d), ko=dk_o, vo=dv_o, ya_s=ya_s, qg=gvec[:, 0:1], kg=gvec[:, 1:2], qhalf=True)

    def ffn_phase(l, j, nxt=None):
        mk = k.mark()
        alloc_ffn()
        if (l, j) == (0, 0):
            bg_add(wine_d, wine_s, wine_d.h, wine_s.h, D, 2832)
            bg_add(woute_d, woute_s, woute_d.h, woute_s.h, D, D)
            bg_add_ffn(1)
        elif (l, j) == (0, 2):
            bg_add_ffn(2)
            bg_add(wino_d, wino_s, wino_d.h, wino_s.h, D, 3328)
        elif (l, j) == (1, 0):
            bg_add(wouto_d, wouto_s, wouto_d.h, wouto_s.h, D, D)
            bg_add_ffn(3)
        bg_start()
        for t in ((0, 1, 4) if (l, j) == (1, 2) else range(NTT)):
            ffn(l, j, t)
            if nxt is not None:
                modulate(nxt[0], nxt[1], t)
        bg_flush()
        k.release(mk)

    def mark_phase(label):
        PHASE_MARKS.append((label, dict(k.cnt)))

    def run_all_marked():
        mark_phase('setup_end')
        for t in range(NTT):
            modulate(0, 0, t)
        ffn_phase(0, 0, nxt=(0, 1)); mark_phase('ffn00')
        for seq in range(3):
            attention(seq, SL[seq], cfgA); mark_phase(f'attnA{seq}')
            mlstm(seq, SL[seq]); mark_phase(f'mlstm{seq}')
        mixer_out(0, woute_s, True); mark_phase('mout0')
        ffn_phase(0, 2, nxt=(1, 0)); mark_phase('ffn02')
        ffn_phase(1, 0, nxt=(1, 1)); mark_phase('ffn10')
        for seq in range(3):
            hgrn(seq, SL[seq]); mark_phase(f'hgrn{seq}')
            attention(seq, SL[seq], cfgD); mark_phase(f'attnD{seq}')
        mixer_out(1, wouto_s, False); mark_phase('mout1')
        ffn_phase(1, 2, nxt=None); mark_phase('ffn12')

    def run_all():
        if debug_stage is None:
            return run_all_marked()
        for t in range(NTT):
            modulate(0, 0, t)
        ffn_phase(0, 0, nxt=(0, 1))
        if stop_after == 'f00':
            return
        only = None if debug_stage is None else debug_stage.get('only')
        if only is not None:
            for nm in only:
                if nm[0] == 'a':
                    attention(int(nm[1]), SL[int(nm[1])], cfgA)
                if nm[0] == 'm':
                    mlstm(int(nm[1]), SL[int(nm[1])])
                if nm[0] == 'o':
                    mixer_out(0, woute_s, True)
            return
        for seq in range(3):
            attention(seq, SL[seq], cfgA)
            mlstm(seq, SL[seq])
        mixer_out(0, woute_s, True)
        if stop_after == 'm0':
            return
        ffn_phase(0, 2, nxt=(1, 0))
        ffn_phase(1, 0, nxt=(1, 1))
        if stop_after == 'f10':
            return
        only1 = None if debug_stage is None else debug_stage.get('only1')
        if only1 is not None:
            for nm in only1:
                if nm[0] == 'a':
                    attention(int(nm[1]), SL[int(nm[1])], cfgD)
                if nm[0] == 'g':
                    hgrn(int(nm[1]), SL[int(nm[1])])
                if nm[0] == 'o':
                    mixer_out(1, wouto_s, False)
            return
        for seq in range(3):
            hgrn(seq, SL[seq])
            attention(seq, SL[seq], cfgD)
        mixer_out(1, wouto_s, False)
        if stop_after == 'm1':
            return
        ffn_phase(1, 2, nxt=None)

    run_all()

    xo = [k.sb(f"xo{i}", [128, D], F32) for i in range(2)]

    def store_x(dst_d, r0, Xt, t0, i):
        xi = xo[i % 2]
        for g in range(2):
            pt = PS[2 + (2 * i + g) % 4]
            for q in range(4):
                c = g * 4 + q
                k.tr(pt, pt[:, q * 128:(q + 1) * 128], Xt, Xt[:, c, t0:t0 + 128], ident, ident[:, :])
            k.copy(ev_rr(), xi, xi[:, g * 512:(g + 1) * 512], pt, pt[:, :])
        k.dma('pool', dst_d[r0:r0 + 128, :], xi[:, :], dst_d, xi)

    for t in (0, 1, 4):
        for b in range(4):
            if t < 4:
                store_x(ys_d, t * TT + b * 128, X[t], b * 128, t * 4 + b)
            else:
                store_x(yp_d, b * 128, X[t], b * 128, t * 4 + b)
    mark_phase('store')
    k.barrier()
    return nc


def _consts(mir=False):
    import ml_dtypes
    c = {}
    c["ident"] = np.eye(128, dtype=np.float32)
    nf = 16
    inv = (10000.0 ** (-np.arange(nf, dtype=np.float32) / nf)).astype(np.float32)
    t = np.arange(LS)
    if mir:
        t = (LS - 1) - t
    row = (t // 64).astype(np.float32)
    colp = (t % 64).astype(np.float32)
    cosT = np.zeros((128, LS), np.float32)
    sinT = np.zeros((128, LS), np.float32)
    for p in range(128):
        d = p % 64
        pos = row if d < 32 else colp
        ang = (pos * inv[d % 16]).astype(np.float32)
        cosT[p] = np.cos(ang)
        sinT[p] = np.sin(ang)
    c["cosT"], c["sinT"] = cosT, sinT
    R = np.zeros((128, 128), np.float32)
    for m in range(128):
        if (m % 32) < 16:
            R[m, m + 16] = -1.0
        else:
            R[m, m - 16] = 1.0
    c["RT"] = np.ascontiguousarray(R.T)
    sh = np.zeros((128, 64), np.float32)
    for i in range(64):
        sh[64 + i, i] = 1.0
    c["shiftT"] = sh
    sel = np.zeros((4, 4, 128), np.float32)
    for h in range(4):
        sel[h, h, :] = 1.0
    c["sel4"] = sel.reshape(4, 512)
    s = np.arange(128)[:, None]
    tt_ = np.arange(128)[None, :]
    hi = (s <= tt_).astype(np.float32)
    lo = (s >= tt_).astype(np.float32)
    same = ((s // 32) == (tt_ // 32)).astype(np.float32)
    b64 = ((s // 64) == (tt_ // 64)).astype(np.float32) / 64.0
    mskb = np.concatenate([np.tile(hi, (1, 4)), np.tile(lo, (1, 4)), hi * same, lo * same, b64], 1)
    c["mskb"] = mskb.astype(ml_dtypes.bfloat16)
    c["negm"] = np.concatenate([(hi - 1.0) * 30000.0, (lo - 1.0) * 30000.0], 1).astype(np.float32)
    return c


def make_in_maps(inp):
    f = lambda a: np.ascontiguousarray(np.asarray(a, dtype=np.float32))
    maps = []
    lnp = np.concatenate([f(inp['ln_g']).reshape(48, 128), f(inp['ln_b']).reshape(48, 128)], 0)
    lbl = f(inp['c_lb_logits']).reshape(2, 4, 128)
    smallc = np.zeros((128, 32), np.float32)
    smallc[:, 0:4] = f(inp['b_norm_g'])[0].T
    smallc[:, 4:8] = f(inp['c_norm_g'])[0].T
    smallc[:, 8] = np.tile(f(inp['d_q_norm'])[0], 2)
    smallc[:, 9] = np.tile(f(inp['d_k_norm'])[0], 2)
    smallc[:, 10:14] = lbl[0].T
    smallc[:, 14:18] = lbl[1].T
    smallc[:, 18:26] = np.broadcast_to(f(inp['a_sink'])[0].reshape(1, 8), (128, 8))
    kgrow = np.broadcast_to(np.tile(f(inp['d_k_norm'])[0], 2)[None, :], (128, 128)).copy()
    base = {
        "ada_w": f(inp['ada_w']),
        "ada_b": f(inp['ada_b']).reshape(144, 128),
        "lnp": lnp,
        "ffn_w1": f(inp['ffn_w1']).reshape(4, D, DFF),
        "ffn_w3": f(inp['ffn_w3']).reshape(4, D, DFF),
        "ffn_w2": f(inp['ffn_w2']).reshape(4, DFF, D),
        "wout_e": f(inp['w_out_even'])[0],
        "wout_o": f(inp['w_out_odd'])[0],
        "smallc": smallc, "kgrow": kgrow,
    }
    win_e = f(inp['w_in_even'])[0]
    win_o = f(inp['w_in_odd'])[0]
    gb = f(inp['b_gate_bias'])[0]
    variants = []
    for mir in (False, True):
        v = dict(base)
        v.update(_consts(mir))
        if not mir:
            v["win_e"], v["win_o"] = win_e, win_o
            v["gbias"] = np.ascontiguousarray(gb.T)
        else:
            we = win_e.copy()
            we[:, 2304:2312], we[:, 2312:2320] = win_e[:, 2312:2320], win_e[:, 2304:2312]
            wo = win_o.copy()
            wo[:, 512:1024], wo[:, 1024:1536] = win_o[:, 1024:1536], win_o[:, 512:1024]
            v["win_e"], v["win_o"] = we, wo
            v["gbias"] = np.ascontiguousarray(gb[[2, 3, 0, 1]].T)
        variants.append(v)

    def dupk(kc):
        return np.ascontiguousarray(np.concatenate([kc[:, 0], kc[:, 0], kc[:, 1], kc[:, 1]], 1))

    for i in range(8):
        b = i // 2
        mir = (i % 2 == 1)
        m = dict(variants[1 if mir else 0])
        xs = f(inp['x_sample'][b])
        xp = f(inp['x_prompt'][2 * i:2 * i + 2])
        stC, stn, stm, stS = (f(inp['state_b_C'][b, 0]), f(inp['state_b_n'][b, 0]), f(inp['state_b_m'][b, 0]),
                              f(inp['state_c_S'][b, 0]))
        if mir:
            xs = np.ascontiguousarray(xs[::-1])
            xp = np.ascontiguousarray(xp[:, ::-1])
            stC, stn, stm, stS = (np.ascontiguousarray(a_[::-1]) for a_ in (stC, stn, stm, stS))
        m.update({
            "xs": xs,
            "xp": xp.reshape(NPS * LP, D),
            "cvec": np.concatenate([f(inp['c'][b]).reshape(8, 128), f(inp['c_ctx']).reshape(8, 128)], 0),
            "kctxa": dupk(f(inp['cache_a_k'][b, 0])), "vctxa": f(inp['cache_a_v'][b, 0]).reshape(512, 128),
            "kctxd": dupk(f(inp['cache_d_k'][b, 0])), "vctxd": f(inp['cache_d_v'][b, 0]).reshape(512, 128),
            "stC": stC, "stn": stn, "stm": stm, "stS": stS,
        })
        maps.append(m)
    return maps


def gather(r):
    H2 = LS // 2
    ys = np.stack([np.concatenate([r[2 * b]["ys"], r[2 * b + 1]["ys"][::-1]], 0) for b in range(4)], 0)

    def per_core(nm, tok_axis=None, dir_axis=None):
        outs = []
        for i in range(8):
            a = r[i][nm]
            if i % 2 == 1:
                if tok_axis is not None:
                    a = np.flip(a, axis=tok_axis)
                if dir_axis is not None:
                    a = np.flip(a, axis=dir_axis)
            outs.append(a)
        return np.concatenate(outs, 0)

    yp = np.concatenate([(r[i]["yp"].reshape(NPS, LP, D)[:, ::-1] if i % 2 else r[i]["yp"].reshape(NPS, LP, D))
                         for i in range(8)], 0)
    ak = per_core("ak_o", tok_axis=1).reshape(16, 1, LP, 2, 64)
    av = per_core("av_o", tok_axis=1).reshape(16, 1, LP, 2, 64)
    bC = per_core("bC_o", dir_axis=1).reshape(16, 1, 2, 4, 128, 128)
    bn = per_core("bn_o", dir_axis=1).reshape(16, 1, 2, 4, 128)
    bm = per_core("bm_o", dir_axis=1).reshape(16, 1, 2, 4)
    cS = per_core("cS_o", dir_axis=1).reshape(16, 1, 2, 4, 128, 128)
    dk = per_core("dk_o", tok_axis=1).reshape(16, 1, LP, 2, 64)
    dv = per_core("dv_o", tok_axis=1).reshape(16, 1, LP, 2, 64)
    return (np.ascontiguousarray(yp), ys, np.ascontiguousarray(ak), np.ascontiguousarray(av), np.ascontiguousarray(bC),
            np.ascontiguousarray(bn), np.ascontiguousarray(bm), np.ascontiguousarray(cS), np.ascontiguousarray(dk),
            np.ascontiguousarray(dv))


def kernel(**inputs):
    nc = build()
    in_maps = make_in_maps(inputs)
    res = run_bass_kernel_spmd(nc, in_maps, core_ids=list(range(8)))
    return gather(res.results)
```
